# Optimizing a Trainium2 kernel written in Bass

```python
import math
import jax, jax.numpy as jnp
from jax import lax
import numpy as np

D_MODEL = 1024
BATCH = 8
SEQ = 2048
DEPTH = 2
DEC_BATCH = 128
DEC_SEQ = 1
PAST_LEN = 16384
PAGE_SIZE = 128

N_MIXERS = 4
W_GROUP = D_MODEL // N_MIXERS
HEAD_DIM = 64
N_HEADS = W_GROUP // HEAD_DIM
S5_CH = 16
S5_GROUPS = W_GROUP // S5_CH
S5_STATE = 64
GLA_RANK = 16
GLA_GATE_TEMP = 16.0
RWKV_W_RANK = 32
RWKV_A_RANK = 32
RWKV_G_RANK = 64
RWKV_COLS = 3 * W_GROUP + RWKV_W_RANK + RWKV_A_RANK + RWKV_G_RANK
RWKV_DECAY_SCALE = 0.6065306597126334
D_FF = 2816
CONV_W = 3
CHUNK = 64
EPS = 1e-6
GN_EPS = 64e-5
IN_SIZES = (W_GROUP,
            W_GROUP, W_GROUP, W_GROUP, N_HEADS, N_HEADS, W_GROUP,
            W_GROUP, W_GROUP, W_GROUP, GLA_RANK, W_GROUP,
            RWKV_COLS)
D_IN = 9 * W_GROUP + 2 * N_HEADS + GLA_RANK + RWKV_COLS
F32 = jnp.float32

kernel_name = 'hybrid_s5_mlstm_gla_rwkv7_step'


def _split(a, sizes):
    idx, acc = [], 0
    for s in sizes[:-1]:
        acc += s
        idx.append(acc)
    return jnp.split(a, idx, axis=-1)


def _heads(a):
    return a.reshape(a.shape[0], a.shape[1], N_HEADS, HEAD_DIM)


def rmsnorm(x, g):
    x32 = x.astype(F32)
    y = x32 * lax.rsqrt(jnp.mean(x32 * x32, axis=-1, keepdims=True) + EPS)
    return (y * g.astype(F32)).astype(x.dtype)


def head_rmsnorm(y, g):
    y = y * lax.rsqrt(jnp.mean(y * y, axis=-1, keepdims=True) + EPS)
    return y.reshape(y.shape[0], y.shape[1], -1) * g


def head_layernorm(y, g):
    mu = jnp.mean(y, axis=-1, keepdims=True)
    yc = y - mu
    y = yc * lax.rsqrt(jnp.mean(yc * yc, axis=-1, keepdims=True) + GN_EPS)
    return y.reshape(y.shape[0], y.shape[1], -1) * g


def s5_mixer(u, h0_re, h0_im, lam_re, lam_im, log_dt, b_re, b_im, c_re, c_im, d_skip, w_glu):
    Bsz, T, _ = u.shape
    uc = u.reshape(Bsz, T, S5_GROUPS, S5_CH).astype(jnp.complex64)
    lam = lax.complex(lam_re.astype(F32), lam_im.astype(F32))
    dt = jnp.exp(log_dt.astype(F32))[:, None]
    lam_bar = jnp.exp(lam * dt)
    b = lax.complex(b_re.astype(F32), b_im.astype(F32))
    b_bar = ((lam_bar - 1.0) / lam)[..., None] * b
    bu = jnp.einsum('gph,btgh->btgp', b_bar, uc)
    a = jnp.broadcast_to(lam_bar, bu.shape)

    def combine(e1, e2):
        a1, x1 = e1
        a2, x2 = e2
        return a1 * a2, a2 * x1 + x2

    a_cum, hs = lax.associative_scan(combine, (a, bu), axis=1)
    h0 = lax.complex(h0_re.astype(F32), h0_im.astype(F32))
    hs = hs + a_cum * h0[:, None]
    c = lax.complex(c_re.astype(F32), c_im.astype(F32))
    y = jnp.einsum('ghp,btgp->btgh', c, hs).real.reshape(Bsz, T, W_GROUP) + d_skip * u
    z = jax.nn.gelu(y)
    out = z * jax.nn.sigmoid(z @ w_glu)
    h_last = hs[:, -1]
    return out, h_last.real, h_last.imag


def _to_chunks(a, L):
    nc = a.shape[1] // L
    return jnp.moveaxis(a.reshape(a.shape[0], nc, L, *a.shape[2:]), 1, 0)


def _from_chunks(a):
    a = jnp.moveaxis(a, 0, 1)
    return a.reshape(a.shape[0], a.shape[1] * a.shape[2], *a.shape[3:])


def mlstm_chunked(q, k, v, ig, logf, C0, n0, m0):
    T = q.shape[1]
    L = math.gcd(T, CHUNK)
    causal = jnp.tril(jnp.ones((L, L), dtype=bool))[None, :, :, None]

    def step(carry, inp):
        C, n, m = carry
        qc, kc, vc, ic, fc = inp
        bc = jnp.cumsum(fc, axis=1)
        dmat = bc[:, :, None, :] - bc[:, None, :, :] + ic[:, None, :, :]
        dmat = jnp.where(causal, dmat, -jnp.inf)
        g = bc + m[:, None, :]
        m_row = jnp.maximum(g, jnp.max(dmat, axis=2))
        w = jnp.exp(dmat - m_row[:, :, None, :])
        w0 = jnp.exp(g - m_row)
        ws = w * jnp.einsum('bthd,bshd->btsh', qc, kc)
        num = w0[..., None] * jnp.einsum('bhvd,bthd->bthv', C, qc) + jnp.einsum('btsh,bshv->bthv', ws, vc)
        den = w0 * jnp.einsum('bhd,bthd->bth', n, qc) + jnp.sum(ws, axis=2)
        hc = num / jnp.maximum(jnp.abs(den), jnp.exp(-m_row))[..., None]
        m_new = m_row[:, -1]
        wl = jnp.exp(bc[:, -1:, :] - bc + ic - m_new[:, None, :])
        w0l = jnp.exp(bc[:, -1] + m - m_new)
        C_new = w0l[..., None, None] * C + jnp.einsum('bsh,bshv,bshd->bhvd', wl, vc, kc)
        n_new = w0l[..., None] * n + jnp.einsum('bsh,bshd->bhd', wl, kc)
        return (C_new, n_new, m_new), hc

    xs = (_to_chunks(q, L), _to_chunks(k, L), _to_chunks(v, L), _to_chunks(ig, L), _to_chunks(logf, L))
    (C, n, m), hs = lax.scan(step, (C0, n0, m0), xs)
    return _from_chunks(hs), C, n, m


def gla_chunked(q, k, v, log_alpha, S0):
    T = q.shape[1]
    L = math.gcd(T, CHUNK)
    causal = jnp.tril(jnp.ones((L, L), dtype=bool))[None, :, :, None, None]

    def step(S, inp):
        qc, kc, vc, lc = inp
        bc = jnp.cumsum(lc, axis=1)
        diff = bc[:, :, None] - bc[:, None, :]
        decay = jnp.where(causal, jnp.exp(jnp.where(causal, diff, 0.0)), 0.0)
        att = jnp.einsum('bthd,btshd,bshd->btsh', qc, decay, kc)
        o = jnp.einsum('btsh,bshv->bthv', att, vc) + jnp.einsum('bthd,bhdv->bthv', qc * jnp.exp(bc), S)
        b_last = bc[:, -1]
        S_new = jnp.exp(b_last)[..., None] * S + jnp.einsum('bshd,bshv->bhdv', kc * jnp.exp(b_last[:, None] - bc), vc)
        return S_new, o

    xs = (_to_chunks(q, L), _to_chunks(k, L), _to_chunks(v, L), _to_chunks(log_alpha, L))
    S, os = lax.scan(step, S0, xs)
    return _from_chunks(os), S


def rwkv7_scan(r, w, k, v, kk, a, S0):
    def step(S, inp):
        r_t, w_t, k_t, v_t, kk_t, a_t = inp
        sk = jnp.einsum('bhvk,bhk->bhv', S, kk_t)
        S = S * w_t[:, :, None, :] - sk[..., None] * (a_t * kk_t)[:, :, None, :] + v_t[..., None] * k_t[:, :, None, :]
        return S, jnp.einsum('bhvk,bhk->bhv', S, r_t)

    xs = tuple(jnp.moveaxis(t, 1, 0) for t in (r, w, k, v, kk, a))
    S, o = lax.scan(step, S0, xs)
    return jnp.moveaxis(o, 0, 1), S


def hybrid_layer(x, s5_re, s5_im, ml_C, ml_n, ml_m, gla_S, rw_S, rw_shift, ffn_buf,
                 norm_mix, w_in, s5_lam_re, s5_lam_im, s5_log_dt, s5_b_re, s5_b_im,
                 s5_c_re, s5_c_im, s5_d, s5_w_glu, ml_gate_bias, ml_norm,
                 gla_w_alpha, gla_b_alpha, gla_norm, rw_mu, rw_w0, rw_w2, rw_a0,
                 rw_a2, rw_g2, rw_k_k, rw_k_a, rw_r_k, rw_norm, w_out, norm_ffn,
                 ffn_w_up, ffn_conv_w, ffn_conv_b, ffn_w_down):
    B, T, _ = x.shape
    h = rmsnorm(x, norm_mix)
    proj = (h @ w_in).astype(F32)
    (u, mq, mk, mv, mi, mf, mo, gq, gk, gv, ga, gg, rcols) = _split(proj, IN_SIZES)

    y_s5, s5_re_new, s5_im_new = s5_mixer(u, s5_re, s5_im, s5_lam_re, s5_lam_im, s5_log_dt,
                                          s5_b_re, s5_b_im, s5_c_re, s5_c_im, s5_d, s5_w_glu)

    i_pre = mi + ml_gate_bias[:N_HEADS]
    logf = jax.nn.log_sigmoid(mf + ml_gate_bias[N_HEADS:])
    h_ml, ml_C_new, ml_n_new, ml_m_new = mlstm_chunked(
        _heads(mq), _heads(mk) * HEAD_DIM ** -0.5, _heads(mv), i_pre, logf,
        ml_C.astype(F32), ml_n.astype(F32), ml_m.astype(F32))
    y_ml = head_rmsnorm(h_ml, ml_norm) * jax.nn.sigmoid(mo)

    log_alpha = jax.nn.log_sigmoid(ga @ gla_w_alpha + gla_b_alpha) / GLA_GATE_TEMP
    o_gla, gla_S_new = gla_chunked(_heads(gq) * HEAD_DIM ** -0.5, _heads(gk), _heads(gv),
                                   _heads(log_alpha), gla_S.astype(F32))
    y_gla = head_rmsnorm(o_gla, gla_norm) * jax.nn.silu(gg)

    prev = jnp.concatenate([rw_shift.astype(F32)[:, None], rcols[:, :-1]], axis=1)
    xm = rcols + rw_mu * (prev - rcols)
    rr, rk, rv, rwd, rad, rgd = _split(xm, (W_GROUP, W_GROUP, W_GROUP, RWKV_W_RANK, RWKV_A_RANK, RWKV_G_RANK))
    decay = jnp.exp(-RWKV_DECAY_SCALE * jax.nn.sigmoid(rw_w0 + jnp.tanh(rwd) @ rw_w2))
    a = jax.nn.sigmoid(rw_a0 + rad @ rw_a2)
    g = jax.nn.sigmoid(rgd) @ rw_g2
    kk = _heads(rk * rw_k_k)
    kk = kk * lax.rsqrt(jnp.maximum(jnp.sum(kk * kk, axis=-1, keepdims=True), 1e-24))
    kt = rk * (1.0 + (a - 1.0) * rw_k_a)
    rh, kth, vh = _heads(rr), _heads(kt), _heads(rv)
    o_rw, rw_S_new = rwkv7_scan(rh, _heads(decay), kth, vh, kk, _heads(a), rw_S.astype(F32))
    bonus = jnp.sum(rh * kth * rw_r_k, axis=-1, keepdims=True) * vh
    y_rw = (head_layernorm(o_rw, rw_norm) + bonus.reshape(B, T, W_GROUP)) * g

    mix = jnp.concatenate([y_s5, y_ml, y_gla, y_rw], axis=-1) @ w_out
    x = x + mix.astype(x.dtype)

    h2 = rmsnorm(x, norm_ffn)
    up = (h2 @ ffn_w_up).astype(F32)
    ug, uv = jnp.split(up, 2, axis=-1)
    padded = jnp.concatenate([ffn_buf.astype(F32), ug], axis=1)
    conv = ffn_conv_b.astype(F32)
    for j in range(CONV_W):
        conv = conv + ffn_conv_w[j] * padded[:, j:j + T]
    ffn_buf_new = padded[:, -(CONV_W - 1):]
    ffn = (jax.nn.gelu(conv) * uv) @ ffn_w_down
    x = x + ffn.astype(x.dtype)
    new_states = (s5_re_new, s5_im_new, ml_C_new, ml_n_new, ml_m_new, gla_S_new, rw_S_new,
                  rcols[:, -1], ffn_buf_new)
    return x, new_states


def run_trunk(x, states, layer_params, norm_final):
    outs = [[] for _ in states]
    for l in range(DEPTH):
        x, new = hybrid_layer(x, *[s[l] for s in states], *[p[l] for p in layer_params])
        for lst, s in zip(outs, new):
            lst.append(s)
    return rmsnorm(x, norm_final), tuple(jnp.stack(lst) for lst in outs)


def setup_inputs(seed: int = 0) -> dict:
    key = jax.random.key(seed)
    keys = jax.random.split(key, 64)
    counter = [0]

    def nk():
        counter[0] += 1
        return keys[counter[0] - 1]

    def nrm(shape, scale):
        return scale * jax.random.normal(nk(), shape, F32)

    def uni(shape, lo, hi):
        return jax.random.uniform(nk(), shape, F32, lo, hi)

    L, H, dh = DEPTH, N_HEADS, HEAD_DIM
    lam_im0 = jnp.pi * jnp.arange(S5_STATE, dtype=F32)
    return {
        'x_prompt': nrm((BATCH, SEQ, D_MODEL), 1.0),
        'x_sample': nrm((DEC_BATCH, DEC_SEQ, D_MODEL), 1.0),
        'state_s5_re': nrm((L, DEC_BATCH, S5_GROUPS, S5_STATE), 0.5),
        'state_s5_im': nrm((L, DEC_BATCH, S5_GROUPS, S5_STATE), 0.5),
        'state_mlstm_C': nrm((L, DEC_BATCH, H, dh, dh), 0.5),
        'state_mlstm_n': nrm((L, DEC_BATCH, H, dh), 0.5),
        'state_mlstm_m': nrm((L, DEC_BATCH, H), 1.0),
        'state_gla_S': nrm((L, DEC_BATCH, H, dh, dh), 0.5),
        'state_rwkv_S': nrm((L, DEC_BATCH, H, dh, dh), 0.5),
        'state_rwkv_shift': nrm((L, DEC_BATCH, RWKV_COLS), 1.0),
        'state_ffn_conv': nrm((L, DEC_BATCH, CONV_W - 1, D_FF), 1.0),
        'norm_mix': 1.0 + nrm((L, D_MODEL), 0.02),
        'w_in': nrm((L, D_MODEL, D_IN), D_MODEL ** -0.5),
        's5_lam_re': -0.5 + nrm((L, S5_GROUPS, S5_STATE), 0.01),
        's5_lam_im': lam_im0 + nrm((L, S5_GROUPS, S5_STATE), 0.01),
        's5_log_dt': uni((L, S5_GROUPS), math.log(1e-3), math.log(1e-1)),
        's5_b_re': nrm((L, S5_GROUPS, S5_STATE, S5_CH), (2.0 * S5_CH) ** -0.5),
        's5_b_im': nrm((L, S5_GROUPS, S5_STATE, S5_CH), (2.0 * S5_CH) ** -0.5),
        's5_c_re': nrm((L, S5_GROUPS, S5_CH, S5_STATE), (2.0 * S5_STATE) ** -0.5),
        's5_c_im': nrm((L, S5_GROUPS, S5_CH, S5_STATE), (2.0 * S5_STATE) ** -0.5),
        's5_d': nrm((L, W_GROUP), 1.0),
        's5_w_glu': nrm((L, W_GROUP, W_GROUP), W_GROUP ** -0.5),
        'ml_gate_bias': jnp.concatenate([nrm((L, H), 0.1), 3.0 + nrm((L, H), 0.5)], axis=-1),
        'ml_norm': 1.0 + nrm((L, W_GROUP), 0.02),
        'gla_w_alpha': nrm((L, GLA_RANK, W_GROUP), GLA_RANK ** -0.5),
        'gla_b_alpha': nrm((L, W_GROUP), 0.1),
        'gla_norm': 1.0 + nrm((L, W_GROUP), 0.02),
        'rw_mu': uni((L, RWKV_COLS), 0.0, 1.0),
        'rw_w0': nrm((L, W_GROUP), 0.5),
        'rw_w2': nrm((L, RWKV_W_RANK, W_GROUP), 0.1 * RWKV_W_RANK ** -0.5),
        'rw_a0': nrm((L, W_GROUP), 0.1),
        'rw_a2': nrm((L, RWKV_A_RANK, W_GROUP), 0.1 * RWKV_A_RANK ** -0.5),
        'rw_g2': nrm((L, RWKV_G_RANK, W_GROUP), RWKV_G_RANK ** -0.5),
        'rw_k_k': 0.85 + nrm((L, W_GROUP), 0.02),
        'rw_k_a': 1.0 + nrm((L, W_GROUP), 0.02),
        'rw_r_k': nrm((L, H, dh), 0.1),
        'rw_norm': 1.0 + nrm((L, W_GROUP), 0.02),
        'w_out': nrm((L, D_MODEL, D_MODEL), D_MODEL ** -0.5),
        'norm_ffn': 1.0 + nrm((L, D_MODEL), 0.02),
        'ffn_w_up': nrm((L, D_MODEL, 2 * D_FF), D_MODEL ** -0.5),
        'ffn_conv_w': nrm((L, CONV_W, D_FF), CONV_W ** -0.5),
        'ffn_conv_b': nrm((L, D_FF), 0.02),
        'ffn_w_down': nrm((L, D_FF, D_MODEL), D_FF ** -0.5),
        'norm_final': 1.0 + nrm((D_MODEL,), 0.02),
    }


def reference(x_prompt, x_sample, state_s5_re, state_s5_im, state_mlstm_C, state_mlstm_n,
              state_mlstm_m, state_gla_S, state_rwkv_S, state_rwkv_shift, state_ffn_conv,
              norm_mix, w_in, s5_lam_re, s5_lam_im, s5_log_dt, s5_b_re, s5_b_im,
              s5_c_re, s5_c_im, s5_d, s5_w_glu, ml_gate_bias, ml_norm,
              gla_w_alpha, gla_b_alpha, gla_norm, rw_mu, rw_w0, rw_w2, rw_a0,
              rw_a2, rw_g2, rw_k_k, rw_k_a, rw_r_k, rw_norm, w_out, norm_ffn,
              ffn_w_up, ffn_conv_w, ffn_conv_b, ffn_w_down, norm_final):
    layer_params = (norm_mix, w_in, s5_lam_re, s5_lam_im, s5_log_dt, s5_b_re, s5_b_im,
                    s5_c_re, s5_c_im, s5_d, s5_w_glu, ml_gate_bias, ml_norm,
                    gla_w_alpha, gla_b_alpha, gla_norm, rw_mu, rw_w0, rw_w2, rw_a0,
                    rw_a2, rw_g2, rw_k_k, rw_k_a, rw_r_k, rw_norm, w_out, norm_ffn,
                    ffn_w_up, ffn_conv_w, ffn_conv_b, ffn_w_down)
    L, H, dh = DEPTH, N_HEADS, HEAD_DIM
    prompt_states = (jnp.zeros((L, BATCH, S5_GROUPS, S5_STATE), F32),
                     jnp.zeros((L, BATCH, S5_GROUPS, S5_STATE), F32),
                     jnp.zeros((L, BATCH, H, dh, dh), F32),
                     jnp.zeros((L, BATCH, H, dh), F32),
                     jnp.zeros((L, BATCH, H), F32),
                     jnp.zeros((L, BATCH, H, dh, dh), F32),
                     jnp.zeros((L, BATCH, H, dh, dh), F32),
                     jnp.zeros((L, BATCH, RWKV_COLS), F32),
                     jnp.zeros((L, BATCH, CONV_W - 1, D_FF), F32))
    sample_states = (state_s5_re, state_s5_im, state_mlstm_C, state_mlstm_n, state_mlstm_m,
                     state_gla_S, state_rwkv_S, state_rwkv_shift, state_ffn_conv)
    y_prompt, new_p = run_trunk(x_prompt, prompt_states, layer_params, norm_final)
    y_sample, new_s = run_trunk(x_sample, sample_states, layer_params, norm_final)
    (p_s5_re, p_s5_im, p_ml_C, p_ml_n, p_ml_m, p_gla_S, p_rw_S, p_rw_shift, p_ffn_conv) = new_p
    (s_s5_re, s_s5_im, s_ml_C, s_ml_n, s_ml_m, s_gla_S, s_rw_S, s_rw_shift, s_ffn_conv) = new_s
    return (y_prompt, y_sample,
            p_s5_re, p_s5_im, p_ml_C, p_ml_n, p_ml_m, p_gla_S, p_rw_S, p_rw_shift, p_ffn_conv,
            s_s5_re, s_s5_im, s_ml_C, s_ml_n, s_ml_m, s_gla_S, s_rw_S, s_rw_shift, s_ffn_conv)
```

```python
import contextlib
import math
import numpy as np
import concourse.bass as bass
import concourse.mybir as mybir
from concourse.alu_op_type import AluOpType as ALU
from concourse.bass_utils import run_bass_kernel_spmd

AF = mybir.ActivationFunctionType
AX = mybir.AxisListType
F32 = mybir.dt.float32
BF16 = mybir.dt.bfloat16
I32 = mybir.dt.int32

ENGS = ['tensor', 'vector', 'scalar', 'gpsimd', 'sync']

NCORES = 8
D = 1024
DIN = 3224
DFF = 2816
NFF = 22
T = 2048
NS = 16
DEPTH = 2
EPS = 1e-6
GN_EPS = 64e-5
OFF = dict(u=0, mq=256, mk=512, mv=768, mi=1024, mf=1028, mo=1032, gq=1288, gk=1544, gv=1800,
           ga=2056, gg=2072, rc=2328)
BO = dict(s5_d=0, ml_norm=256, gla_norm=512, rw_norm=768, rw_w0=1024, rw_a0=1280, rw_k_k=1536,
          rw_k_a=1792, rw_r_k=2048, rw_mu=2304, ml_gate_bias=3200)
BCW = 3208
GELU_C = 1.5957691216057308
NT = 2
STOP = [99]
SAME_ENGINE_SYNC = True
MIXERS = ['gla', 'ml', 'rw']


class Prog:
    def __init__(self, nc, n_dma_sems=32):
        self.nc = nc
        self.stack = contextlib.ExitStack()
        self.ops = {e: [] for e in ENGS}
        self.cnt = {e: 0 for e in ENGS}
        self.known = {e: {} for e in ENGS}
        self.res = {}
        self.n_dma = n_dma_sems
        self.dma_rr = 0
        self.dma_use = [0] * n_dma_sems
        self.sems = {}
        self.uid = 0
        self.out_points = []

    def sb(self, shape, dtype=F32, name=None):
        self.uid += 1
        name = name or ('t%d' % self.uid)
        return self.stack.enter_context(self.nc.sbuf_tensor(name, list(shape), dtype))

    def ps(self, shape, dtype=F32, name=None):
        self.uid += 1
        name = name or ('p%d' % self.uid)
        return self.stack.enter_context(self.nc.psum_tensor(name, list(shape), dtype))

    def _sem(self, name):
        if name not in self.sems:
            self.sems[name] = self.stack.enter_context(self.nc.semaphore('s_' + name))
        return self.sems[name]

    def add(self, eng, fn, reads=(), writes=(), dma=False, is_out=False):
        def flat(ks):
            out = []
            for x in ks:
                if isinstance(x, (list, tuple)):
                    out.extend(flat(x))
                else:
                    out.append(x)
            return out
        reads, writes = flat(reads), flat(writes)
        waits = {}

        def need(sp):
            if sp is None:
                return
            s, v = sp
            if waits.get(s, 0) < v:
                waits[s] = v

        for k in reads:
            st = self.res.get(k)
            if st is not None:
                need(st['w'])
                if isinstance(k, str) and len(k) == 2 and k[0] == 'b':
                    for s, v in st['r'].items():
                        if s != eng:
                            need((s, v))
        for k in writes:
            st = self.res.get(k)
            if st is not None:
                need(st['w'])
                for s, v in st['r'].items():
                    need((s, v))
        if dma:
            j = self.dma_rr
            self.dma_rr = (j + 1) % self.n_dma
            if self.dma_use[j] > 0:
                need(('d%d' % j, 16 * self.dma_use[j]))
            self.dma_use[j] += 1
            sp = ('d%d' % j, 16 * self.dma_use[j])
            inc = 16
        else:
            self.cnt[eng] += 1
            sp = (eng, self.cnt[eng])
            inc = 1
        kn = self.known[eng]
        fw = []
        for s, v in waits.items():
            if s == eng and (eng == 'tensor' or not SAME_ENGINE_SYNC):
                continue
            if kn.get(s, 0) >= v:
                continue
            kn[s] = v
            fw.append((s, v))
        for s, _ in fw:
            self._sem(s)
        self._sem(sp[0])
        self.ops[eng].append((fw, fn, sp[0], inc))
        for k in writes:
            self.res[k] = {'w': sp, 'r': {}}
        for k in reads:
            st = self.res.setdefault(k, {'w': None, 'r': {}})
            if st['r'].get(sp[0], 0) < sp[1]:
                st['r'][sp[0]] = sp[1]
        if is_out:
            self.out_points.append(sp)
        return sp

    def dma(self, out, in_, reads=(), writes=(), eng='sync', is_out=False, **kw):
        return self.add(eng, lambda e: e.dma_start(out=out, in_=in_, **kw),
                        reads, writes, dma=True, is_out=is_out)

    def mm(self, out, lhsT, rhs, start=True, stop=True, reads=(), writes=()):
        assert (lhsT.dtype == F32) == (rhs.dtype == F32), (lhsT.dtype, rhs.dtype)
        return self.add('tensor', lambda e: e.matmul(out, lhsT, rhs, start=start, stop=stop),
                        reads, writes)

    def tr(self, out, in_, ident, reads=(), writes=()):
        return self.add('tensor', lambda e: e.transpose(out, in_, ident), reads, writes)

    def act(self, out, in_, func, reads=(), writes=(), eng='scalar', **kw):
        return self.add(eng, lambda e: e.activation(out, in_, func, **kw), reads, writes)

    def tt(self, out, a, b, op, reads=(), writes=(), eng='vector'):
        return self.add(eng, lambda e: e.tensor_tensor(out, a, b, op), reads, writes)

    def ts(self, out, a, s1, s2, op0, op1=None, reads=(), writes=(), eng='vector'):
        if op1 is None:
            return self.add(eng, lambda e: e.tensor_scalar(out, a, s1, None, op0), reads, writes)
        return self.add(eng, lambda e: e.tensor_scalar(out, a, s1, s2, op0, op1), reads, writes)

    def stt(self, out, a, s, b, op0, op1, reads=(), writes=(), eng='vector'):
        return self.add(eng, lambda e: e.scalar_tensor_tensor(out, a, s, b, op0, op1), reads, writes)

    def cp(self, out, in_, reads=(), writes=(), eng='vector'):
        if eng == 'scalar':
            return self.add(eng, lambda e: e.copy(out, in_), reads, writes)
        return self.add(eng, lambda e: e.tensor_copy(out, in_), reads, writes)

    def memset(self, ap, val, writes=(), eng='vector'):
        return self.add(eng, lambda e: e.memset(ap, val), (), writes)

    def red(self, out, in_, op, reads=(), writes=(), axis=None):
        axis = axis or AX.X
        return self.add('vector', lambda e: e.tensor_reduce(out, in_, axis, op), reads, writes)

    def emit(self):
        nc = self.nc
        final = {}
        for s, v in self.out_points:
            if final.get(s, 0) < v:
                final[s] = v
        with nc.Block() as block:
            def replay(name, e):
                for fw, fn, s, inc in self.ops[name]:
                    for ws, wv in fw:
                        e.wait_ge(self.sems[ws], wv)
                    fn(e).then_inc(self.sems[s], inc)
                if name == 'sync':
                    for fs, fv in final.items():
                        e.wait_ge(self.sems[fs], fv)

            @block.tensor
            def _(e):
                replay('tensor', e)

            @block.vector
            def _(e):
                replay('vector', e)

            @block.scalar
            def _(e):
                replay('scalar', e)

            @block.gpsimd
            def _(e):
                replay('gpsimd', e)

            @block.sync
            def _(e):
                replay('sync', e)

    def close(self):
        self.stack.close()


IN_SPECS = [
    ('xp', [T, D]), ('xs', [NS, D]),
    ('st_s5_re', [DEPTH, NS, 1024]), ('st_s5_im', [DEPTH, NS, 1024]),
    ('st_ml_C', [DEPTH, NS, 16384]), ('st_ml_n', [DEPTH, NS, 256]), ('st_ml_m', [DEPTH, NS, 4]),
    ('st_gla_S', [DEPTH, NS, 16384]), ('st_rw_S', [DEPTH, NS, 16384]),
    ('st_rw_shift', [DEPTH, NS, 896]), ('st_ffn_conv', [DEPTH, NS, 2, DFF]),
    ('norm_mix', [DEPTH, D]), ('w_in', [DEPTH, D, DIN]),
    ('s5_lam_re', [DEPTH, 16, 64]), ('s5_lam_im', [DEPTH, 16, 64]), ('s5_log_dt', [DEPTH, 16]),
    ('s5_b_re', [DEPTH, 16, 64, 16]), ('s5_b_im', [DEPTH, 16, 64, 16]),
    ('s5_c_re', [DEPTH, 16, 16, 64]), ('s5_c_im', [DEPTH, 16, 16, 64]),
    ('s5_d', [DEPTH, 256]), ('s5_w_glu', [DEPTH, 256, 256]),
    ('ml_gate_bias', [DEPTH, 8]), ('ml_norm', [DEPTH, 256]),
    ('gla_w_alpha', [DEPTH, 16, 256]), ('gla_b_alpha', [DEPTH, 256]), ('gla_norm', [DEPTH, 256]),
    ('rw_mu', [DEPTH, 896]), ('rw_w0', [DEPTH, 256]), ('rw_w2', [DEPTH, 32, 256]),
    ('rw_a0', [DEPTH, 256]), ('rw_a2', [DEPTH, 32, 256]), ('rw_g2', [DEPTH, 64, 256]),
    ('rw_k_k', [DEPTH, 256]), ('rw_k_a', [DEPTH, 256]), ('rw_r_k', [DEPTH, 256]),
    ('rw_norm', [DEPTH, 256]), ('w_out', [DEPTH, D, D]), ('norm_ffn', [DEPTH, D]),
    ('ffn_w_up', [DEPTH, D, 2 * DFF]), ('ffn_conv_w', [DEPTH, 3, DFF]), ('ffn_conv_b', [DEPTH, DFF]),
    ('ffn_w_down', [DEPTH, DFF, D]), ('norm_final', [1, D]),
]
OUT_SPECS = [
    ('y_p', [T, D]), ('y_s', [NS, D]),
    ('p_s5_re', [DEPTH, 1024]), ('p_s5_im', [DEPTH, 1024]), ('p_ml_C', [DEPTH, 16384]),
    ('p_ml_n', [DEPTH, 256]), ('p_ml_m', [DEPTH, 4]), ('p_gla_S', [DEPTH, 16384]),
    ('p_rw_S', [DEPTH, 16384]), ('p_rw_shift', [DEPTH, 896]), ('p_ffn_conv', [DEPTH, 2, DFF]),
    ('s_s5_re', [DEPTH, NS, 1024]), ('s_s5_im', [DEPTH, NS, 1024]), ('s_ml_C', [DEPTH, NS, 16384]),
    ('s_ml_n', [DEPTH, NS, 256]), ('s_ml_m', [DEPTH, NS, 4]), ('s_gla_S', [DEPTH, NS, 16384]),
    ('s_rw_S', [DEPTH, NS, 16384]), ('s_rw_shift', [DEPTH, NS, 896]), ('s_ffn_conv', [DEPTH, NS, 2, DFF]),
]


class K:
    pass


def pkeys(ti, c0, c1):
    return ['P%d_b%d' % (ti, b) for b in range(c0 // 512, (c1 - 1) // 512 + 1)]


def v3(ap2d, n, j=8):
    return ap2d[:, 0:j * n].rearrange("p (j i) -> p j i", i=n)


def build(debug=(), ngroups=None, nlayers_setup=DEPTH):
    nc = bass.Bass("TRN2", target_bir_lowering=False)
    I = {}
    for name, shp in IN_SPECS:
        I[name] = nc.dram_tensor(name, shp, F32, kind="ExternalInput").ap()
    O = {}
    for name, shp in OUT_SPECS:
        O[name] = nc.dram_tensor(name, shp, F32, kind="ExternalOutput").ap()
    DBG = {}
    for name, shp in debug:
        DBG[name] = nc.dram_tensor(name, shp, F32, kind="ExternalOutput").ap()
    p = Prog(nc)
    k = K()
    k.p, k.I, k.O, k.DBG = p, I, O, DBG

    def mk_mask(name, pattern, op, fill, base, cm, init=1.0):
        t = p.sb([128, 128], name=name)
        p.memset(t[:], init, writes=[name], eng='gpsimd')
        p.add('gpsimd', lambda e: e.affine_select(t[:], t[:], pattern, op, fill, base=base,
                                                  channel_multiplier=cm), reads=[name], writes=[name])
        return t

    k.ident = mk_mask('ident', [[-1, 128]], ALU.is_equal, 0.0, 0, 1)
    k.triu = mk_mask('triu', [[1, 128]], ALU.is_ge, 0.0, 0, -1)
    k.maskadd = mk_mask('maskadd', [[-1, 128]], ALU.is_ge, -1e30, 0, 1, init=0.0)
    k.sellast = mk_mask('sellast', [[0, 128]], ALU.is_equal, 0.0, -127, 1)
    k.masksl = mk_mask('masksl', [[-1, 128]], ALU.is_ge, 0.0, -1, 1)
    k.masksu = mk_mask('masksu', [[1, 128]], ALU.is_ge, 0.0, -1, -1)
    k.shm = mk_mask('shm', [[1, 128]], ALU.is_equal, 0.0, -1, -1)
    k.shprev = mk_mask('shprev', [[128, 128]], ALU.is_equal, 0.0, -127, 1)
    k.ones = p.sb([128, 128], name='ones')
    p.memset(k.ones[:], 1.0, writes=['ones'])
    k.mhalf = p.sb([128, 4], name='mhalf')
    p.memset(k.mhalf[:], -0.5, writes=['mhalf'])
    gq = p.sb([128, 1], I32, name='gq')
    p.add('gpsimd', lambda e: e.iota(gq[:], [[0, 1]], base=0, channel_multiplier=1), (), ['gq'])
    gf = p.sb([128, 8], name='gf')
    gi2 = p.sb([128, 2], I32, name='gi2')
    p.cp(gf[:, 0:1], gq[:], reads=['gq'], writes=['gf'])
    p.ts(gf[:, 1:2], gf[:, 0:1], -7.5, 1.0 / 16, ALU.add, ALU.mult, reads=['gf'], writes=['gf'])
    p.cp(gi2[:, 0:1], gf[:, 1:2], reads=['gf'], writes=['gi2'])
    p.cp(gf[:, 2:3], gi2[:, 0:1], reads=['gi2'], writes=['gf'])
    p.ts(gf[:, 3:4], gf[:, 2:3], -0.5, 0.5, ALU.add, ALU.mult, reads=['gf'], writes=['gf'])
    p.cp(gi2[:, 1:2], gf[:, 3:4], reads=['gf'], writes=['gi2'])
    p.cp(gf[:, 4:5], gi2[:, 1:2], reads=['gi2'], writes=['gf'])
    k.gpar = p.sb([128, 2], name='gpar')
    p.stt(k.gpar[:, 1:2], gf[:, 4:5], -2.0, gf[:, 2:3], ALU.mult, ALU.add, reads=['gf'], writes=['gpar'])
    p.ts(k.gpar[:, 0:1], k.gpar[:, 1:2], -1.0, 1.0, ALU.mult, ALU.add, reads=['gpar'], writes=['gpar'])
    k.halfpi = p.sb([128, 1], name='halfpi')
    p.memset(k.halfpi[:], float(np.pi / 2), writes=['halfpi'])
    k.onec = p.sb([128, 1], name='onec')
    p.memset(k.onec[:], 1.0, writes=['onec'])
    kvi = p.sb([128, 128], I32, name='kvi')
    p.add('gpsimd', lambda e: e.iota(kvi[:], [[1, 128]], base=1, channel_multiplier=0), (), ['kvi'])
    k.kvec = p.sb([128, 128], name='kvec')
    p.cp(k.kvec[:], kvi[:], reads=['kvi'], writes=['kvec'])

    k.pp = [p.ps([128, 1024], name='pp%d' % i) for i in range(4)]

    def bank(i):
        return k.pp[i // 2][:, (i % 2) * 512:(i % 2) * 512 + 512]
    k.bank = bank
    k.scd = [p.sb([128, 2048], name='scd%d' % i) for i in range(4)]
    k.sc = [k.scd[i // 2][:, (i % 2) * 1024:(i % 2) * 1024 + 1024] for i in range(8)]
    k.qi = k.sc[7].bitcast(I32)
    k.uTb = p.sb([128, 2, 128], BF16, name='uTb')
    k.hmask = p.sb([128, 2], name='hmask')
    p.memset(k.hmask[:], 0.0, writes=['hmask'])
    p.memset(k.hmask[0:64, 0:1], 1.0, writes=['hmask'])
    p.memset(k.hmask[64:128, 1:2], 1.0, writes=['hmask'])
    k.hmask8 = p.sb([128, 2], name='hmask8')
    p.ts(k.hmask8[:], k.hmask[:], 0.125, None, ALU.mult, reads=['hmask'], writes=['hmask8'])
    k.scr = {}
    for nme, shp in (('q', [NS, 256]), ('k', [NS, 256]), ('v', [NS, 256]), ('a', [NS, 256]), ('b', [NS, 256]),
                     ('c', [NS, 256]), ('o', [NS, 256]), ('s', [NS, 64])):
        k.scr[nme] = nc.dram_tensor('scr_' + nme, shp, F32, kind="Internal").ap()
    k.sm = p.sb([128, 64], name='smallsc')

    k.NB = 3
    k.wb = [p.sb([128, 8, 512], BF16, name='wb%d' % i) for i in range(k.NB)]
    k.wcnt = 0

    k.bcs = p.sb([128, BCW], name='bcs')
    k.bcscr = nc.dram_tensor('bcscr', [DEPTH, BCW], F32, kind="Internal").ap()
    k.tab = p.sb([128, 4, 8, 128], name='tab')
    k.tabscr = nc.dram_tensor('tabscr', [DEPTH, 128, 4096], F32, kind="Internal").ap()
    k.wblk = {}
    k.nc = nc
    convert_weights(k)
    k.W = [setup_layer(k, l) for l in range(nlayers_setup)]

    k.X = [p.sb([128, D], name='X%d' % i) for i in range(NT)]
    k.P = [p.sb([128, DIN], name='P%d' % i) for i in range(NT)]
    k.Y = [p.sb([128, D], name='Y%d' % i) for i in range(NT)]
    if NT >= 2:
        k.repS = k.P[1][:, 0:2048]
        k.rep = k.P[1][:, 2048:2560].rearrange("p (a b) -> p a b", b=64)
    else:
        k.repS = p.sb([128, 2048], name='repS')[:, :]
        k.rep = p.sb([128, 8, 64], name='rep')[:, :, :]
    NTOK = NT * 128
    k.hT = p.sb([128, 8, NTOK], BF16, name='hT')
    k.actT = p.sb([128, NFF, NTOK], BF16, name='actT')
    k.ugb = p.sb([128, 4, NTOK + 2], name='ugb')
    k.ugb2 = k.ugb
    k.mqT = p.sb([128, 4, 128], BF16, name='mqT')
    k.mkT = p.sb([128, 4, 128], BF16, name='mkT')
    k.mAT = p.sb([128, 4, 128], BF16, name='mAT')
    k.mV = p.sb([128, 264], BF16, name='mV')
    k.mKh = p.sb([128, 256], BF16, name='mKh')
    k.gsm = p.sb([128, 16], name='gsm')
    k.rstdm = p.sb([128, max(NT, 1)], name='rstdm')
    k.rwb = p.sb([128, 8, 512], BF16, name='rwb')
    k.rwb2 = p.sb([128, 5, 256], BF16, name='rwb2')
    k.rwT = p.sb([128, 12, 128], BF16, name='rwT')
    k.mVa = p.sb([128, 4, 65], BF16, name='mVa')
    p.memset(k.mVa[:], 1.0, writes=['mVa'])
    k.msm = p.sb([128, 32], name='msm')
    k.msm2 = p.sb([128, 32], name='msm2')
    k.msm3 = p.sb([128, 16], name='msm3')
    k.scr['s16'] = nc.dram_tensor('scr_s16', [NS, 16], F32, kind="Internal").ap()

    groups = [dict(kind='s', tiles=[(0, NS)])]
    for g0 in range(0, T // 128, NT):
        groups.append(dict(kind='p', tiles=[(g0 + i, 128) for i in range(NT)], first=(g0 == 0),
                           last=(g0 + NT >= T // 128)))
    for g in (groups if ngroups is None else groups[:ngroups]):
        run_group(k, g)

    p.emit()
    p.close()
    return nc


def setup_layer(k, l):
    p, I = k.p, k.I
    W = {}
    nm = lambda s: '%s_l%d' % (s, l)

    def colvec(name, src_ap, ncols):
        t = p.sb([128, ncols], name=nm(name))
        p.dma(t[:], src_ap, writes=[nm(name)], allow_slow_non_contiguous=True)
        return t

    W['gmix'] = colvec('gmix', I['norm_mix'][l].rearrange("(k q) -> q k", q=128), 8)
    W['gffn'] = colvec('gffn', I['norm_ffn'][l].rearrange("(k q) -> q k", q=128), 8)
    W['convb'] = colvec('convb', I['ffn_conv_b'][l].rearrange("(c q) -> q c", q=128), NFF)
    cw = p.sb([128, 3, NFF], name=nm('convw'))
    for j in range(3):
        p.dma(cw[:, j, :], I['ffn_conv_w'][l, j].rearrange("(c q) -> q c", q=128), writes=[nm('convw')],
              allow_slow_non_contiguous=True)
    W['convw'] = cw
    for name, off in BO.items():
        src = I[name]
        w = src.shape[-1]
        p.dma(k.bcscr[l:l + 1, off:off + w], src[l:l + 1, :], writes=['bcscr'])
    W['bc'] = k.bcs
    wg = p.sb([128, 2, 256], name=nm('wglu'))
    p.dma(wg[:], I['s5_w_glu'][l].rearrange("(c q) n -> q c n", q=128), writes=[nm('wglu')])
    W['wglu'] = wg

    bre = p.sb([128, 8, 128], BF16, name=nm('bre'))
    bim = p.sb([128, 8, 128], BF16, name=nm('bim'))
    cre = p.sb([128, 2, 128], name=nm('cre'))
    cim = p.sb([128, 2, 128], name=nm('cim'))
    scs = k.sc
    for dst, key, src in ((bre, 'bre', 's5_b_re'), (bim, 'bim', 's5_b_im')):
        Bn = scs[1][:, 0:128].rearrange("p (j h) -> p j h", h=16)
        p.dma(Bn, I[src][l].rearrange("(j gl) p h -> (gl p) j h", gl=2), writes=['sc1'])
        for j in range(8):
            c0 = ((2 * j) % 8) * 16
            zi = 2 + (j % 2)
            Z, zk, bk = scs[zi][:, 0:128], 'sc%d' % zi, 'b%d' % zi
            p.memset(Z, 0.0, writes=[zk])
            p.cp(Z[0:64, c0:c0 + 16], Bn[0:64, j, :], reads=['sc1'], writes=[zk])
            p.cp(Z[64:128, c0 + 16:c0 + 32], Bn[64:128, j, :], reads=['sc1'], writes=[zk])
            p.tr(k.bank(zi)[:, 0:128], Z, k.ident[:, :], reads=[zk, 'ident'], writes=[bk])
            p.cp(dst[:, j, :], k.bank(zi)[:, 0:128], reads=[bk], writes=[nm(key)], eng='scalar')
    for dst, key, src in ((cre, 'cre', 's5_c_re'), (cim, 'cim', 's5_c_im')):
        Cn = scs[4][:, 0:128].rearrange("p (c q) -> p c q", q=64)
        p.dma(Cn, I[src][l].rearrange("(c g) h p -> (g h) c p", g=8), writes=['sc4'])
        for kc in range(2):
            Zc = scs[5][:, kc * 128:(kc + 1) * 128]
            p.ts(Zc[:, 0:64], Cn[:, kc, :], k.gpar[:, 0:1], None, ALU.mult, reads=['sc4', 'gpar'], writes=['sc5'])
            p.ts(Zc[:, 64:128], Cn[:, kc, :], k.gpar[:, 1:2], None, ALU.mult, reads=['sc4', 'gpar'], writes=['sc5'])
            p.tr(k.bank(4 + kc)[:, 0:128], Zc, k.ident[:, :], reads=['sc5', 'ident'], writes=['b%d' % (4 + kc)])
            p.cp(dst[:, kc, :], k.bank(4 + kc)[:, 0:128], reads=['b%d' % (4 + kc)], writes=[nm(key)], eng='scalar')
    p.ts(cim[:], cim[:], -1.0, None, ALU.mult, reads=[nm('cim')], writes=[nm('cim')])
    W['bre'], W['bim'], W['cre'], W['ncim'] = bre, bim, cre, cim

    lre = colvec('lamre', I['s5_lam_re'][l].rearrange("(j gl) p -> (gl p) j", gl=2), 8)
    lim = colvec('lamim', I['s5_lam_im'][l].rearrange("(j gl) p -> (gl p) j", gl=2), 8)
    ldt = p.sb([128, 8], name=nm('ldt'))
    for gl in range(2):
        src = I['s5_log_dt'][l:l + 1, :].rearrange("o (j gl) -> o gl j", gl=2)[:, gl, :]
        p.dma(ldt[gl * 64:gl * 64 + 64, :], src.partition_broadcast(64), writes=[nm('ldt')],
              allow_slow_non_contiguous=True)
    sv = p.sb([128, 12, 8], name=nm('s5sv'))
    svk = nm('s5sv')
    rw_ = dict(reads=[svk, nm('lamre'), nm('lamim'), nm('ldt')], writes=[svk])
    dt, a_, th, na = sv[:, 0, :], sv[:, 1, :], sv[:, 2, :], sv[:, 3, :]
    p.act(dt, ldt[:], AF.Exp, **rw_)
    p.tt(a_, lre[:], dt, ALU.mult, **rw_)
    p.tt(th, lim[:], dt, ALU.mult, **rw_)
    p.ts(na, a_, -1.0, None, ALU.mult, **rw_)
    sc = k.sc
    S = lambda i: sc[i][:].rearrange("p (j i) -> p j i", i=128)
    kb = k.kvec[:].unsqueeze(1).to_broadcast([128, 8, 128])
    bj = lambda v: v.unsqueeze(2).to_broadcast([128, 8, 128])
    allk = ['sc%d' % i for i in range(8)]
    rws = dict(reads=allk + [svk, 'kvec', 'halfpi'], writes=allk)
    p.tt(S(0), kb, bj(a_), ALU.mult, **rws)
    p.act(sc[0][:], sc[0][:], AF.Exp, **rws)
    p.tt(S(1), kb, bj(na), ALU.mult, **rws)
    p.act(sc[1][:], sc[1][:], AF.Exp, **rws)
    p.tt(S(2), kb, bj(th), ALU.mult, **rws)
    qi = k.qi
    p.ts(sc[3][:], sc[2][:], float(1.0 / (2 * np.pi)), None, ALU.mult, **rws)
    p.cp(qi[:], sc[3][:], reads=allk, writes=['sc7'])
    p.cp(sc[3][:], qi[:], reads=['sc7'], writes=allk)
    p.stt(sc[2][:], sc[3][:], float(-2 * np.pi), sc[2][:], ALU.mult, ALU.add, **rws)
    p.act(sc[3][:], sc[2][:], AF.Sin, scale=0.5, **rws)
    p.act(sc[4][:], sc[2][:], AF.Abs, **rws)
    p.act(sc[4][:], sc[4][:], AF.Sin, scale=-0.5, bias=k.halfpi[:], **rws)
    p.stt(sc[5][:], sc[3][:], 2.0, sc[4][:], ALU.mult, ALU.mult, **rws)
    p.tt(sc[6][:], sc[3][:], sc[3][:], ALU.mult, **rws)
    p.ts(sc[6][:], sc[6][:], -2.0, 1.0, ALU.mult, ALU.add, **rws)
    ep_re, ep_im, ei_re, ei_im = (k.tab[:, i, :, :] for i in range(4))
    tk = ['tab']
    rwt = dict(reads=allk + tk + [svk], writes=tk + allk + [svk])
    F = lambda t: t.rearrange("p j i -> p (j i)")
    p.tt(F(ep_re), sc[0][:], sc[6][:], ALU.mult, **rwt)
    p.tt(F(ep_im), sc[0][:], sc[5][:], ALU.mult, **rwt)
    p.tt(sc[2][:], sc[1][:], sc[6][:], ALU.mult, **rwt)
    p.stt(sc[3][:], sc[1][:], -1.0, sc[5][:], ALU.mult, ALU.mult, **rwt)
    nre, nim, den, cre_, cim_, t1, t2 = (sv[:, i, :] for i in range(4, 11))
    rwv = dict(reads=[svk, nm('lamre'), nm('lamim')] + tk, writes=[svk])
    p.ts(nre, ep_re[:, :, 0], -1.0, None, ALU.add, **rwv)
    p.cp(nim, ep_im[:, :, 0], **rwv)
    p.tt(den, lre[:], lre[:], ALU.mult, **rwv)
    p.tt(t1, lim[:], lim[:], ALU.mult, **rwv)
    p.tt(den, den, t1, ALU.add, **rwv)
    p.add('vector', lambda e: e.reciprocal(den, den), **rwv)
    p.tt(t1, nre, lre[:], ALU.mult, **rwv)
    p.tt(t2, nim, lim[:], ALU.mult, **rwv)
    p.tt(t1, t1, t2, ALU.add, **rwv)
    p.tt(cre_, t1, den, ALU.mult, **rwv)
    p.tt(t1, nim, lre[:], ALU.mult, **rwv)
    p.tt(t2, nre, lim[:], ALU.mult, **rwv)
    p.tt(t1, t1, t2, ALU.subtract, **rwv)
    p.tt(cim_, t1, den, ALU.mult, **rwv)
    p.tt(S(4), S(2), bj(cre_), ALU.mult, **rwt)
    p.tt(S(5), S(3), bj(cim_), ALU.mult, **rwt)
    p.tt(F(ei_re), sc[4][:], sc[5][:], ALU.subtract, **rwt)
    p.tt(S(4), S(3), bj(cre_), ALU.mult, **rwt)
    p.tt(S(5), S(2), bj(cim_), ALU.mult, **rwt)
    p.tt(F(ei_im), sc[4][:], sc[5][:], ALU.add, **rwt)
    W['ep_re'], W['ep_im'], W['ei_re'], W['ei_im'] = ep_re, ep_im, ei_re, ei_im
    W['tk'] = tk
    p.dma(k.tabscr[l], k.tab[:].rearrange("p a j i -> p (a j i)"), reads=['tab'], writes=['tabscr'])
    W['hp_re'] = p.sb([128, 8], name=nm('hp_re'))
    W['hp_im'] = p.sb([128, 8], name=nm('hp_im'))
    p.memset(W['hp_re'][:], 0.0, writes=[nm('hp')])
    p.memset(W['hp_im'][:], 0.0, writes=[nm('hp')])
    W['carry'] = p.sb([128, NFF, 2], name=nm('carry'))
    p.memset(W['carry'][:], 0.0, writes=[nm('carry')])

    wa = p.sb([16, 256], name=nm('walpha'))
    p.dma(wa[:], I['gla_w_alpha'][l], writes=[nm('walpha')])
    W['walpha'] = wa
    nb = colvec('nbalpha', I['gla_b_alpha'][l].rearrange("(c q) -> q c", q=128), 2)
    p.ts(nb[:], nb[:], -1.0, None, ALU.mult, reads=[nm('nbalpha')], writes=[nm('nbalpha')])
    W['nbalpha'] = nb
    W['glaS'] = p.sb([128, 2, 64], name=nm('glaS'))
    W['glaSb'] = p.sb([128, 2, 64], BF16, name=nm('glaSb'))
    p.memset(W['glaS'][:], 0.0, writes=[nm('glaS')])
    p.memset(W['glaSb'][:], 0.0, writes=[nm('glaS')])

    W['mlCT'] = p.sb([128, 2, 65], name=nm('mlCT'))
    W['mlCTb'] = p.sb([128, 2, 65], BF16, name=nm('mlCTb'))
    W['mprev'] = p.sb([128, 4], name=nm('mprev'))
    p.memset(W['mlCT'][:], 0.0, writes=[nm('mlCT')])
    p.memset(W['mlCTb'][:], 0.0, writes=[nm('mlCT')])
    p.memset(W['mprev'][:], 0.0, writes=[nm('mprev')])

    for key, src, r0, r1 in (('rwWw', 'rw_w2', 0, 32), ('rwWa', 'rw_a2', 32, 64), ('rwWg', 'rw_g2', 64, 128)):
        t = p.sb([128, 256], name=nm(key))
        p.memset(t[:], 0.0, writes=[nm(key)])
        p.dma(t[r0:r1, :], I[src][l], writes=[nm(key)])
        W[key] = t
    W['rwA'] = p.sb([128, 2, 64], name=nm('rwA'))
    p.memset(W['rwA'][:], 0.0, writes=[nm('rwA')])
    W['rwAb'] = p.sb([128, 2, 64], BF16, name=nm('rwAb'))
    p.memset(W['rwAb'][:], 0.0, writes=[nm('rwA')])
    W['rcprev'] = p.sb([128, 896], name=nm('rcprev'))
    p.memset(W['rcprev'][:], 0.0, writes=[nm('rcprev')])
    return W


def wslice(k, name, l, r0, r1, c0, c1):
    key = 'ws_%s_%d_%d_%d' % (name, l, r0, c0)
    if key not in k.wblk:
        k.wblk[key] = k.nc.dram_tensor(key, [128, (r1 - r0) // 128, c1 - c0], BF16, kind="Internal").ap()
    return k.wblk[key], key


def convert_weights(k):
    p, I = k.p, k.I
    for l in range(DEPTH):
        blocks = []
        for c0 in range(0, DIN, 512):
            blocks.append(('w_in', 0, D, c0, min(c0 + 512, DIN)))
        for half in range(2):
            blocks.append(('w_out', 0, D, half * 512, half * 512 + 512))
        for jb in range(0, DFF, 512):
            w = min(512, DFF - jb)
            blocks.append(('ffn_w_up', 0, D, jb, jb + w))
            blocks.append(('ffn_w_up', 0, D, DFF + jb, DFF + jb + w))
        for half in range(2):
            for (k0, kc_n) in ((0, 8), (8, 8), (16, 6)):
                blocks.append(('ffn_w_down', k0 * 128, (k0 + kc_n) * 128, half * 512, half * 512 + 512))
        for name, r0, r1, c0, c1 in blocks:
            dst, key = wslice(k, name, l, r0, r1, c0, c1)
            p.dma(dst, I[name][l, r0:r1, c0:c1].rearrange("(k q) c -> q k c", q=128), writes=[key], eng='gpsimd')


def wblock(k, name, l, r0, r1, c0, c1):
    p = k.p
    i = k.wcnt % k.NB
    k.wcnt += 1
    key = 'wb%d' % i
    src, ckey = wslice(k, name, l, r0, r1, c0, c1)
    kc = (r1 - r0) // 128
    p.dma(k.wb[i][:, 0:kc, 0:c1 - c0], src, reads=[ckey], writes=[key])
    return k.wb[i], key


def rmsnorm_T(k, xt, xkey, n, gcol, gkey, tok0, outT, outkey, rstd_out=None, rstd_key=None):
    p = k.p
    sm = k.sm
    p.act(k.sc[7][:n, :], xt[:n, :], AF.Square, reads=[xkey], writes=['sc7', 'sm'], accum_out=sm[:n, 0:1])
    p.ts(sm[:n, 1:2], sm[:n, 0:1], 1.0 / D, EPS, ALU.mult, ALU.add, reads=['sm'], writes=['sm'])
    if rstd_out is not None:
        p.tt(rstd_out[:n, 0:1], sm[:n, 1:2], k.mhalf[:n, 0:1], ALU.pow, reads=['sm', 'mhalf'], writes=[rstd_key], eng='gpsimd')
        src, skey = xt, xkey
    else:
        p.tt(sm[:n, 2:3], sm[:n, 1:2], k.mhalf[:n, 0:1], ALU.pow, reads=['sm', 'mhalf'], writes=['sm'], eng='gpsimd')
        p.ts(k.sc[7][:n, :], xt[:n, :], sm[:n, 2:3], None, ALU.mult, reads=[xkey, 'sm'], writes=['sc7'])
        src, skey = k.sc[7], 'sc7'
    for kc in range(8):
        p.tr(k.pp[0][:, kc * n:(kc + 1) * n], src[:n, kc * 128:(kc + 1) * 128], k.ident[:n, :n],
             reads=[skey, 'ident'], writes=['b0', 'b1'])
    p.tt(outT[:, :, tok0:tok0 + n], v3(k.pp[0], n), gcol[:, :].unsqueeze(2).to_broadcast([128, 8, n]),
         ALU.mult, reads=['b0', 'b1', gkey], writes=[outkey])


def gelu(k, out, x, t1, t2, rw):
    p = k.p
    p.tt(t1, x, x, ALU.mult, **rw)
    p.ts(t1, t1, 0.044715, 1.0, ALU.mult, ALU.add, **rw)
    p.tt(t1, t1, x, ALU.mult, **rw)
    p.act(t2, t1, AF.Sigmoid, scale=GELU_C, **rw)
    p.tt(out, x, t2, ALU.mult, **rw)


def s5_tile(k, l, g, ti, n, tidx):
    p, W = k.p, k.W[l]
    nm = lambda s: '%s_l%d' % (s, l)
    Pt, Yt = k.P[ti], k.Y[ti]
    pk, yk = pkeys(ti, 0, 256), 'Y%d' % ti
    sc = k.sc
    prompt = g['kind'] == 'p'
    for c in range(2):
        p.tr(k.bank(7)[:, c * n:(c + 1) * n], Pt[:n, c * 128:(c + 1) * 128], k.ident[:n, :n],
             reads=[pk, 'ident'], writes=['b7'])
    uT = k.uTb[:, :, 0:n]
    p.cp(uT, k.bank(7)[:, 0:2 * n].rearrange("p (c i) -> p c i", i=n), reads=['b7'], writes=['sc0'], eng='scalar')
    for j in range(8):
        p.mm(k.pp[1][:, j * n:(j + 1) * n], W['bre'][:, j, :], uT[:, j // 4, :],
             reads=['sc0', nm('bre')], writes=['b2', 'b3'])
        p.mm(k.pp[2][:, j * n:(j + 1) * n], W['bim'][:, j, :], uT[:, j // 4, :],
             reads=['sc0', nm('bim')], writes=['b4', 'b5'])
    if prompt:
        tab = lambda t: t[:, :, :]
    else:
        tab = lambda t: t[:, :, 0:1].to_broadcast([128, 8, n])
    bre3, bim3 = v3(k.pp[1], n), v3(k.pp[2], n)
    S = lambda i: v3(sc[i], n)
    tk = W['tk']
    rw = dict(reads=['b2', 'b3', 'b4', 'b5', 'sc1', 'sc2', 'sc3', 'sc4', 'sc5', 'sc6'] + tk,
              writes=['sc1', 'sc2', 'sc3', 'sc4', 'sc5', 'sc6'])
    p.tt(S(1), bre3, tab(W['ei_re']), ALU.mult, **rw)
    p.tt(S(2), bim3, tab(W['ei_im']), ALU.mult, **rw)
    p.tt(S(3), S(1), S(2), ALU.subtract, **rw)
    p.tt(S(1), bim3, tab(W['ei_re']), ALU.mult, **rw)
    p.tt(S(2), bre3, tab(W['ei_im']), ALU.mult, **rw)
    p.tt(S(4), S(1), S(2), ALU.add, **rw)
    if prompt and STOP[0] == 11:
        return
    if prompt:
        for j in range(8):
            for zi, ci, hp in ((3, 5, 'hp_re'), (4, 6, 'hp_im')):
                p.add('vector', lambda e, zi=zi, ci=ci, hp=hp, j=j: e.tensor_tensor_scan(
                    sc[ci][:, j * n:(j + 1) * n], k.ones[:, 0:n], sc[zi][:, j * n:(j + 1) * n],
                    W[hp][:, j:j + 1], ALU.mult, ALU.add),
                    reads=['sc3', 'sc4', 'ones', nm('hp')], writes=['sc%d' % ci])
    else:
        for si, ci, src in ((3, 5, 'st_s5_re'), (4, 6, 'st_s5_im')):
            p.dma(sc[7][:n, :], k.I[src][l], writes=['sc7'])
            for j in range(8):
                p.tr(k.bank(0)[:, j * n:(j + 1) * n], sc[7][:n, j * 128:(j + 1) * 128], k.ident[:n, :n],
                     reads=['sc7', 'ident'], writes=['b0'])
            p.tt(S(ci), S(si), v3(k.bank(0), n), ALU.add, reads=['b0', 'sc%d' % si], writes=['sc%d' % ci])
    if prompt and STOP[0] == 12:
        return
    rw = dict(reads=['sc1', 'sc2', 'sc3', 'sc4', 'sc5', 'sc6'] + tk, writes=['sc1', 'sc2', 'sc3', 'sc4'])
    p.tt(S(3), S(5), tab(W['ep_re']), ALU.mult, **rw)
    p.tt(S(4), S(6), tab(W['ep_im']), ALU.mult, **rw)
    p.tt(S(1), S(3), S(4), ALU.subtract, **rw)
    p.tt(S(3), S(6), tab(W['ep_re']), ALU.mult, **rw)
    p.tt(S(4), S(5), tab(W['ep_im']), ALU.mult, **rw)
    p.tt(S(2), S(3), S(4), ALU.add, **rw)
    if prompt:
        p.cp(W['hp_re'][:, :], S(1)[:, :, n - 1], reads=['sc1'], writes=[nm('hp')])
        p.cp(W['hp_im'][:, :], S(2)[:, :, n - 1], reads=['sc2'], writes=[nm('hp')])
        if g['last'] and tidx == T // 128 - 1:
            for hp, dst in (('hp_re', 'p_s5_re'), ('hp_im', 'p_s5_im')):
                p.dma(k.O[dst][l].rearrange("(j q) -> q j", q=128), W[hp][:, :], reads=[nm('hp')],
                      is_out=True, allow_slow_non_contiguous=True)
    else:
        for hi, dst in ((1, 's_s5_re'), (2, 's_s5_im')):
            for j in range(8):
                p.tr(k.pp[0][:n, j * 128:(j + 1) * 128], sc[hi][:, j * n:(j + 1) * n], k.ident[:, :],
                     reads=['sc%d' % hi, 'ident'], writes=['b0', 'b1'] if j >= 4 else ['b0'])
            p.cp(sc[7][:n, :], k.pp[0][:n, :], reads=['b0', 'b1'], writes=['sc7'], eng='scalar')
            p.dma(k.O[dst][l], sc[7][:n, :], reads=['sc7'], is_out=True)
    yield
    if prompt and STOP[0] == 13:
        return
    for j in range(8):
        cc0 = ((2 * j) % 8) * 16
        p.mm(k.bank(6)[:n, 32 * j:32 * j + 32], S(1)[:, j, :], W['cre'][:, j // 4, cc0:cc0 + 32], start=True, stop=False,
             reads=['sc1', nm('cre')], writes=['b6'])
        p.mm(k.bank(6)[:n, 32 * j:32 * j + 32], S(2)[:, j, :], W['ncim'][:, j // 4, cc0:cc0 + 32], start=False, stop=True,
             reads=['sc2', nm('cim')], writes=['b6'])
    if prompt and STOP[0] == 14:
        return
    bcs = W['bc']
    rw = dict(reads=['b6', pk, 'bc', 'sc3', 'sc4', 'sc5'], writes=['sc3', 'sc4', 'sc5'])
    ys, t1, t2 = sc[3][:n, 0:256], sc[4][:n, 0:256], sc[5][:n, 0:256]
    p.tt(ys, Pt[:n, 0:256], bcs[:n, BO['s5_d']:BO['s5_d'] + 256], ALU.mult, **rw)
    p.tt(ys, ys, k.bank(6)[:n, 0:256], ALU.add, **rw)
    z = sc[3][:n, 256:512]
    gelu(k, z, ys, t1, t2, rw)
    yield
    for c in range(2):
        p.tr(k.bank(7)[:, c * n:(c + 1) * n], sc[3][:n, 256 + c * 128:256 + (c + 1) * 128], k.ident[:n, :n],
             reads=['sc3', 'ident'], writes=['b7'])
    p.cp(sc[4][:, 0:2 * n], k.bank(7)[:, 0:2 * n], reads=['b7'], writes=['sc4'], eng='scalar')
    zT = sc[4][:, 0:2 * n].rearrange("p (c i) -> p c i", i=n)
    for c in range(2):
        p.mm(k.bank(6)[:n, 0:256], zT[:, c, :], W['wglu'][:, c, :], start=(c == 0), stop=(c == 1),
             reads=['sc4', nm('wglu')], writes=['b6'])
    p.act(t2, k.bank(6)[:n, 0:256], AF.Sigmoid, reads=['b6'], writes=['sc5'])
    p.tt(Yt[:n, 0:256], z, t2, ALU.mult, reads=['sc3', 'sc5'], writes=[yk])


def head_post(k, o_ap, o_reads, n, gvec, gvkey, gate, gate_reads, out_ap, out_key, layernorm=False,
              extra=None, extra_reads=()):
    p = k.p
    sm = k.sm
    s7 = k.sc[7]
    h3 = lambda ap: ap.rearrange("p (h d) -> p h d", d=64)
    yv = s7[:n, 256:512]
    rw = dict(reads=list(o_reads) + ['sc7', 'sm'], writes=['sc7', 'sm'])
    if layernorm:
        p.red(sm[:n, 4:8], h3(o_ap), ALU.add, **rw)
        p.ts(sm[:n, 4:8], sm[:n, 4:8], -1.0 / 64, None, ALU.mult, **rw)
        p.tt(h3(yv), h3(o_ap), sm[:n, 4:8].unsqueeze(2).to_broadcast([n, 4, 64]), ALU.add, **rw)
        src = yv
        eps = GN_EPS
    else:
        src = o_ap
        eps = EPS
    p.act(s7[:n, 0:256], src, AF.Square, **rw)
    p.red(sm[:n, 8:12], h3(s7[:n, 0:256]), ALU.add, **rw)
    p.ts(sm[:n, 8:12], sm[:n, 8:12], 1.0 / 64, eps, ALU.mult, ALU.add, **rw)
    p.tt(sm[:n, 12:16], sm[:n, 8:12], k.mhalf[:n, 0:4], ALU.pow, reads=rw['reads'] + ['mhalf'], writes=rw['writes'], eng='gpsimd')
    p.tt(h3(yv), h3(src), sm[:n, 12:16].unsqueeze(2).to_broadcast([n, 4, 64]), ALU.mult, **rw)
    p.tt(yv, yv, gvec, ALU.mult, reads=['sc7', gvkey], writes=['sc7'])
    if extra is not None:
        p.tt(yv, yv, extra, ALU.add, reads=['sc7'] + list(extra_reads), writes=['sc7'])
    p.tt(out_ap, yv, gate, ALU.mult, reads=['sc7'] + list(gate_reads), writes=[out_key])


def gla_pre(k, l, ti, n, prompt):
    p, W = k.p, k.W[l]
    nm = lambda s: '%s_l%d' % (s, l)
    Pt, pk, sc = k.P[ti], pkeys(ti, 1288, 2328), k.sc
    p.tr(k.bank(2)[0:16, 0:n], Pt[:n, OFF['ga']:OFF['ga'] + 16], k.ident[:n, :n], reads=[pk, 'ident'], writes=['b2'])
    p.cp(sc[0][0:16, 0:n], k.bank(2)[0:16, 0:n], reads=['b2'], writes=['sc0'], eng='scalar')
    for c in range(2):
        p.mm(k.bank(3)[:, c * n:(c + 1) * n], W['walpha'][0:16, c * 128:(c + 1) * 128], sc[0][0:16, 0:n],
             reads=['sc0', nm('walpha')], writes=['b3'])
    for c in range(2):
        p.act(sc[1][:, c * n:(c + 1) * n], k.bank(3)[:, c * n:(c + 1) * n], AF.Exp, scale=-1.0,
              bias=W['nbalpha'][:, c:c + 1], reads=['b3', nm('nbalpha')], writes=['sc1'])
    p.act(sc[1][:, 0:2 * n], sc[1][:, 0:2 * n], AF.Ln, bias=k.onec[:], reads=['sc1', 'onec'], writes=['sc1'])
    if prompt:
        for c in range(2):
            p.add('vector', lambda e, c=c: e.tensor_tensor_scan(
                sc[2][:, c * n:(c + 1) * n], k.ones[:, 0:n], sc[1][:, c * n:(c + 1) * n], 0.0, ALU.mult, ALU.add),
                reads=['sc1', 'ones'], writes=['sc2'])
    else:
        p.cp(sc[2][:, 0:2 * n], sc[1][:, 0:2 * n], reads=['sc1'], writes=['sc2'])


def gla_prompt(k, l, g, ti, n, tidx):
    p, W = k.p, k.W[l]
    nm = lambda s: '%s_l%d' % (s, l)
    Pt, pk, sc = k.P[ti], pkeys(ti, 1288, 2328), k.sc
    gla_pre(k, l, ti, n, True)
    cum = sc[2][:, 0:2 * n]
    cl = cum.rearrange("p (c i) -> p c i", i=n)[:, :, n - 1]
    gs = k.gsm
    p.ts(gs[:, 0:2], cl, -1.0 / 16, None, ALU.mult, reads=['sc2'], writes=['gsm'])
    p.act(gs[:, 2:4], cl, AF.Exp, scale=-1.0 / 16, reads=['sc2'], writes=['gsm'])
    p.act(sc[3][:, 0:2 * n], cum, AF.Exp, scale=-1.0 / 16, reads=['sc2'], writes=['sc3'])
    p.act(sc[4][:, 0:2 * n], cum, AF.Exp, scale=1.0 / 16, reads=['sc2'], writes=['sc4'])
    for c in range(2):
        p.act(sc[5][:, c * n:(c + 1) * n], sc[2][:, c * n:(c + 1) * n], AF.Exp, scale=1.0 / 16,
              bias=gs[:, c:c + 1], reads=['sc2', 'gsm'], writes=['sc5'])
    for c in range(2):
        p.tr(k.bank(2)[:, c * n:(c + 1) * n], Pt[:n, OFF['gq'] + c * 128:OFF['gq'] + (c + 1) * 128],
             k.ident[:n, :n], reads=[pk, 'ident'], writes=['b2'])
        p.tr(k.bank(2)[:, (2 + c) * n:(3 + c) * n], Pt[:n, OFF['gk'] + c * 128:OFF['gk'] + (c + 1) * 128],
             k.ident[:n, :n], reads=[pk, 'ident'], writes=['b2'])
    for h in range(4):
        pr, hl = h // 2, h % 2
        p.stt(k.mqT[:, h, :n], k.bank(2)[:, pr * n:(pr + 1) * n], k.hmask8[:, hl:hl + 1],
              sc[3][:, pr * n:(pr + 1) * n], ALU.mult, ALU.mult, reads=['b2', 'hmask8', 'sc3'], writes=['mqT'])
        p.stt(k.mkT[:, h, :n], k.bank(2)[:, (2 + pr) * n:(3 + pr) * n], k.hmask[:, hl:hl + 1],
              sc[4][:, pr * n:(pr + 1) * n], ALU.mult, ALU.mult, reads=['b2', 'hmask', 'sc4'], writes=['mkT'])
    p.tt(sc[6][:, 0:2 * n], k.bank(2)[:, 2 * n:4 * n], sc[5][:, 0:2 * n], ALU.mult, reads=['b2', 'sc5'], writes=['sc6'])
    for h in range(4):
        p.mm(k.bank(3)[:n, h * n:(h + 1) * n], k.mkT[:, h, :n], k.mqT[:, h, :n], reads=['mkT', 'mqT'], writes=['b3'])
    p.tt(k.mAT[:n, :, :n], v3(k.bank(3), n, 4)[:n], k.triu[:n, :n].unsqueeze(1).to_broadcast([n, 4, n]), ALU.mult,
         reads=['b3', 'triu'], writes=['mAT'])
    p.cp(k.mV[:n, 0:256], Pt[:n, OFF['gv']:OFF['gv'] + 256], reads=[pk], writes=['mV'], eng='scalar')
    for h in range(4):
        p.mm(k.bank(4)[:n, h * 64:(h + 1) * 64], k.mAT[:n, h, :n], k.mV[:n, h * 64:(h + 1) * 64], start=True, stop=False,
             reads=['mAT', 'mV'], writes=['b4'])
        p.mm(k.bank(4)[:n, h * 64:(h + 1) * 64], k.mqT[:, h, :n], W['glaSb'][:, h // 2, :], start=False, stop=True,
             reads=['mqT', nm('glaS')], writes=['b4'])
    p.cp(sc[0][:n, 256:512], k.bank(4)[:n, 0:256], reads=['b4'], writes=['sc0'], eng='scalar')
    for c in range(2):
        p.tr(k.bank(5)[:n, c * 128:(c + 1) * 128], sc[6][:, c * n:(c + 1) * n], k.ident[:, :],
             reads=['sc6', 'ident'], writes=['b5'])
    p.cp(k.mKh[:n, 0:256], k.bank(5)[:n, 0:256], reads=['b5'], writes=['mKh'])
    for pr in range(2):
        p.mm(k.bank(5)[:, pr * 256:(pr + 1) * 256], k.mKh[:n, pr * 128:(pr + 1) * 128], k.mV[:n, 0:256],
             reads=['mKh', 'mV'], writes=['b5'])
    S = W['glaS']
    for pr in range(2):
        for hl in range(2):
            r0 = hl * 64
            c0 = pr * 256 + (2 * pr + hl) * 64
            p.stt(S[r0:r0 + 64, pr, :], S[r0:r0 + 64, pr, :], gs[r0:r0 + 64, 2 + pr:3 + pr], k.bank(5)[r0:r0 + 64, c0:c0 + 64],
                  ALU.mult, ALU.add, reads=['b5', 'gsm', nm('glaS')], writes=[nm('glaS')])
    p.cp(W['glaSb'][:], S[:], reads=[nm('glaS')], writes=[nm('glaS')])
    if g['last'] and tidx == T // 128 - 1:
        for pr in range(2):
            p.dma(k.O['p_gla_S'][l].rearrange("(pr q v) -> pr q v", pr=2, v=64)[pr], S[:, pr, :],
                  reads=[nm('glaS')], is_out=True)
    p.act(sc[7][:n, 512:768], Pt[:n, OFF['gg']:OFF['gg'] + 256], AF.Sigmoid, reads=[pk], writes=['sc7'])
    p.tt(sc[7][:n, 512:768], sc[7][:n, 512:768], Pt[:n, OFF['gg']:OFF['gg'] + 256], ALU.mult, reads=[pk, 'sc7'], writes=['sc7'])
    head_post(k, sc[0][:n, 256:512], ['sc0'], n, W['bc'][:n, BO['gla_norm']:BO['gla_norm'] + 256], 'bc',
              sc[7][:n, 512:768], ['sc7'], k.Y[ti][:n, 512:768], 'Y%d' % ti)


def rep_gather(k, name, src_ap, src_reads, dst, width=64):
    p = k.p
    scr = k.scr[name]
    p.dma(scr, src_ap, reads=src_reads, writes=['scr_' + name])
    if width == 64:
        for vh in range(2):
            p.dma(dst[vh:128:2, :], scr.rearrange("i (h d) -> (i h) d", d=64), reads=['scr_' + name], writes=['rep'])
    else:
        p.dma(dst, scr.rearrange("i (q w) -> (i q) w", w=32), reads=['scr_' + name], writes=['rep'])


def rep_scatter(k, src, dst_sb, dst_key):
    p = k.p
    scr = k.scr['o']
    p.dma(scr.rearrange("i (q w) -> (i q) w", w=32), src, reads=['rep', 'repS'], writes=['scr_o'])
    p.dma(dst_sb, scr, reads=['scr_o'], writes=[dst_key])


def gla_sample(k, l, ti, n):
    p, W = k.p, k.W[l]
    nm = lambda s: '%s_l%d' % (s, l)
    Pt, pk, sc = k.P[ti], pkeys(ti, 1288, 2328), k.sc
    gla_pre(k, l, ti, n, False)
    p.act(sc[3][:, 0:2 * n], sc[2][:, 0:2 * n], AF.Exp, scale=-1.0 / 16, reads=['sc2'], writes=['sc3'])
    for c in range(2):
        p.tr(k.bank(2)[:n, c * 128:(c + 1) * 128], sc[3][:, c * n:(c + 1) * n], k.ident[:, :], reads=['sc3', 'ident'],
             writes=['b2'])
    p.cp(sc[4][:n, 0:256], k.bank(2)[:n, 0:256], reads=['b2'], writes=['sc4'], eng='scalar')
    rep = k.rep
    rep_gather(k, 'q', Pt[:n, OFF['gq']:OFF['gq'] + 256], [pk], rep[:, 0, :])
    rep_gather(k, 'k', Pt[:n, OFF['gk']:OFF['gk'] + 256], [pk], rep[:, 1, :])
    rep_gather(k, 'a', sc[4][:n, 0:256], ['sc4'], rep[:, 2, :])
    rep_gather(k, 'v', Pt[:n, OFF['gv']:OFF['gv'] + 256], [pk], rep[:, 3, 0:32], width=32)
    S = k.repS
    S3 = S[:].rearrange("p (d v) -> p d v", v=32)
    src = k.I['st_gla_S'][l].rearrange("i (h d vh v) -> i h vh d v", h=4, d=64, vh=2, v=32)
    for h in range(4):
        for vh in range(2):
            q = h * 2 + vh
            p.dma(S[q:128:8, :].rearrange("p (d v) -> p d v", v=32), src[:, h, vh, :, :], writes=['repS'])
    A = k.scd[0][:].rearrange("p (d v) -> p d v", v=32)
    B = k.scd[1][:].rearrange("p (d v) -> p d v", v=32)
    bd = lambda ap: ap.unsqueeze(2).to_broadcast([128, 64, 32])
    bv = lambda ap: ap.unsqueeze(1).to_broadcast([128, 64, 32])
    rw = dict(reads=['rep', 'repS', 'sc0', 'sc1', 'sc2', 'sc3'], writes=['sc0', 'sc1', 'sc2', 'sc3'])
    p.tt(A, S3, bd(rep[:, 2, :]), ALU.mult, **rw)
    p.tt(B, bd(rep[:, 1, :]), bv(rep[:, 3, 0:32]), ALU.mult, **rw)
    p.tt(S3, A, B, ALU.add, reads=['sc0', 'sc1', 'sc2', 'sc3'], writes=['repS'])
    dst = k.O['s_gla_S'][l].rearrange("i (h d vh v) -> i h vh d v", h=4, d=64, vh=2, v=32)
    for h in range(4):
        for vh in range(2):
            q = h * 2 + vh
            p.dma(dst[:, h, vh, :, :], S[q:128:8, :].rearrange("p (d v) -> p d v", v=32), reads=['repS'], is_out=True)
    p.tt(A, S3, bd(rep[:, 0, :]), ALU.mult, reads=['rep', 'repS'], writes=['sc0', 'sc1'])
    p.red(rep[:, 4, 0:32], k.scd[0][:].rearrange("p (d v) -> p v d", v=32), ALU.add, reads=['sc0', 'sc1'], writes=['rep'])
    p.ts(rep[:, 4, 0:32], rep[:, 4, 0:32], 0.125, None, ALU.mult, reads=['rep'], writes=['rep'])
    rep_scatter(k, rep[:, 4, 0:32], sc[0][:n, 256:512], 'sc0')
    p.act(sc[7][:n, 512:768], Pt[:n, OFF['gg']:OFF['gg'] + 256], AF.Sigmoid, reads=[pk], writes=['sc7'])
    p.tt(sc[7][:n, 512:768], sc[7][:n, 512:768], Pt[:n, OFF['gg']:OFF['gg'] + 256], ALU.mult, reads=[pk, 'sc7'], writes=['sc7'])
    head_post(k, sc[0][:n, 256:512], ['sc0'], n, W['bc'][:n, BO['gla_norm']:BO['gla_norm'] + 256], 'bc',
              sc[7][:n, 512:768], ['sc7'], k.Y[ti][:n, 512:768], 'Y%d' % ti)


def ml_gates(k, l, ti, n):
    p, W = k.p, k.W[l]
    nm = lambda s: '%s_l%d' % (s, l)
    Pt, pk = k.P[ti], pkeys(ti, 256, 1288)
    m = k.msm
    bcs = W['bc']
    rw = dict(reads=[pk, 'bc', 'msm', 'onec'], writes=['msm'])
    p.tt(m[:n, 0:4], Pt[:n, OFF['mi']:OFF['mi'] + 4], bcs[:n, 3200:3204], ALU.add, **rw)
    p.tt(m[:n, 4:8], Pt[:n, OFF['mf']:OFF['mf'] + 4], bcs[:n, 3204:3208], ALU.add, **rw)
    p.act(m[:n, 4:8], m[:n, 4:8], AF.Exp, scale=-1.0, **rw)
    p.act(m[:n, 4:8], m[:n, 4:8], AF.Ln, bias=k.onec[:n, :], **rw)


def ml_prompt(k, l, g, ti, n, tidx):
    p, W = k.p, k.W[l]
    nm = lambda s: '%s_l%d' % (s, l)
    Pt, pk, sc = k.P[ti], pkeys(ti, 256, 1288), k.sc
    m, m2, m3 = k.msm, k.msm2, k.msm3
    ml_gates(k, l, ti, n)
    rw = dict(reads=['msm', 'msm2', 'msm3', 'b2', nm('mprev')], writes=['msm', 'msm2', 'msm3'])
    p.mm(k.bank(2)[:n, 0:4], k.triu[:n, :n], m[:n, 4:8], reads=['triu', 'msm'], writes=['b2'])
    p.ts(m[:n, 8:12], k.bank(2)[:n, 0:4], -1.0, None, ALU.mult, **rw)
    p.cp(m[:n, 28:32], m[:n, 8:12], **rw)
    p.tt(m[:n, 12:16], m[:n, 0:4], m[:n, 8:12], ALU.subtract, **rw)
    for h in range(4):
        p.ts(sc[1][:n, h * 128:(h + 1) * 128], k.ident[:n, :n], m[:n, 12 + h:13 + h], None, ALU.mult,
             reads=['ident', 'msm'], writes=['sc1'])
    for h in range(4):
        p.mm(k.bank(2)[:n, h * 128:(h + 1) * 128], k.ones[:n, :n], sc[1][:n, h * 128:(h + 1) * 128],
             reads=['ones', 'sc1'], writes=['b2'])
    for h in range(4):
        p.stt(sc[2][:n, h * 128:(h + 1) * 128], k.bank(2)[:n, h * 128:(h + 1) * 128], m[:n, 8 + h:9 + h],
              k.maskadd[:n, :n], ALU.add, ALU.add, reads=['b2', 'msm', 'maskadd'], writes=['sc2'])
    p.red(m[:n, 16:20], sc[2][:n, 0:512].rearrange("p (h s) -> p h s", s=128), ALU.max, reads=['sc2', 'msm'], writes=['msm'])
    p.tt(m[:n, 20:24], m[:n, 8:12], W['mprev'][:n, :], ALU.add, **rw)
    p.tt(m[:n, 24:28], m[:n, 20:24], m[:n, 16:20], ALU.max, **rw)
    p.ts(m2[:n, 20:24], m[:n, 24:28], -1.0, None, ALU.mult, **rw)
    for h in range(4):
        p.act(sc[3][:n, h * 128:(h + 1) * 128], sc[2][:n, h * 128:(h + 1) * 128], AF.Exp, bias=m2[:n, 20 + h:21 + h],
              reads=['sc2', 'msm2'], writes=['sc3'])
    p.tt(m2[:n, 0:4], m[:n, 20:24], m2[:n, 20:24], ALU.add, **rw)
    p.act(m2[:n, 0:4], m2[:n, 0:4], AF.Exp, **rw)
    p.act(m2[:n, 12:16], m2[:n, 20:24], AF.Exp, **rw)
    for c in range(2):
        p.tr(k.bank(3)[:, c * n:(c + 1) * n], Pt[:n, OFF['mq'] + c * 128:OFF['mq'] + (c + 1) * 128],
             k.ident[:n, :n], reads=[pk, 'ident'], writes=['b3'])
        p.tr(k.bank(3)[:, (2 + c) * n:(3 + c) * n], Pt[:n, OFF['mk'] + c * 128:OFF['mk'] + (c + 1) * 128],
             k.ident[:n, :n], reads=[pk, 'ident'], writes=['b3'])
    for h in range(4):
        pr, hl = h // 2, h % 2
        p.ts(k.mqT[:, h, :n], k.bank(3)[:, pr * n:(pr + 1) * n], k.hmask[:, hl:hl + 1], None, ALU.mult,
             reads=['b3', 'hmask'], writes=['mqT'])
        p.ts(k.mkT[:, h, :n], k.bank(3)[:, (2 + pr) * n:(3 + pr) * n], k.hmask8[:, hl:hl + 1], None, ALU.mult,
             reads=['b3', 'hmask8'], writes=['mkT'])
    for h in range(4):
        p.mm(k.bank(4)[:n, h * n:(h + 1) * n], k.mqT[:, h, :n], k.mkT[:, h, :n], reads=['mqT', 'mkT'], writes=['b4'])
    p.tt(sc[4][:n, 0:512], sc[3][:n, 0:512], k.bank(4)[:n, 0:512], ALU.mult, reads=['sc3', 'b4'], writes=['sc4'])
    for h in range(4):
        p.tr(k.bank(5)[:n, h * n:(h + 1) * n], sc[4][:n, h * 128:(h + 1) * 128], k.ident[:n, :n],
             reads=['sc4', 'ident'], writes=['b5'])
    p.cp(k.mAT[:n, :, :n], v3(k.bank(5), n, 4)[:n], reads=['b5'], writes=['mAT'], eng='scalar')
    p.cp(k.mVa[:n, :, 0:64], Pt[:n, OFF['mv']:OFF['mv'] + 256].rearrange("p (h d) -> p h d", d=64), reads=[pk],
         writes=['mVa'], eng='scalar')
    for h in range(4):
        p.mm(k.bank(6)[:n, h * 65:(h + 1) * 65], k.mAT[:n, h, :n], k.mVa[:n, h, :], reads=['mAT', 'mVa'], writes=['b6'])
        p.mm(k.bank(7)[:n, h * 65:(h + 1) * 65], k.mqT[:, h, :n], W['mlCTb'][:, h // 2, :], reads=['mqT', nm('mlCT')],
             writes=['b7'])
    nd = sc[5][:n, 0:260].rearrange("p (h e) -> p h e", e=65)
    p.cp(sc[5][:n, 0:260], k.bank(6)[:n, 0:260], reads=['b6'], writes=['sc5'], eng='scalar')
    p.tt(sc[6][:n, 0:260].rearrange("p (h e) -> p h e", e=65), k.bank(7)[:n, 0:260].rearrange("p (h e) -> p h e", e=65),
         m2[:n, 0:4].unsqueeze(2).to_broadcast([n, 4, 65]), ALU.mult, reads=['b7', 'msm2'], writes=['sc6'])
    p.tt(sc[5][:n, 0:260], sc[5][:n, 0:260], sc[6][:n, 0:260], ALU.add, reads=['sc5', 'sc6'], writes=['sc5'])
    p.ts(m2[:n, 4:8], nd[:, :, 64], -1.0, None, ALU.mult, reads=['sc5', 'msm2'], writes=['msm2'])
    p.tt(m2[:n, 8:12], nd[:, :, 64], m2[:n, 4:8], ALU.max, reads=['sc5', 'msm2'], writes=['msm2'])
    p.tt(m2[:n, 8:12], m2[:n, 8:12], m2[:n, 12:16], ALU.max, **rw)
    p.add('vector', lambda e: e.reciprocal(m2[:n, 16:20], m2[:n, 8:12]), **rw)
    p.tt(sc[0][:n, 256:512].rearrange("p (h d) -> p h d", d=64), nd[:, :, 0:64],
         m2[:n, 16:20].unsqueeze(2).to_broadcast([n, 4, 64]), ALU.mult, reads=['sc5', 'msm2'], writes=['sc0'])
    p.mm(k.bank(2)[:, 0:8], k.sellast[:n, :], m[:n, 24:32], reads=['sellast', 'msm'], writes=['b2'])
    p.cp(m3[:, 0:8], k.bank(2)[:, 0:8], **rw)
    p.tt(m3[:, 12:16], m3[:, 4:8], m3[:, 0:4], ALU.subtract, **rw)
    p.tt(m3[:, 8:12], m3[:, 12:16], W['mprev'][:, :], ALU.add, **rw)
    p.act(m3[:, 8:12], m3[:, 8:12], AF.Exp, **rw)
    p.tt(m2[:n, 24:28], m[:n, 12:16], m3[:n, 12:16], ALU.add, **rw)
    p.act(m2[:n, 24:28], m2[:n, 24:28], AF.Exp, **rw)
    p.cp(W['mprev'][:, :], m3[:, 0:4], reads=['msm3'], writes=[nm('mprev')])
    p.stt(k.mKh[:n, 0:256].rearrange("p (h d) -> p h d", d=64),
          Pt[:n, OFF['mk']:OFF['mk'] + 256].rearrange("p (h d) -> p h d", d=64), 0.125,
          m2[:n, 24:28].unsqueeze(2).to_broadcast([n, 4, 64]), ALU.mult, ALU.mult, reads=[pk, 'msm2'], writes=['mKh'])
    CT = W['mlCT']
    for pr in range(2):
        p.mm(k.bank(2 + pr)[:, 0:260], k.mKh[:n, pr * 128:(pr + 1) * 128], k.mVa[:n].rearrange("p h e -> p (h e)"),
             reads=['mKh', 'mVa'], writes=['b%d' % (2 + pr)])
        for hl in range(2):
            r0 = hl * 64
            h = 2 * pr + hl
            p.stt(CT[r0:r0 + 64, pr, :], CT[r0:r0 + 64, pr, :], m3[r0:r0 + 64, 8 + h:9 + h],
                  k.bank(2 + pr)[r0:r0 + 64, h * 65:(h + 1) * 65], ALU.mult, ALU.add,
                  reads=['b%d' % (2 + pr), 'msm3', nm('mlCT')], writes=[nm('mlCT')])
    p.cp(W['mlCTb'][:], CT[:], reads=[nm('mlCT')], writes=[nm('mlCT')])
    if g['last'] and tidx == T // 128 - 1:
        for pr in range(2):
            p.tr(k.bank(4)[0:64, pr * 128:(pr + 1) * 128], CT[:, pr, 0:64], k.ident[:, :], reads=[nm('mlCT'), 'ident'],
                 writes=['b4'])
        p.cp(sc[1][0:64, 0:256], k.bank(4)[0:64, 0:256], reads=['b4'], writes=['sc1'])
        p.dma(k.O['p_ml_C'][l].rearrange("(h v d) -> v h d", h=4, v=64), sc[1][0:64, 0:256].rearrange("p (h d) -> p h d", d=64),
              reads=['sc1'], is_out=True)
        p.dma(k.O['p_ml_n'][l].rearrange("(pr q) -> q pr", q=128), CT[:, :, 64], reads=[nm('mlCT')], is_out=True,
              allow_slow_non_contiguous=True)
        p.dma(k.O['p_ml_m'][l:l + 1, :], W['mprev'][0:1, :], reads=[nm('mprev')], is_out=True)
    p.act(sc[7][:n, 512:768], Pt[:n, OFF['mo']:OFF['mo'] + 256], AF.Sigmoid, reads=[pk], writes=['sc7'])
    head_post(k, sc[0][:n, 256:512], ['sc0'], n, W['bc'][:n, BO['ml_norm']:BO['ml_norm'] + 256], 'bc',
              sc[7][:n, 512:768], ['sc7'], k.Y[ti][:n, 256:512], 'Y%d' % ti)


def ml_sample(k, l, ti, n):
    p, W = k.p, k.W[l]
    nm = lambda s: '%s_l%d' % (s, l)
    Pt, pk, sc = k.P[ti], pkeys(ti, 256, 1288), k.sc
    m, m2, m3 = k.msm, k.msm2, k.msm3
    ml_gates(k, l, ti, n)
    rw = dict(reads=['msm', 'msm2', 'msm3'], writes=['msm', 'msm2', 'msm3'])
    p.dma(m3[:n, 0:4], k.I['st_ml_m'][l], writes=['msm3'])
    p.tt(m[:n, 8:12], m3[:n, 0:4], m[:n, 4:8], ALU.subtract, **rw)
    p.tt(m[:n, 12:16], m[:n, 8:12], m[:n, 0:4], ALU.max, **rw)
    p.dma(k.O['s_ml_m'][l], m[:n, 12:16], reads=['msm'], is_out=True)
    pk4 = m2[:n, 0:16].rearrange("p (h w) -> p h w", w=4)
    p.tt(pk4[:, :, 0], m[:n, 8:12], m[:n, 12:16], ALU.subtract, **rw)
    p.tt(pk4[:, :, 1], m[:n, 0:4], m[:n, 12:16], ALU.subtract, **rw)
    p.ts(pk4[:, :, 2], m[:n, 12:16], -1.0, None, ALU.mult, **rw)
    p.act(m2[:n, 0:16], m2[:n, 0:16], AF.Exp, **rw)
    rep = k.rep
    p.dma(k.scr['s16'], m2[:n, 0:16], reads=['msm2'], writes=['scr_s16'])
    for vh in range(2):
        p.dma(rep[vh:128:2, 5, 0:4], k.scr['s16'].rearrange("i (h w) -> (i h) w", w=4), reads=['scr_s16'], writes=['rep'])
        p.dma(rep[vh:128:2, 2, :], k.I['st_ml_n'][l].rearrange("i (h d) -> (i h) d", d=64), writes=['rep'])
    rep_gather(k, 'q', Pt[:n, OFF['mq']:OFF['mq'] + 256], [pk], rep[:, 0, :])
    rep_gather(k, 'k', Pt[:n, OFF['mk']:OFF['mk'] + 256], [pk], rep[:, 1, :])
    rep_gather(k, 'v', Pt[:n, OFF['mv']:OFF['mv'] + 256], [pk], rep[:, 3, 0:32], width=32)
    C = k.repS
    C3 = C[:].rearrange("p (v d) -> p v d", d=64)
    p.dma(C[:], k.I['st_ml_C'][l].rearrange("i (q f) -> (i q) f", f=2048), writes=['repS'])
    A = k.scd[0][:].rearrange("p (v d) -> p v d", d=64)
    bq = lambda ap: ap.unsqueeze(1).to_broadcast([128, 32, 64])
    bvv = lambda ap: ap.unsqueeze(2).to_broadcast([128, 32, 64])
    w0c, wc, enc = rep[:, 5, 0:1], rep[:, 5, 1:2], rep[:, 5, 2:3]
    rr = dict(reads=['rep', 'repS', 'sc0', 'sc1', 'sc2', 'sc3'], writes=['rep', 'sc0', 'sc1', 'sc2', 'sc3'])
    t6 = rep[:, 6, :]
    sA = rep[:, 7, :]
    p.tt(t6, rep[:, 0, :], rep[:, 1, :], ALU.mult, **rr)
    p.red(sA[:, 0:1], t6, ALU.add, **rr)
    p.tt(t6, rep[:, 0, :], rep[:, 2, :], ALU.mult, **rr)
    p.red(sA[:, 1:2], t6, ALU.add, **rr)
    p.stt(sA[:, 2:3], sA[:, 0:1], 0.125, wc, ALU.mult, ALU.mult, **rr)
    p.stt(sA[:, 3:4], sA[:, 1:2], w0c, sA[:, 2:3], ALU.mult, ALU.add, **rr)
    p.ts(sA[:, 4:5], sA[:, 3:4], -1.0, None, ALU.mult, **rr)
    p.tt(sA[:, 5:6], sA[:, 3:4], sA[:, 4:5], ALU.max, **rr)
    p.tt(sA[:, 5:6], sA[:, 5:6], enc, ALU.max, **rr)
    p.add('vector', lambda e: e.reciprocal(sA[:, 5:6], sA[:, 5:6]), **rr)
    p.tt(A, C3, bq(rep[:, 0, :]), ALU.mult, **rr)
    p.red(rep[:, 4, 0:32], A, ALU.add, **rr)
    p.ts(rep[:, 4, 0:32], rep[:, 4, 0:32], w0c, None, ALU.mult, **rr)
    p.stt(rep[:, 4, 0:32], rep[:, 3, 0:32], sA[:, 2:3], rep[:, 4, 0:32], ALU.mult, ALU.add, **rr)
    p.ts(rep[:, 4, 0:32], rep[:, 4, 0:32], sA[:, 5:6], None, ALU.mult, **rr)
    rep_scatter(k, rep[:, 4, 0:32], sc[6][:n, 256:512], 'sc6')
    p.ts(t6, rep[:, 1, :], wc, 0.125, ALU.mult, ALU.mult, **rr)
    p.tt(A, bvv(rep[:, 3, 0:32]), bq(t6), ALU.mult, **rr)
    p.stt(C3, C3, w0c, A, ALU.mult, ALU.add, reads=['rep', 'repS', 'sc0', 'sc1'], writes=['repS'])
    p.dma(k.O['s_ml_C'][l].rearrange("i (q f) -> (i q) f", f=2048), C[:], reads=['repS'], is_out=True)
    p.stt(rep[:, 2, :], rep[:, 2, :], w0c, t6, ALU.mult, ALU.add, **rr)
    p.dma(k.O['s_ml_n'][l].rearrange("i (h d) -> (i h) d", d=64), rep[0:128:2, 2, :], reads=['rep'], is_out=True)
    p.act(sc[7][:n, 512:768], Pt[:n, OFF['mo']:OFF['mo'] + 256], AF.Sigmoid, reads=[pk], writes=['sc7'])
    head_post(k, sc[6][:n, 256:512], ['sc6'], n, W['bc'][:n, BO['ml_norm']:BO['ml_norm'] + 256], 'bc',
              sc[7][:n, 512:768], ['sc7'], k.Y[ti][:n, 256:512], 'Y%d' % ti)


RWS = 0.6065306597126334


def rw_pre(k, l, g, ti, n, prompt):
    p, W = k.p, k.W[l]
    nm = lambda s: '%s_l%d' % (s, l)
    Pt, pk, sc = k.P[ti], pkeys(ti, 2328, 3224), k.sc
    bcs = W['bc']
    rc0 = OFF['rc']
    B = lambda name, w=256: bcs[:n, BO[name]:BO[name] + w]
    if prompt:
        for half in range(2):
            c0 = half * 448
            p.mm(k.bank(2 + half)[:n, 0:448], k.shm[:n, :n], Pt[:n, rc0 + c0:rc0 + c0 + 448], start=True, stop=False,
                 reads=[pk, 'shm'], writes=['b%d' % (2 + half)])
            p.mm(k.bank(2 + half)[:n, 0:448], k.shprev[:, :n], W['rcprev'][:, c0:c0 + 448], start=False, stop=True,
                 reads=[nm('rcprev'), 'shprev'], writes=['b%d' % (2 + half)])
            p.tt(sc[0][:n, c0:c0 + 448], k.bank(2 + half)[:n, 0:448], Pt[:n, rc0 + c0:rc0 + c0 + 448], ALU.subtract,
                 reads=['b%d' % (2 + half), pk], writes=['sc0'])
        p.cp(W['rcprev'][:, :], Pt[:, rc0:rc0 + 896], reads=[pk], writes=[nm('rcprev')], eng='gpsimd')
    else:
        p.dma(sc[0][:n, 0:896], k.I['st_rw_shift'][l], writes=['sc0'])
        p.tt(sc[0][:n, 0:896], sc[0][:n, 0:896], Pt[:n, rc0:rc0 + 896], ALU.subtract, reads=['sc0', pk], writes=['sc0'])
    p.tt(sc[0][:n, 0:896], sc[0][:n, 0:896], B('rw_mu', 896), ALU.mult, reads=['sc0', 'bc'], writes=['sc0'])
    p.tt(sc[0][:n, 0:896], sc[0][:n, 0:896], Pt[:n, rc0:rc0 + 896], ALU.add, reads=['sc0', pk], writes=['sc0'])
    rr, rk, rv = sc[0][:n, 0:256], sc[0][:n, 256:512], sc[0][:n, 512:768]
    p.act(sc[1][:n, 0:32], sc[0][:n, 768:800], AF.Tanh, reads=['sc0'], writes=['sc1'])
    p.cp(sc[1][:n, 32:64], sc[0][:n, 800:832], reads=['sc0'], writes=['sc1'])
    p.act(sc[1][:n, 64:128], sc[0][:n, 832:896], AF.Sigmoid, reads=['sc0'], writes=['sc1'])
    p.tr(k.bank(4)[:, 0:n], sc[1][:n, 0:128], k.ident[:n, :n], reads=['sc1', 'ident'], writes=['b4'])
    p.cp(sc[1][:, 128:128 + n], k.bank(4)[:, 0:n], reads=['b4'], writes=['sc1'], eng='scalar')
    T3T = sc[1][:, 128:128 + n]
    p.mm(k.bank(2)[:n, 0:256], T3T, W['rwWw'][:, :], reads=['sc1', nm('rwWw')], writes=['b2'])
    p.mm(k.bank(2)[:n, 256:512], T3T, W['rwWa'][:, :], reads=['sc1', nm('rwWa')], writes=['b2'])
    p.mm(k.bank(3)[:n, 0:256], T3T, W['rwWg'][:, :], reads=['sc1', nm('rwWg')], writes=['b3'])
    r2 = dict(reads=['b2', 'b3', 'sc0', 'sc2', 'sc3', 'sc4', 'sm', 'bc'], writes=['sc2', 'sc3', 'sc4', 'sm'])
    sgw, a_, g_ = sc[2][:n, 0:256], sc[2][:n, 256:512], sc[2][:n, 512:768]
    p.tt(sgw, k.bank(2)[:n, 0:256], B('rw_w0'), ALU.add, **r2)
    p.act(sgw, sgw, AF.Sigmoid, **r2)
    p.tt(a_, k.bank(2)[:n, 256:512], B('rw_a0'), ALU.add, **r2)
    p.act(a_, a_, AF.Sigmoid, **r2)
    p.cp(g_, k.bank(3)[:n, 0:256], **r2)
    kk, kt, be, tmp = sc[3][:n, 0:256], sc[3][:n, 256:512], sc[3][:n, 512:768], sc[3][:n, 768:1024]
    sm = k.sm
    h3 = lambda ap: ap.rearrange("p (h d) -> p h d", d=64)
    p.tt(kk, rk, B('rw_k_k'), ALU.mult, **r2)
    p.tt(tmp, kk, kk, ALU.mult, **r2)
    p.red(sm[:n, 16:20], h3(tmp), ALU.add, **r2)
    p.ts(sm[:n, 16:20], sm[:n, 16:20], 1e-24, None, ALU.max, **r2)
    p.tt(sm[:n, 20:24], sm[:n, 16:20], k.mhalf[:n, 0:4], ALU.pow, reads=r2['reads'] + ['mhalf'], writes=r2['writes'], eng='gpsimd')
    p.tt(h3(kk), h3(kk), sm[:n, 20:24].unsqueeze(2).to_broadcast([n, 4, 64]), ALU.mult, **r2)
    p.stt(tmp, a_, -1.0, B('rw_k_a'), ALU.add, ALU.mult, **r2)
    p.stt(kt, tmp, 1.0, rk, ALU.add, ALU.mult, **r2)
    p.tt(be, a_, kk, ALU.mult, **r2)
    p.tt(tmp, rr, kt, ALU.mult, **r2)
    p.tt(tmp, tmp, B('rw_r_k'), ALU.mult, **r2)
    p.red(sm[:n, 24:28], h3(tmp), ALU.add, **r2)
    p.tt(h3(sc[4][:n, 0:256]), h3(rv), sm[:n, 24:28].unsqueeze(2).to_broadcast([n, 4, 64]), ALU.mult, **r2)
    p.cp(sc[4][:n, 256:512], rv, **r2)


def rw_post(k, l, ti, n, o_ap, o_reads, g_ap, g_reads, bv_ap, bv_reads):
    W = k.W[l]
    nm = lambda s: '%s_l%d' % (s, l)
    head_post(k, o_ap, o_reads, n, W['bc'][:n, BO['rw_norm']:BO['rw_norm'] + 256], 'bc', g_ap, g_reads,
              k.Y[ti][:n, 768:1024], 'Y%d' % ti, layernorm=True, extra=bv_ap, extra_reads=bv_reads)


def rw_prompt(k, l, g, ti, n, tidx):
    p, W = k.p, k.W[l]
    nm = lambda s: '%s_l%d' % (s, l)
    Pt, pk, sc = k.P[ti], pkeys(ti, 2328, 3224), k.sc
    rw_pre(k, l, g, ti, n, True)
    if STOP[0] == 21:
        return
    rr = sc[0][:n, 0:256]
    sgw = sc[2][:n, 0:256]
    kk, kt, be = sc[3][:n, 0:256], sc[3][:n, 256:512], sc[3][:n, 512:768]
    V = sc[4][:n, 256:512]
    gs = k.gsm
    p.mm(k.bank(3)[:n, 256:512], k.triu[:n, :n], sgw, reads=['triu', 'sc2'], writes=['b3'])
    cws = k.bank(3)[:n, 256:512]
    eP, eN, ePm1 = sc[5][:n, 0:256], sc[5][:n, 256:512], sc[5][:n, 512:768]
    p.act(eP, cws, AF.Exp, scale=-RWS, reads=['b3'], writes=['sc5'])
    p.act(eN, cws, AF.Exp, scale=RWS, reads=['b3'], writes=['sc5'])
    p.tt(ePm1, cws, sgw, ALU.subtract, reads=['b3', 'sc2'], writes=['sc5'])
    p.act(ePm1, ePm1, AF.Exp, scale=-RWS, reads=['sc5'], writes=['sc5'])
    p.cp(sc[5][:n, 768:1024], cws, reads=['b3'], writes=['sc5'])
    for pr in range(2):
        p.mm(k.bank(4)[:, 256 + 8 * pr:264 + 8 * pr], sc[5][:n, 768 + pr * 128:768 + (pr + 1) * 128], k.sellast[:n, 0:8],
             reads=['sc5', 'sellast'], writes=['b4'])
    p.act(gs[:, 4:6], k.bank(4)[:, 256:272:8], AF.Exp, scale=-RWS, reads=['b4'], writes=['gsm'])
    r6 = dict(reads=['sc0', 'sc3', 'sc5', 'sc6'], writes=['sc6'])
    p.tt(sc[6][:n, 0:256], kk, ePm1, ALU.mult, **r6)
    p.tt(sc[6][:n, 256:512], be, eN, ALU.mult, **r6)
    p.tt(sc[6][:n, 512:768], kt, eN, ALU.mult, **r6)
    p.tt(sc[6][:n, 768:1024], rr, eP, ALU.mult, **r6)
    if STOP[0] == 22:
        return
    for X in range(4):
        for c in range(2):
            p.tr(k.pp[3][:, (X * 2 + c) * n:(X * 2 + c + 1) * n], sc[6][:n, X * 256 + c * 128:X * 256 + (c + 1) * 128],
                 k.ident[:n, :n], reads=['sc6', 'ident'], writes=['b6', 'b7'])
    if STOP[0] == 27:
        return
    kaTm = k.rwT[:, 0:4, :]
    rTm = k.rwT[:, 4:8, :]
    for h in range(4):
        pr, hl = h // 2, h % 2
        p.ts(kaTm[:, h, :n], k.pp[3][:, pr * n:(pr + 1) * n], k.hmask[:, hl:hl + 1], None, ALU.mult,
             reads=['b6', 'b7', 'hmask'], writes=['rwT'])
        p.ts(rTm[:, h, :n], k.pp[3][:, (6 + pr) * n:(7 + pr) * n], k.hmask[:, hl:hl + 1], None, ALU.mult,
             reads=['b6', 'b7', 'hmask'], writes=['rwT'])
    if STOP[0] == 28:
        return
    beT = k.rwT[:, 8:10, :]
    ktT = k.rwT[:, 10:12, :]
    p.cp(beT, k.pp[3][:, 2 * n:4 * n].rearrange("p (c i) -> p c i", i=128), reads=['b6', 'b7'], writes=['rwT'], eng='scalar')
    p.cp(ktT, k.pp[3][:, 4 * n:6 * n].rearrange("p (c i) -> p c i", i=128), reads=['b6', 'b7'], writes=['rwT'], eng='scalar')
    p.ts(k.rwb2[:n, 3, :], sc[6][:n, 256:512], -1.0, None, ALU.mult, reads=['sc6'], writes=['rwTok'])
    p.cp(k.rwb2[:n, 4, :], sc[6][:n, 512:768], reads=['sc6'], writes=['rwTok'], eng='gpsimd')
    if STOP[0] == 23:
        return
    H4 = lambda ap: ap.rearrange("p (h i) -> p h i", i=128)
    mb = lambda m_: m_[:n, :n].unsqueeze(1).to_broadcast([n, 4, n])
    rb = k.rwb
    LA, LAT, MT, QKT, nQBT, TT, LB, LBT = (rb[:n, i, :] for i in range(8))
    Vb, RHSb, Ub = k.rwb2[:n, 0, :], k.rwb2[:n, 1, :], k.rwb2[:n, 2, :]
    for h in range(4):
        pr = h // 2
        p.mm(k.bank(2)[:n, h * n:(h + 1) * n], kaTm[:, h, :n], beT[:, pr, :n], reads=['rwT'], writes=['b2'])
        p.mm(k.bank(3)[:n, h * n:(h + 1) * n], beT[:, pr, :n], kaTm[:, h, :n], reads=['rwT'], writes=['b3'])
        p.mm(k.bank(4)[:n, h * n:(h + 1) * n], ktT[:, pr, :n], kaTm[:, h, :n], reads=['rwT'], writes=['b4'])
        p.mm(k.bank(5)[:n, h * n:(h + 1) * n], ktT[:, pr, :n], rTm[:, h, :n], reads=['rwT'], writes=['b5'])
    p.stt(H4(LA), H4(k.bank(2)[:n, 0:512]), -1.0, mb(k.masksl), ALU.mult, ALU.mult, reads=['b2', 'masksl'], writes=['rwLA'])
    p.stt(H4(LAT), H4(k.bank(3)[:n, 0:512]), -1.0, mb(k.masksu), ALU.mult, ALU.mult, reads=['b3', 'masksu'], writes=['rwLAT'])
    p.tt(H4(MT), H4(k.bank(4)[:n, 0:512]), mb(k.masksu), ALU.mult, reads=['b4', 'masksu'], writes=['rwMT'])
    p.tt(H4(QKT), H4(k.bank(5)[:n, 0:512]), mb(k.triu), ALU.mult, reads=['b5', 'triu'], writes=['rwQKT'])
    for h in range(4):
        pr = h // 2
        p.mm(k.bank(2)[:n, h * n:(h + 1) * n], beT[:, pr, :n], rTm[:, h, :n], reads=['rwT'], writes=['b2'])
    p.stt(H4(nQBT), H4(k.bank(2)[:n, 0:512]), -1.0, mb(k.triu), ALU.mult, ALU.mult, reads=['b2', 'triu'], writes=['rwQBT'])
    if STOP[0] == 24:
        return
    p.tt(H4(TT), H4(LAT), mb(k.ident), ALU.add, reads=['rwLAT', 'ident'], writes=['rwTT'])
    p.cp(Vb, V, reads=['sc4'], writes=['rwVb'], eng='gpsimd')
    cur = (LA, LAT, 'rwLA', 'rwLAT')
    nxt = (LB, LBT, 'rwLB', 'rwLBT')
    for lvl in range(1, 7):
        cL, cLT, ckL, ckLT = cur
        nL, nLT, nkL, nkLT = nxt
        last = lvl == 6
        for h in range(4):
            sl = slice(h * n, (h + 1) * n)
            p.mm(k.bank(2)[:n, sl], cLT[:, sl], cL[:, sl], reads=[ckL, ckLT], writes=['b2'])
            if not last:
                p.mm(k.bank(3)[:n, sl], cL[:, sl], cLT[:, sl], reads=[ckL, ckLT], writes=['b3'])
        p.cp(nL, k.bank(2)[:n, 0:512], reads=['b2', nkL], writes=[nkL])
        if not last:
            p.cp(nLT, k.bank(3)[:n, 0:512], reads=['b3', nkLT], writes=[nkLT], eng='scalar')
        for h in range(4):
            sl = slice(h * n, (h + 1) * n)
            p.mm(k.bank(4)[:n, sl], nL[:, sl], TT[:, sl], reads=[nkL, 'rwTT'], writes=['b4'])
        p.tt(TT, TT, k.bank(4)[:n, 0:512], ALU.add, reads=['b4', 'rwTT'], writes=['rwTT'])
        cur, nxt = nxt, cur
    if STOP[0] == 25:
        return
    A = W['rwA']
    RHS, U, Osb = sc[1][:n, 768:1024], sc[6][:n, 0:256], sc[6][:n, 768:1024]
    for h in range(4):
        pr = h // 2
        hs = slice(h * 64, (h + 1) * 64)
        p.mm(k.bank(5)[:n, hs], kaTm[:, h, :n], W['rwAb'][:, pr, :], start=True, stop=False, reads=['rwT', nm('rwA')], writes=['b5'])
        p.mm(k.bank(5)[:n, hs], H4(MT)[:, h, :], Vb[:, hs], start=False, stop=True, reads=['rwMT', 'rwVb'], writes=['b5'])
    p.cp(RHSb, k.bank(5)[:n, 0:256], reads=['b5'], writes=['rwRHS'])
    for h in range(4):
        hs = slice(h * 64, (h + 1) * 64)
        p.mm(k.bank(6)[:n, hs], H4(TT)[:, h, :], RHSb[:, hs], reads=['rwTT', 'rwRHS'], writes=['b6'])
    p.cp(U, k.bank(6)[:n, 0:256], reads=['b6'], writes=['sc6'])
    p.cp(Ub, k.bank(6)[:n, 0:256], reads=['b6'], writes=['rwUb'], eng='scalar')
    for h in range(4):
        pr = h // 2
        hs = slice(h * 64, (h + 1) * 64)
        p.mm(k.bank(7)[:n, hs], rTm[:, h, :n], W['rwAb'][:, pr, :], start=True, stop=False, reads=['rwT', nm('rwA')], writes=['b7'])
        p.mm(k.bank(7)[:n, hs], H4(QKT)[:, h, :], Vb[:, hs], start=False, stop=False, reads=['rwQKT', 'rwVb'], writes=['b7'])
        p.mm(k.bank(7)[:n, hs], H4(nQBT)[:, h, :], Ub[:, hs], start=False, stop=True, reads=['rwQBT', 'rwUb'], writes=['b7'])
    p.cp(Osb, k.bank(7)[:n, 0:256], reads=['b7'], writes=['sc6'], eng='scalar')
    if STOP[0] == 26:
        return
    for pr in range(2):
        ps = slice(pr * 128, (pr + 1) * 128)
        p.mm(k.bank(5)[:, 256 + pr * 128:256 + (pr + 1) * 128], k.rwb2[:n, 4, ps], Vb[:, ps],
             start=True, stop=False, reads=['rwTok', 'rwVb'], writes=['b5'])
        p.mm(k.bank(5)[:, 256 + pr * 128:256 + (pr + 1) * 128], k.rwb2[:n, 3, ps], Ub[:, ps],
             start=False, stop=True, reads=['rwTok', 'rwUb'], writes=['b5'])
        for hl in range(2):
            r0 = hl * 64
            c0 = 256 + pr * 128 + hl * 64
            p.tt(A[r0:r0 + 64, pr, :], A[r0:r0 + 64, pr, :], k.bank(5)[r0:r0 + 64, c0:c0 + 64], ALU.add,
                 reads=['b5', nm('rwA')], writes=[nm('rwA')])
            p.ts(A[r0:r0 + 64, pr, :], A[r0:r0 + 64, pr, :], gs[r0:r0 + 64, 4 + pr:5 + pr], None, ALU.mult,
                 reads=['gsm', nm('rwA')], writes=[nm('rwA')])
    p.cp(W['rwAb'][:], A[:], reads=[nm('rwA')], writes=[nm('rwA')])
    if g['last'] and tidx == T // 128 - 1:
        for pr in range(2):
            p.tr(k.bank(4)[0:64, pr * 128:(pr + 1) * 128], A[:, pr, :], k.ident[:, :], reads=[nm('rwA'), 'ident'], writes=['b4'])
        p.cp(sc[1][0:64, 0:256], k.bank(4)[0:64, 0:256], reads=['b4'], writes=['sc1'])
        p.dma(k.O['p_rw_S'][l].rearrange("(h v d) -> v h d", h=4, v=64), sc[1][0:64, 0:256].rearrange("p (h d) -> p h d", d=64),
              reads=['sc1'], is_out=True)
    rw_post(k, l, ti, n, Osb, ['sc6'], sc[2][:n, 512:768], ['sc2'], sc[4][:n, 0:256], ['sc4'])


def rw_sample(k, l, g, ti, n):
    p, W = k.p, k.W[l]
    nm = lambda s: '%s_l%d' % (s, l)
    Pt, pk, sc = k.P[ti], pkeys(ti, 2328, 3224), k.sc
    rw_pre(k, l, g, ti, n, False)
    p.act(sc[2][:n, 0:256], sc[2][:n, 0:256], AF.Exp, scale=-RWS, reads=['sc2'], writes=['sc2'])
    p.cp(sc[6][:n, 0:256], sc[2][:n, 512:768], reads=['sc2'], writes=['sc6'])
    p.cp(sc[6][:n, 256:512], sc[4][:n, 0:256], reads=['sc4'], writes=['sc6'])
    rep = k.rep
    rep_gather(k, 'q', sc[0][:n, 0:256], ['sc0'], rep[:, 0, :])
    rep_gather(k, 'k', sc[3][:n, 256:512], ['sc3'], rep[:, 1, :])
    rep_gather(k, 'a', sc[3][:n, 0:256], ['sc3'], rep[:, 2, :])
    rep_gather(k, 'b', sc[3][:n, 512:768], ['sc3'], rep[:, 5, :])
    rep_gather(k, 'c', sc[2][:n, 0:256], ['sc2'], rep[:, 6, :])
    rep_gather(k, 'v', sc[0][:n, 512:768], ['sc0'], rep[:, 3, 0:32], width=32)
    S = k.repS
    S3 = S[:].rearrange("p (v d) -> p v d", d=64)
    p.dma(S[:], k.I['st_rw_S'][l].rearrange("i (q f) -> (i q) f", f=2048), writes=['repS'])
    A = k.scd[0][:].rearrange("p (v d) -> p v d", d=64)
    Bt = k.scd[1][:].rearrange("p (v d) -> p v d", d=64)
    bq = lambda ap: ap.unsqueeze(1).to_broadcast([128, 32, 64])
    bvv = lambda ap: ap.unsqueeze(2).to_broadcast([128, 32, 64])
    rr_ = dict(reads=['rep', 'repS', 'sc0', 'sc1', 'sc2', 'sc3'], writes=['rep', 'sc0', 'sc1', 'sc2', 'sc3'])
    p.tt(A, S3, bq(rep[:, 2, :]), ALU.mult, **rr_)
    p.red(rep[:, 4, 0:32], A, ALU.add, **rr_)
    p.tt(A, S3, bq(rep[:, 6, :]), ALU.mult, **rr_)
    p.tt(Bt, bvv(rep[:, 4, 0:32]), bq(rep[:, 5, :]), ALU.mult, **rr_)
    p.tt(A, A, Bt, ALU.subtract, **rr_)
    p.tt(Bt, bvv(rep[:, 3, 0:32]), bq(rep[:, 1, :]), ALU.mult, **rr_)
    p.tt(S3, A, Bt, ALU.add, reads=['sc0', 'sc1', 'sc2', 'sc3', 'repS'], writes=['repS'])
    p.dma(k.O['s_rw_S'][l].rearrange("i (q f) -> (i q) f", f=2048), S[:], reads=['repS'], is_out=True)
    p.tt(A, S3, bq(rep[:, 0, :]), ALU.mult, **rr_)
    p.red(rep[:, 4, 32:64], A, ALU.add, **rr_)
    rep_scatter(k, rep[:, 4, 32:64], sc[6][:n, 768:1024], 'sc6')
    rw_post(k, l, ti, n, sc[6][:n, 768:1024], ['sc6'], sc[6][:n, 0:256], ['sc6'], sc[6][:n, 256:512], ['sc6'])


def ptsel(k, l, src_dbg, name, ap, reads):
    if name in k.DBG:
        k.p.dma(k.DBG[name], ap, reads=reads, is_out=True)


def run_group(k, g):
    p, I, O = k.p, k.I, k.O
    prompt = g['kind'] == 'p'
    tiles = g['tiles']
    ntok = sum(n for _, n in tiles)
    for ti, (tidx, n) in enumerate(tiles):
        src = I['xp'][tidx * 128:tidx * 128 + n, :] if prompt else I['xs']
        p.dma(k.X[ti][:n, :], src, writes=['X%d' % ti])
    for l in range(DEPTH):
        W = k.W[l]
        nm = lambda s: '%s_l%d' % (s, l)
        p.dma(k.bcs[:], k.bcscr[l:l + 1, :].partition_broadcast(128), reads=['bcscr'], writes=['bc'])
        p.dma(k.tab[:].rearrange("p a j i -> p (a j i)"), k.tabscr[l], reads=['tabscr'], writes=['tab'])
        tok0 = 0
        for ti, (tidx, n) in enumerate(tiles):
            rmsnorm_T(k, k.X[ti], 'X%d' % ti, n, W['gmix'], nm('gmix'), tok0, k.hT, 'hT',
                      rstd_out=k.rstdm[:, ti:ti + 1], rstd_key='rstd%d' % ti)
            tok0 += n
        if len(MIXERS) < 3:
            for ti, (tidx, n) in enumerate(tiles):
                p.memset(k.Y[ti][:, 256:1024], 0.0, writes=['Y%d' % ti])

        s5g = [s5_tile(k, l, g, ti, n, tidx) for ti, (tidx, n) in enumerate(tiles)]
        if STOP[0] <= 1 and prompt:
            s5g = []
        sched = {0: [0], 3: [0], 4: [0, 1], 6: [1]} if len(s5g) == 2 else {0: [0], 3: [0], 4: [0]}
        for c0 in range(0, DIN, 512):
            w = min(512, DIN - c0)
            wt, wk = wblock(k, 'w_in', l, 0, D, c0, c0 + w)
            tok0 = 0
            for ti, (tidx, n) in enumerate(tiles):
                bi = ((c0 // 512) * len(tiles) + ti) % 2
                for kc in range(8):
                    p.mm(k.bank(bi)[:n, 0:w], k.hT[:, kc, tok0:tok0 + n], wt[:, kc, 0:w],
                         start=(kc == 0), stop=(kc == 7), reads=['hT', wk], writes=['b%d' % bi])
                p.act(k.P[ti][:n, c0:c0 + w], k.bank(bi)[:n, 0:w], AF.Identity, scale=k.rstdm[:n, ti:ti + 1],
                      reads=['b%d' % bi, 'rstd%d' % ti],
                      writes=['P%d_b%d' % (ti, c0 // 512)] + (['rep', 'repS'] if ti == 1 else []))
                tok0 += n
            for gi in sched.get(c0 // 512, []):
                if gi < len(s5g):
                    next(s5g[gi], None)
        for gen in s5g:
            for _ in gen:
                pass
        if STOP[0] <= 1 and prompt:
            return
        for ti, (tidx, n) in enumerate(tiles):
            yk = 'Y%d' % ti
            if 'ml' in MIXERS:
                if prompt:
                    ml_prompt(k, l, g, ti, n, tidx)
                else:
                    ml_sample(k, l, ti, n)
            if 'gla' in MIXERS:
                if prompt:
                    gla_prompt(k, l, g, ti, n, tidx)
                else:
                    gla_sample(k, l, ti, n)
            if 'rw' in MIXERS:
                if prompt:
                    rw_prompt(k, l, g, ti, n, tidx)
                else:
                    rw_sample(k, l, g, ti, n)
            if prompt and g['last'] and tidx == T // 128 - 1:
                p.dma(O['p_rw_shift'][l:l + 1, :], k.P[ti][n - 1:n, OFF['rc']:OFF['rc'] + 896], reads=pkeys(ti, 2328, 3224),
                      is_out=True)
            if not prompt:
                p.dma(O['s_rw_shift'][l], k.P[ti][:n, OFF['rc']:OFF['rc'] + 896], reads=pkeys(ti, 2328, 3224), is_out=True)
            if l == 0 and prompt and tidx == 0:
                ptsel(k, l, None, 'dbg_P', k.P[ti][:, :], pkeys(ti, 0, DIN))
                ptsel(k, l, None, 'dbg_Y', k.Y[ti][:, :], [yk])
        if (STOP[0] <= 2 or 10 < STOP[0] < 20) and prompt:
            return
        tok0 = 0
        for ti, (tidx, n) in enumerate(tiles):
            yt = k.Y[ti]
            for kc in range(8):
                p.tr(k.pp[0][:, kc * n:(kc + 1) * n], yt[:n, kc * 128:(kc + 1) * 128], k.ident[:n, :n],
                     reads=['Y%d' % ti, 'ident'], writes=['b0', 'b1'])
            p.cp(k.hT[:, :, tok0:tok0 + n], v3(k.pp[0], n), reads=['b0', 'b1'], writes=['hT'])
            tok0 += n
        for half in range(2):
            wt, wk = wblock(k, 'w_out', l, 0, D, half * 512, half * 512 + 512)
            tok0 = 0
            for ti, (tidx, n) in enumerate(tiles):
                bi = 2 * half + ti % 2
                for kc in range(8):
                    p.mm(k.bank(bi)[:n, :], k.hT[:, kc, tok0:tok0 + n], wt[:, kc, :],
                         start=(kc == 0), stop=(kc == 7), reads=['hT', wk], writes=['b%d' % bi])
                xs_ = k.X[ti][:n, half * 512:half * 512 + 512]
                p.tt(xs_, xs_, k.bank(bi)[:n, :], ALU.add, reads=['b%d' % bi, 'X%d' % ti], writes=['X%d' % ti])
                tok0 += n
        if STOP[0] <= 3 and prompt:
            return
        tok0 = 0
        for ti, (tidx, n) in enumerate(tiles):
            rmsnorm_T(k, k.X[ti], 'X%d' % ti, n, W['gffn'], nm('gffn'), tok0, k.hT, 'hT')
            tok0 += n
        N = ntok
        if not prompt:
            taps = []
            for j in range(2):
                tp = k.sc[1 + j]
                p.dma(k.sc[0][:N, 0:1024], I['st_ffn_conv'][l, :, j, 0:1024], writes=['sc0'])
                p.dma(k.sc[7][:N, 0:1024], I['st_ffn_conv'][l, :, j, 1024:2048], writes=['sc7'])
                p.dma(k.sc[3][:N, 0:768], I['st_ffn_conv'][l, :, j, 2048:2816], writes=['sc3'])
                for c in range(NFF):
                    srct, sk_ = ((k.sc[0], 'sc0'), (k.sc[7], 'sc7'), (k.sc[3], 'sc3'))[c // 8]
                    cc = c % 8
                    p.tr(k.bank(2 + j)[:, c * N:(c + 1) * N], srct[:N, cc * 128:(cc + 1) * 128],
                         k.ident[:N, :N], reads=[sk_, 'ident'], writes=['b%d' % (2 + j)])
                p.cp(tp[:, 0:NFF * N], k.bank(2 + j)[:, 0:NFF * N], reads=['b%d' % (2 + j)],
                     writes=['sc%d' % (1 + j)], eng='scalar')
                taps.append(tp)
            p.dma(O['s_ffn_conv'][l, :, 0, :], I['st_ffn_conv'][l, :, 1, :], is_out=True)
        ugb = k.ugb
        cw, cb = W['convw'], W['convb']
        for jb in range(0, DFF, 512):
            w = min(512, DFF - jb)
            nch = w // 128
            c0 = jb // 128
            wtg, wkg = wblock(k, 'ffn_w_up', l, 0, D, jb, jb + w)
            wtv, wkv = wblock(k, 'ffn_w_up', l, 0, D, DFF + jb, DFF + jb + w)
            par = (jb // 512) % 2
            ugP, uvP = k.pp[2 * par], k.pp[2 * par + 1]
            ugk = ['b%d' % (4 * par), 'b%d' % (4 * par + 1)]
            uvk = ['b%d' % (4 * par + 2), 'b%d' % (4 * par + 3)]
            for cl in range(nch):
                for kc in range(8):
                    p.mm(ugP[:, cl * N:(cl + 1) * N], wtg[:, kc, cl * 128:(cl + 1) * 128], k.hT[:, kc, 0:N],
                         start=(kc == 0), stop=(kc == 7), reads=['hT', wkg], writes=ugk)
            for cl in range(nch):
                for kc in range(8):
                    p.mm(uvP[:, cl * N:(cl + 1) * N], wtv[:, kc, cl * 128:(cl + 1) * 128], k.hT[:, kc, 0:N],
                         start=(kc == 0), stop=(kc == 7), reads=['hT', wkv], writes=uvk)
            V3 = lambda ap: ap[:, 0:nch * N].rearrange("p (c i) -> p c i", i=N)
            ug3, uv3 = V3(ugP), V3(uvP)
            if prompt and par == 1:
                fsi = (0, 1, 2, 5)
            else:
                fsi = (3, 4, 6, 7)
            c1, t1, t2, t3 = (V3(k.sc[i]) for i in fsi)
            kc1, kt1, kt2, kt3 = ('sc%d' % i for i in fsi)
            ugb = k.ugb2 if (prompt and par == 1) else k.ugb
            ugbk = 'ugb'
            wb_ = lambda j: cw[:, j, c0:c0 + nch].unsqueeze(2).to_broadcast([128, nch, N])
            bb_ = cb[:, c0:c0 + nch].unsqueeze(2).to_broadcast([128, nch, N])
            wk_ = [nm('convw'), nm('convb')]
            if prompt:
                p.cp(ugb[:, 0:nch, 0:2], W['carry'][:, c0:c0 + nch, :], reads=[nm('carry')], writes=[ugbk])
                p.cp(ugb[:, 0:nch, 2:2 + N], ug3, reads=ugk, writes=[ugbk], eng='scalar')
                p.cp(W['carry'][:, c0:c0 + nch, :], ugb[:, 0:nch, N:N + 2], reads=[ugbk], writes=[nm('carry')])
                u0, u1, u2 = ugb[:, 0:nch, 0:N], ugb[:, 0:nch, 1:1 + N], ugb[:, 0:nch, 2:2 + N]
                ukeys = [ugbk]
            else:
                p.cp(ugb[:, 0:nch, 2:2 + N], ug3, reads=ugk, writes=[ugbk], eng='scalar')
                u0 = taps[0][:, c0 * N:(c0 + nch) * N].rearrange("p (c i) -> p c i", i=N)
                u1 = taps[1][:, c0 * N:(c0 + nch) * N].rearrange("p (c i) -> p c i", i=N)
                u2 = ugb[:, 0:nch, 2:2 + N]
                ukeys = [ugbk, 'sc1', 'sc2']
                p.cp(k.sc[5][:, c0 * N:(c0 + nch) * N].rearrange("p (c i) -> p c i", i=N), u2, reads=[ugbk], writes=['sc5'])
            for cl in range(nch):
                c = c0 + cl
                a_ = c1[:, cl, :]
                p.ts(a_, ug3[:, cl, :], cw[:, 2, c:c + 1], cb[:, c:c + 1], ALU.mult, ALU.add,
                     reads=ugk + wk_, writes=[kc1])
                p.stt(a_, u1[:, cl, :], cw[:, 1, c:c + 1], a_, ALU.mult, ALU.add, reads=ukeys + wk_ + [kc1], writes=[kc1])
                p.stt(a_, u0[:, cl, :], cw[:, 0, c:c + 1], a_, ALU.mult, ALU.add, reads=ukeys + wk_ + [kc1], writes=[kc1])
            p.act(t3, c1, AF.Gelu_apprx_tanh, reads=[kc1], writes=[kt3])
            p.tt(k.actT[:, c0:c0 + nch, 0:N], t3, uv3, ALU.mult, reads=[kt3] + uvk, writes=['actT'])
        if prompt and g['last']:
            for j in range(2):
                p.dma(O['p_ffn_conv'][l, j].rearrange("(c q) -> q c", q=128), W['carry'][:, :, j],
                      reads=[nm('carry')], is_out=True, allow_slow_non_contiguous=True)
        if not prompt:
            for c in range(NFF):
                p.tr(k.pp[1 + c // 8][:N, (c % 8) * 128:(c % 8 + 1) * 128], k.sc[5][:, c * N:(c + 1) * N],
                     k.ident[:, :], reads=['sc5', 'ident'], writes=['b%d' % (2 + 2 * (c // 8)), 'b%d' % (3 + 2 * (c // 8))])
            for q in range(3):
                wq = 1024 if q < 2 else 768
                p.cp(k.sc[0][:N, 0:wq], k.pp[1 + q][:N, 0:wq], reads=['b%d' % (2 + 2 * q), 'b%d' % (3 + 2 * q)],
                     writes=['sc0'], eng='scalar')
                p.dma(O['s_ffn_conv'][l, :, 1, q * 1024:q * 1024 + wq], k.sc[0][:N, 0:wq], reads=['sc0'], is_out=True)
        if STOP[0] <= 4 and prompt:
            return
        kgs = [(0, 8), (8, 8), (16, 6)]
        for half in range(2):
            for gi, (k0, kc_n) in enumerate(kgs):
                wt, wk = wblock(k, 'ffn_w_down', l, k0 * 128, (k0 + kc_n) * 128, half * 512, half * 512 + 512)
                tok0 = 0
                for ti, (tidx, n) in enumerate(tiles):
                    bi = 2 * half + ti % 2
                    for kc in range(kc_n):
                        p.mm(k.bank(bi)[:n, :], k.actT[:, k0 + kc, tok0:tok0 + n], wt[:, kc, :],
                             start=(gi == 0 and kc == 0), stop=(gi == 2 and kc == kc_n - 1),
                             reads=['actT', wk], writes=['b%d' % bi])
                    if gi == 2:
                        xs_ = k.X[ti][:n, half * 512:half * 512 + 512]
                        p.tt(xs_, xs_, k.bank(bi)[:n, :], ALU.add, reads=['b%d' % bi, 'X%d' % ti],
                             writes=['X%d' % ti])
                    tok0 += n
    p.dma(k.sc[6][:, :], I['norm_final'].partition_broadcast(128), writes=['sc6'])
    for ti, (tidx, n) in enumerate(tiles):
        sm = k.sm
        xt, xkey = k.X[ti], 'X%d' % ti
        p.act(k.sc[7][:n, :], xt[:n, :], AF.Square, reads=[xkey], writes=['sc7', 'sm'], accum_out=sm[:n, 0:1])
        p.ts(sm[:n, 1:2], sm[:n, 0:1], 1.0 / D, EPS, ALU.mult, ALU.add, reads=['sm'], writes=['sm'])
        p.tt(sm[:n, 2:3], sm[:n, 1:2], k.mhalf[:n, 0:1], ALU.pow, reads=['sm', 'mhalf'], writes=['sm'], eng='gpsimd')
        p.stt(k.sc[7][:n, :], xt[:n, :], sm[:n, 2:3], k.sc[6][:n, :], ALU.mult, ALU.mult,
              reads=[xkey, 'sm', 'sc6'], writes=['sc7'])
        dst = O['y_p'][tidx * 128:tidx * 128 + n, :] if prompt else O['y_s']
        p.dma(dst, k.sc[7][:n, :], reads=['sc7'], is_out=True)


_CACHE = {}


def make_in_maps(inputs):
    f = lambda a: np.ascontiguousarray(np.asarray(a, dtype=np.float32))
    maps = []
    for c in range(NCORES):
        r = slice(NS * c, NS * (c + 1))
        m = {}
        m['xp'] = f(inputs['x_prompt'][c])
        m['xs'] = f(inputs['x_sample'][r, 0, :])
        m['st_s5_re'] = f(inputs['state_s5_re'][:, r].reshape(DEPTH, NS, 1024))
        m['st_s5_im'] = f(inputs['state_s5_im'][:, r].reshape(DEPTH, NS, 1024))
        m['st_ml_C'] = f(inputs['state_mlstm_C'][:, r].reshape(DEPTH, NS, 16384))
        m['st_ml_n'] = f(inputs['state_mlstm_n'][:, r].reshape(DEPTH, NS, 256))
        m['st_ml_m'] = f(inputs['state_mlstm_m'][:, r])
        m['st_gla_S'] = f(inputs['state_gla_S'][:, r].reshape(DEPTH, NS, 16384))
        m['st_rw_S'] = f(inputs['state_rwkv_S'][:, r].reshape(DEPTH, NS, 16384))
        m['st_rw_shift'] = f(inputs['state_rwkv_shift'][:, r])
        m['st_ffn_conv'] = f(inputs['state_ffn_conv'][:, r])
        for name, shp in IN_SPECS[11:]:
            m[name] = f(np.asarray(inputs[name]).reshape(shp))
        maps.append(m)
    return maps


def assemble(results):
    g = lambda name: [np.asarray(r[name]) for r in results]
    B = NCORES
    out = []
    out.append(np.stack(g('y_p'), 0))
    out.append(np.concatenate(g('y_s'), 0).reshape(B * NS, 1, D))
    pst = lambda name, shp: np.stack(g(name), 1).reshape((DEPTH, B) + shp)
    out.append(pst('p_s5_re', (16, 64)))
    out.append(pst('p_s5_im', (16, 64)))
    out.append(pst('p_ml_C', (4, 64, 64)))
    out.append(pst('p_ml_n', (4, 64)))
    out.append(pst('p_ml_m', (4,)))
    out.append(pst('p_gla_S', (4, 64, 64)))
    out.append(pst('p_rw_S', (4, 64, 64)))
    out.append(pst('p_rw_shift', (896,)))
    out.append(pst('p_ffn_conv', (2, DFF)))
    sst = lambda name, shp: np.concatenate(g(name), 1).reshape((DEPTH, B * NS) + shp)
    out.append(sst('s_s5_re', (16, 64)))
    out.append(sst('s_s5_im', (16, 64)))
    out.append(sst('s_ml_C', (4, 64, 64)))
    out.append(sst('s_ml_n', (4, 64)))
    out.append(sst('s_ml_m', (4,)))
    out.append(sst('s_gla_S', (4, 64, 64)))
    out.append(sst('s_rw_S', (4, 64, 64)))
    out.append(sst('s_rw_shift', (896,)))
    out.append(sst('s_ffn_conv', (2, DFF)))
    return tuple(np.ascontiguousarray(o, dtype=np.float32) for o in out)


def kernel(**inputs):
    if 'nc' not in _CACHE:
        _CACHE['nc'] = build()
    nc = _CACHE['nc']
    maps = make_in_maps(inputs)
    res = run_bass_kernel_spmd(nc, maps, core_ids=list(range(NCORES)))
    return assemble(res.results)
```

```python
import contextlib
import math
import numpy as np
import concourse.bass as bass
import concourse.mybir as mybir
from concourse.alu_op_type import AluOpType as ALU
from concourse.bass_utils import run_bass_kernel_spmd

AF = mybir.ActivationFunctionType
AX = mybir.AxisListType
F32 = mybir.dt.float32
BF16 = mybir.dt.bfloat16
I32 = mybir.dt.int32

ENGS = ['tensor', 'vector', 'scalar', 'gpsimd', 'sync']

NCORES = 8
D = 1024
DIN = 3224
DFF = 2816
NFF = 22
T = 2048
NS = 16
DEPTH = 2
EPS = 1e-6
GN_EPS = 64e-5
OFF = dict(u=0, mq=256, mk=512, mv=768, mi=1024, mf=1028, mo=1032, gq=1288, gk=1544, gv=1800,
           ga=2056, gg=2072, rc=2328)
BO = dict(s5_d=0, ml_norm=256, gla_norm=512, rw_norm=768, rw_w0=1024, rw_a0=1280, rw_k_k=1536,
          rw_k_a=1792, rw_r_k=2048, rw_mu=2304, ml_gate_bias=3200)
BCW = 3208
GELU_C = 1.5957691216057308
NT = 2
STOP = [99]
SAME_ENGINE_SYNC = True
MIXERS = ['gla', 'ml', 'rw']


class Prog:
    def __init__(self, nc, n_dma_sems=32):
        self.nc = nc
        self.stack = contextlib.ExitStack()
        self.ops = {e: [] for e in ENGS}
        self.cnt = {e: 0 for e in ENGS}
        self.known = {e: {} for e in ENGS}
        self.res = {}
        self.n_dma = n_dma_sems
        self.dma_rr = 0
        self.dma_use = [0] * n_dma_sems
        self.sems = {}
        self.uid = 0
        self.out_points = []

    def sb(self, shape, dtype=F32, name=None):
        self.uid += 1
        name = name or ('t%d' % self.uid)
        return self.stack.enter_context(self.nc.sbuf_tensor(name, list(shape), dtype))

    def ps(self, shape, dtype=F32, name=None):
        self.uid += 1
        name = name or ('p%d' % self.uid)
        return self.stack.enter_context(self.nc.psum_tensor(name, list(shape), dtype))

    def _sem(self, name):
        if name not in self.sems:
            self.sems[name] = self.stack.enter_context(self.nc.semaphore('s_' + name))
        return self.sems[name]

    def add(self, eng, fn, reads=(), writes=(), dma=False, is_out=False):
        def flat(ks):
            out = []
            for x in ks:
                if isinstance(x, (list, tuple)):
                    out.extend(flat(x))
                else:
                    out.append(x)
            return out
        reads, writes = flat(reads), flat(writes)
        waits = {}

        def need(sp):
            if sp is None:
                return
            s, v = sp
            if waits.get(s, 0) < v:
                waits[s] = v

        for k in reads:
            st = self.res.get(k)
            if st is not None:
                need(st['w'])
                if isinstance(k, str) and len(k) == 2 and k[0] == 'b':
                    for s, v in st['r'].items():
                        if s != eng:
                            need((s, v))
        for k in writes:
            st = self.res.get(k)
            if st is not None:
                need(st['w'])
                for s, v in st['r'].items():
                    need((s, v))
        if dma:
            j = self.dma_rr
            self.dma_rr = (j + 1) % self.n_dma
            if self.dma_use[j] > 0:
                need(('d%d' % j, 16 * self.dma_use[j]))
            self.dma_use[j] += 1
            sp = ('d%d' % j, 16 * self.dma_use[j])
            inc = 16
        else:
            self.cnt[eng] += 1
            sp = (eng, self.cnt[eng])
            inc = 1
        kn = self.known[eng]
        fw = []
        for s, v in waits.items():
            if s == eng and (eng == 'tensor' or not SAME_ENGINE_SYNC):
                continue
            if kn.get(s, 0) >= v:
                continue
            kn[s] = v
            fw.append((s, v))
        for s, _ in fw:
            self._sem(s)
        self._sem(sp[0])
        self.ops[eng].append((fw, fn, sp[0], inc))
        for k in writes:
            self.res[k] = {'w': sp, 'r': {}}
        for k in reads:
            st = self.res.setdefault(k, {'w': None, 'r': {}})
            if st['r'].get(sp[0], 0) < sp[1]:
                st['r'][sp[0]] = sp[1]
        if is_out:
            self.out_points.append(sp)
        return sp

    def dma(self, out, in_, reads=(), writes=(), eng='sync', is_out=False, **kw):
        return self.add(eng, lambda e: e.dma_start(out=out, in_=in_, **kw),
                        reads, writes, dma=True, is_out=is_out)

    def mm(self, out, lhsT, rhs, start=True, stop=True, reads=(), writes=()):
        assert (lhsT.dtype == F32) == (rhs.dtype == F32), (lhsT.dtype, rhs.dtype)
        return self.add('tensor', lambda e: e.matmul(out, lhsT, rhs, start=start, stop=stop),
                        reads, writes)

    def tr(self, out, in_, ident, reads=(), writes=()):
        return self.add('tensor', lambda e: e.transpose(out, in_, ident), reads, writes)

    def act(self, out, in_, func, reads=(), writes=(), eng='scalar', **kw):
        return self.add(eng, lambda e: e.activation(out, in_, func, **kw), reads, writes)

    def tt(self, out, a, b, op, reads=(), writes=(), eng='vector'):
        return self.add(eng, lambda e: e.tensor_tensor(out, a, b, op), reads, writes)

    def ts(self, out, a, s1, s2, op0, op1=None, reads=(), writes=(), eng='vector'):
        if op1 is None:
            return self.add(eng, lambda e: e.tensor_scalar(out, a, s1, None, op0), reads, writes)
        return self.add(eng, lambda e: e.tensor_scalar(out, a, s1, s2, op0, op1), reads, writes)

    def stt(self, out, a, s, b, op0, op1, reads=(), writes=(), eng='vector'):
        return self.add(eng, lambda e: e.scalar_tensor_tensor(out, a, s, b, op0, op1), reads, writes)

    def cp(self, out, in_, reads=(), writes=(), eng='vector'):
        if eng == 'scalar':
            return self.add(eng, lambda e: e.copy(out, in_), reads, writes)
        return self.add(eng, lambda e: e.tensor_copy(out, in_), reads, writes)

    def memset(self, ap, val, writes=(), eng='vector'):
        return self.add(eng, lambda e: e.memset(ap, val), (), writes)

    def red(self, out, in_, op, reads=(), writes=(), axis=None):
        axis = axis or AX.X
        return self.add('vector', lambda e: e.tensor_reduce(out, in_, axis, op), reads, writes)

    def emit(self):
        nc = self.nc
        final = {}
        for s, v in self.out_points:
            if final.get(s, 0) < v:
                final[s] = v
        with nc.Block() as block:
            def replay(name, e):
                for fw, fn, s, inc in self.ops[name]:
                    for ws, wv in fw:
                        e.wait_ge(self.sems[ws], wv)
                    fn(e).then_inc(self.sems[s], inc)
                if name == 'sync':
                    for fs, fv in final.items():
                        e.wait_ge(self.sems[fs], fv)

            @block.tensor
            def _(e):
                replay('tensor', e)

            @block.vector
            def _(e):
                replay('vector', e)

            @block.scalar
            def _(e):
                replay('scalar', e)

            @block.gpsimd
            def _(e):
                replay('gpsimd', e)

            @block.sync
            def _(e):
                replay('sync', e)

    def close(self):
        self.stack.close()


IN_SPECS = [
    ('xp', [T, D]), ('xs', [NS, D]),
    ('st_s5_re', [DEPTH, NS, 1024]), ('st_s5_im', [DEPTH, NS, 1024]),
    ('st_ml_C', [DEPTH, NS, 16384]), ('st_ml_n', [DEPTH, NS, 256]), ('st_ml_m', [DEPTH, NS, 4]),
    ('st_gla_S', [DEPTH, NS, 16384]), ('st_rw_S', [DEPTH, NS, 16384]),
    ('st_rw_shift', [DEPTH, NS, 896]), ('st_ffn_conv', [DEPTH, NS, 2, DFF]),
    ('norm_mix', [DEPTH, D]), ('w_in', [DEPTH, D, DIN]),
    ('s5_lam_re', [DEPTH, 16, 64]), ('s5_lam_im', [DEPTH, 16, 64]), ('s5_log_dt', [DEPTH, 16]),
    ('s5_b_re', [DEPTH, 16, 64, 16]), ('s5_b_im', [DEPTH, 16, 64, 16]),
    ('s5_c_re', [DEPTH, 16, 16, 64]), ('s5_c_im', [DEPTH, 16, 16, 64]),
    ('s5_d', [DEPTH, 256]), ('s5_w_glu', [DEPTH, 256, 256]),
    ('ml_gate_bias', [DEPTH, 8]), ('ml_norm', [DEPTH, 256]),
    ('gla_w_alpha', [DEPTH, 16, 256]), ('gla_b_alpha', [DEPTH, 256]), ('gla_norm', [DEPTH, 256]),
    ('rw_mu', [DEPTH, 896]), ('rw_w0', [DEPTH, 256]), ('rw_w2', [DEPTH, 32, 256]),
    ('rw_a0', [DEPTH, 256]), ('rw_a2', [DEPTH, 32, 256]), ('rw_g2', [DEPTH, 64, 256]),
    ('rw_k_k', [DEPTH, 256]), ('rw_k_a', [DEPTH, 256]), ('rw_r_k', [DEPTH, 256]),
    ('rw_norm', [DEPTH, 256]), ('w_out', [DEPTH, D, D]), ('norm_ffn', [DEPTH, D]),
    ('ffn_w_up', [DEPTH, D, 2 * DFF]), ('ffn_conv_w', [DEPTH, 3, DFF]), ('ffn_conv_b', [DEPTH, DFF]),
    ('ffn_w_down', [DEPTH, DFF, D]), ('norm_final', [1, D]),
]
OUT_SPECS = [
    ('y_p', [T, D]), ('y_s', [NS, D]),
    ('p_s5_re', [DEPTH, 1024]), ('p_s5_im', [DEPTH, 1024]), ('p_ml_C', [DEPTH, 16384]),
    ('p_ml_n', [DEPTH, 256]), ('p_ml_m', [DEPTH, 4]), ('p_gla_S', [DEPTH, 16384]),
    ('p_rw_S', [DEPTH, 16384]), ('p_rw_shift', [DEPTH, 896]), ('p_ffn_conv', [DEPTH, 2, DFF]),
    ('s_s5_re', [DEPTH, NS, 1024]), ('s_s5_im', [DEPTH, NS, 1024]), ('s_ml_C', [DEPTH, NS, 16384]),
    ('s_ml_n', [DEPTH, NS, 256]), ('s_ml_m', [DEPTH, NS, 4]), ('s_gla_S', [DEPTH, NS, 16384]),
    ('s_rw_S', [DEPTH, NS, 16384]), ('s_rw_shift', [DEPTH, NS, 896]), ('s_ffn_conv', [DEPTH, NS, 2, DFF]),
]


class K:
    pass


def pkeys(ti, c0, c1):
    return ['P%d_b%d' % (ti, b) for b in range(c0 // 512, (c1 - 1) // 512 + 1)]


def v3(ap2d, n, j=8):
    return ap2d[:, 0:j * n].rearrange("p (j i) -> p j i", i=n)


def build(debug=(), ngroups=None, nlayers_setup=DEPTH):
    nc = bass.Bass("TRN2", target_bir_lowering=False)
    I = {}
    for name, shp in IN_SPECS:
        I[name] = nc.dram_tensor(name, shp, F32, kind="ExternalInput").ap()
    O = {}
    for name, shp in OUT_SPECS:
        O[name] = nc.dram_tensor(name, shp, F32, kind="ExternalOutput").ap()
    DBG = {}
    for name, shp in debug:
        DBG[name] = nc.dram_tensor(name, shp, F32, kind="ExternalOutput").ap()
    p = Prog(nc)
    k = K()
    k.p, k.I, k.O, k.DBG = p, I, O, DBG

    def mk_mask(name, pattern, op, fill, base, cm, init=1.0):
        t = p.sb([128, 128], name=name)
        p.memset(t[:], init, writes=[name], eng='gpsimd')
        p.add('gpsimd', lambda e: e.affine_select(t[:], t[:], pattern, op, fill, base=base,
                                                  channel_multiplier=cm), reads=[name], writes=[name])
        return t

    k.ident = mk_mask('ident', [[-1, 128]], ALU.is_equal, 0.0, 0, 1)
    k.triu = mk_mask('triu', [[1, 128]], ALU.is_ge, 0.0, 0, -1)
    k.maskadd = mk_mask('maskadd', [[-1, 128]], ALU.is_ge, -1e30, 0, 1, init=0.0)
    k.sellast = mk_mask('sellast', [[0, 128]], ALU.is_equal, 0.0, -127, 1)
    k.masksl = mk_mask('masksl', [[-1, 128]], ALU.is_ge, 0.0, -1, 1)
    k.masksu = mk_mask('masksu', [[1, 128]], ALU.is_ge, 0.0, -1, -1)
    k.shm = mk_mask('shm', [[1, 128]], ALU.is_equal, 0.0, -1, -1)
    k.shprev = mk_mask('shprev', [[128, 128]], ALU.is_equal, 0.0, -127, 1)
    k.ones = p.sb([128, 128], name='ones')
    p.memset(k.ones[:], 1.0, writes=['ones'])
    k.mhalf = p.sb([128, 4], name='mhalf')
    p.memset(k.mhalf[:], -0.5, writes=['mhalf'])
    gq = p.sb([128, 1], I32, name='gq')
    p.add('gpsimd', lambda e: e.iota(gq[:], [[0, 1]], base=0, channel_multiplier=1), (), ['gq'])
    gf = p.sb([128, 8], name='gf')
    gi2 = p.sb([128, 2], I32, name='gi2')
    p.cp(gf[:, 0:1], gq[:], reads=['gq'], writes=['gf'])
    p.ts(gf[:, 1:2], gf[:, 0:1], -7.5, 1.0 / 16, ALU.add, ALU.mult, reads=['gf'], writes=['gf'])
    p.cp(gi2[:, 0:1], gf[:, 1:2], reads=['gf'], writes=['gi2'])
    p.cp(gf[:, 2:3], gi2[:, 0:1], reads=['gi2'], writes=['gf'])
    p.ts(gf[:, 3:4], gf[:, 2:3], -0.5, 0.5, ALU.add, ALU.mult, reads=['gf'], writes=['gf'])
    p.cp(gi2[:, 1:2], gf[:, 3:4], reads=['gf'], writes=['gi2'])
    p.cp(gf[:, 4:5], gi2[:, 1:2], reads=['gi2'], writes=['gf'])
    k.gpar = p.sb([128, 2], name='gpar')
    p.stt(k.gpar[:, 1:2], gf[:, 4:5], -2.0, gf[:, 2:3], ALU.mult, ALU.add, reads=['gf'], writes=['gpar'])
    p.ts(k.gpar[:, 0:1], k.gpar[:, 1:2], -1.0, 1.0, ALU.mult, ALU.add, reads=['gpar'], writes=['gpar'])
    k.halfpi = p.sb([128, 1], name='halfpi')
    p.memset(k.halfpi[:], float(np.pi / 2), writes=['halfpi'])
    k.onec = p.sb([128, 1], name='onec')
    p.memset(k.onec[:], 1.0, writes=['onec'])
    kvi = p.sb([128, 128], I32, name='kvi')
    p.add('gpsimd', lambda e: e.iota(kvi[:], [[1, 128]], base=1, channel_multiplier=0), (), ['kvi'])
    k.kvec = p.sb([128, 128], name='kvec')
    p.cp(k.kvec[:], kvi[:], reads=['kvi'], writes=['kvec'])

    k.pp = [p.ps([128, 1024], name='pp%d' % i) for i in range(4)]

    def bank(i):
        return k.pp[i // 2][:, (i % 2) * 512:(i % 2) * 512 + 512]
    k.bank = bank
    k.scd = [p.sb([128, 2048], name='scd%d' % i) for i in range(4)]
    k.sc = [k.scd[i // 2][:, (i % 2) * 1024:(i % 2) * 1024 + 1024] for i in range(8)]
    k.qi = k.sc[7].bitcast(I32)
    k.uTb = p.sb([128, 2, 128], BF16, name='uTb')
    k.hmask = p.sb([128, 2], name='hmask')
    p.memset(k.hmask[:], 0.0, writes=['hmask'])
    p.memset(k.hmask[0:64, 0:1], 1.0, writes=['hmask'])
    p.memset(k.hmask[64:128, 1:2], 1.0, writes=['hmask'])
    k.hmask8 = p.sb([128, 2], name='hmask8')
    p.ts(k.hmask8[:], k.hmask[:], 0.125, None, ALU.mult, reads=['hmask'], writes=['hmask8'])
    k.scr = {}
    for nme, shp in (('q', [NS, 256]), ('k', [NS, 256]), ('v', [NS, 256]), ('a', [NS, 256]), ('b', [NS, 256]),
                     ('c', [NS, 256]), ('o', [NS, 256]), ('s', [NS, 64])):
        k.scr[nme] = nc.dram_tensor('scr_' + nme, shp, F32, kind="Internal").ap()
    k.sm = p.sb([128, 64], name='smallsc')

    k.NB = 3
    k.wb = [p.sb([128, 8, 512], BF16, name='wb%d' % i) for i in range(k.NB)]
    k.wcnt = 0

    k.bcs = p.sb([128, BCW], name='bcs')
    k.bcscr = nc.dram_tensor('bcscr', [DEPTH, BCW], F32, kind="Internal").ap()
    k.tab = p.sb([128, 4, 8, 128], name='tab')
    k.tabscr = nc.dram_tensor('tabscr', [DEPTH, 128, 4096], F32, kind="Internal").ap()
    k.wblk = {}
    k.nc = nc
    convert_weights(k)
    k.W = [setup_layer(k, l) for l in range(nlayers_setup)]

    k.X = [p.sb([128, D], name='X%d' % i) for i in range(NT)]
    k.P = [p.sb([128, DIN], name='P%d' % i) for i in range(NT)]
    k.Y = [p.sb([128, D], name='Y%d' % i) for i in range(NT)]
    if NT >= 2:
        k.repS = k.P[1][:, 0:2048]
        k.rep = k.P[1][:, 2048:2560].rearrange("p (a b) -> p a b", b=64)
    else:
        k.repS = p.sb([128, 2048], name='repS')[:, :]
        k.rep = p.sb([128, 8, 64], name='rep')[:, :, :]
    NTOK = NT * 128
    k.hT = p.sb([128, 8, NTOK], BF16, name='hT')
    k.actT = p.sb([128, NFF, NTOK], BF16, name='actT')
    k.ugb = p.sb([128, 4, NTOK + 2], name='ugb')
    k.ugb2 = k.ugb
    k.mqT = p.sb([128, 4, 128], BF16, name='mqT')
    k.mkT = p.sb([128, 4, 128], BF16, name='mkT')
    k.mAT = p.sb([128, 4, 128], BF16, name='mAT')
    k.mV = p.sb([128, 264], BF16, name='mV')
    k.mKh = p.sb([128, 256], BF16, name='mKh')
    k.gsm = p.sb([128, 16], name='gsm')
    k.rstdm = p.sb([128, max(NT, 1)], name='rstdm')
    k.rwb = p.sb([128, 8, 512], BF16, name='rwb')
    k.rwb2 = p.sb([128, 5, 256], BF16, name='rwb2')
    k.rwT = p.sb([128, 12, 128], BF16, name='rwT')
    k.mVa = p.sb([128, 4, 65], BF16, name='mVa')
    p.memset(k.mVa[:], 1.0, writes=['mVa'])
    k.msm = p.sb([128, 32], name='msm')
    k.msm2 = p.sb([128, 32], name='msm2')
    k.msm3 = p.sb([128, 16], name='msm3')
    k.scr['s16'] = nc.dram_tensor('scr_s16', [NS, 16], F32, kind="Internal").ap()

    groups = [dict(kind='s', tiles=[(0, NS)])]
    for g0 in range(0, T // 128, NT):
        groups.append(dict(kind='p', tiles=[(g0 + i, 128) for i in range(NT)], first=(g0 == 0),
                           last=(g0 + NT >= T // 128)))
    for g in (groups if ngroups is None else groups[:ngroups]):
        run_group(k, g)

    p.emit()
    p.close()
    return nc


def setup_layer(k, l):
    p, I = k.p, k.I
    W = {}
    nm = lambda s: '%s_l%d' % (s, l)

    def colvec(name, src_ap, ncols):
        t = p.sb([128, ncols], name=nm(name))
        p.dma(t[:], src_ap, writes=[nm(name)], allow_slow_non_contiguous=True)
        return t

    W['gmix'] = colvec('gmix', I['norm_mix'][l].rearrange("(k q) -> q k", q=128), 8)
    W['gffn'] = colvec('gffn', I['norm_ffn'][l].rearrange("(k q) -> q k", q=128), 8)
    W['convb'] = colvec('convb', I['ffn_conv_b'][l].rearrange("(c q) -> q c", q=128), NFF)
    cw = p.sb([128, 3, NFF], name=nm('convw'))
    for j in range(3):
        p.dma(cw[:, j, :], I['ffn_conv_w'][l, j].rearrange("(c q) -> q c", q=128), writes=[nm('convw')],
              allow_slow_non_contiguous=True)
    W['convw'] = cw
    for name, off in BO.items():
        src = I[name]
        w = src.shape[-1]
        p.dma(k.bcscr[l:l + 1, off:off + w], src[l:l + 1, :], writes=['bcscr'])
    W['bc'] = k.bcs
    wg = p.sb([128, 2, 256], name=nm('wglu'))
    p.dma(wg[:], I['s5_w_glu'][l].rearrange("(c q) n -> q c n", q=128), writes=[nm('wglu')])
    W['wglu'] = wg

    bre = p.sb([128, 8, 128], BF16, name=nm('bre'))
    bim = p.sb([128, 8, 128], BF16, name=nm('bim'))
    cre = p.sb([128, 2, 128], name=nm('cre'))
    cim = p.sb([128, 2, 128], name=nm('cim'))
    scs = k.sc
    for dst, key, src in ((bre, 'bre', 's5_b_re'), (bim, 'bim', 's5_b_im')):
        Bn = scs[1][:, 0:128].rearrange("p (j h) -> p j h", h=16)
        p.dma(Bn, I[src][l].rearrange("(j gl) p h -> (gl p) j h", gl=2), writes=['sc1'])
        for j in range(8):
            c0 = ((2 * j) % 8) * 16
            zi = 2 + (j % 2)
            Z, zk, bk = scs[zi][:, 0:128], 'sc%d' % zi, 'b%d' % zi
            p.memset(Z, 0.0, writes=[zk])
            p.cp(Z[0:64, c0:c0 + 16], Bn[0:64, j, :], reads=['sc1'], writes=[zk])
            p.cp(Z[64:128, c0 + 16:c0 + 32], Bn[64:128, j, :], reads=['sc1'], writes=[zk])
            p.tr(k.bank(zi)[:, 0:128], Z, k.ident[:, :], reads=[zk, 'ident'], writes=[bk])
            p.cp(dst[:, j, :], k.bank(zi)[:, 0:128], reads=[bk], writes=[nm(key)], eng='scalar')
    for dst, key, src in ((cre, 'cre', 's5_c_re'), (cim, 'cim', 's5_c_im')):
        Cn = scs[4][:, 0:128].rearrange("p (c q) -> p c q", q=64)
        p.dma(Cn, I[src][l].rearrange("(c g) h p -> (g h) c p", g=8), writes=['sc4'])
        for kc in range(2):
            Zc = scs[5][:, kc * 128:(kc + 1) * 128]
            p.ts(Zc[:, 0:64], Cn[:, kc, :], k.gpar[:, 0:1], None, ALU.mult, reads=['sc4', 'gpar'], writes=['sc5'])
            p.ts(Zc[:, 64:128], Cn[:, kc, :], k.gpar[:, 1:2], None, ALU.mult, reads=['sc4', 'gpar'], writes=['sc5'])
            p.tr(k.bank(4 + kc)[:, 0:128], Zc, k.ident[:, :], reads=['sc5', 'ident'], writes=['b%d' % (4 + kc)])
            p.cp(dst[:, kc, :], k.bank(4 + kc)[:, 0:128], reads=['b%d' % (4 + kc)], writes=[nm(key)], eng='scalar')
    p.ts(cim[:], cim[:], -1.0, None, ALU.mult, reads=[nm('cim')], writes=[nm('cim')])
    W['bre'], W['bim'], W['cre'], W['ncim'] = bre, bim, cre, cim

    lre = colvec('lamre', I['s5_lam_re'][l].rearrange("(j gl) p -> (gl p) j", gl=2), 8)
    lim = colvec('lamim', I['s5_lam_im'][l].rearrange("(j gl) p -> (gl p) j", gl=2), 8)
    ldt = p.sb([128, 8], name=nm('ldt'))
    for gl in range(2):
        src = I['s5_log_dt'][l:l + 1, :].rearrange("o (j gl) -> o gl j", gl=2)[:, gl, :]
        p.dma(ldt[gl * 64:gl * 64 + 64, :], src.partition_broadcast(64), writes=[nm('ldt')],
              allow_slow_non_contiguous=True)
    sv = p.sb([128, 12, 8], name=nm('s5sv'))
    svk = nm('s5sv')
    rw_ = dict(reads=[svk, nm('lamre'), nm('lamim'), nm('ldt')], writes=[svk])
    dt, a_, th, na = sv[:, 0, :], sv[:, 1, :], sv[:, 2, :], sv[:, 3, :]
    p.act(dt, ldt[:], AF.Exp, **rw_)
    p.tt(a_, lre[:], dt, ALU.mult, **rw_)
    p.tt(th, lim[:], dt, ALU.mult, **rw_)
    p.ts(na, a_, -1.0, None, ALU.mult, **rw_)
    sc = k.sc
    S = lambda i: sc[i][:].rearrange("p (j i) -> p j i", i=128)
    kb = k.kvec[:].unsqueeze(1).to_broadcast([128, 8, 128])
    bj = lambda v: v.unsqueeze(2).to_broadcast([128, 8, 128])
    allk = ['sc%d' % i for i in range(8)]
    rws = dict(reads=allk + [svk, 'kvec', 'halfpi'], writes=allk)
    p.tt(S(0), kb, bj(a_), ALU.mult, **rws)
    p.act(sc[0][:], sc[0][:], AF.Exp, **rws)
    p.tt(S(1), kb, bj(na), ALU.mult, **rws)
    p.act(sc[1][:], sc[1][:], AF.Exp, **rws)
    p.tt(S(2), kb, bj(th), ALU.mult, **rws)
    qi = k.qi
    p.ts(sc[3][:], sc[2][:], float(1.0 / (2 * np.pi)), None, ALU.mult, **rws)
    p.cp(qi[:], sc[3][:], reads=allk, writes=['sc7'])
    p.cp(sc[3][:], qi[:], reads=['sc7'], writes=allk)
    p.stt(sc[2][:], sc[3][:], float(-2 * np.pi), sc[2][:], ALU.mult, ALU.add, **rws)
    p.act(sc[3][:], sc[2][:], AF.Sin, scale=0.5, **rws)
    p.act(sc[4][:], sc[2][:], AF.Abs, **rws)
    p.act(sc[4][:], sc[4][:], AF.Sin, scale=-0.5, bias=k.halfpi[:], **rws)
    p.stt(sc[5][:], sc[3][:], 2.0, sc[4][:], ALU.mult, ALU.mult, **rws)
    p.tt(sc[6][:], sc[3][:], sc[3][:], ALU.mult, **rws)
    p.ts(sc[6][:], sc[6][:], -2.0, 1.0, ALU.mult, ALU.add, **rws)
    ep_re, ep_im, ei_re, ei_im = (k.tab[:, i, :, :] for i in range(4))
    tk = ['tab']
    rwt = dict(reads=allk + tk + [svk], writes=tk + allk + [svk])
    F = lambda t: t.rearrange("p j i -> p (j i)")
    p.tt(F(ep_re), sc[0][:], sc[6][:], ALU.mult, **rwt)
    p.tt(F(ep_im), sc[0][:], sc[5][:], ALU.mult, **rwt)
    p.tt(sc[2][:], sc[1][:], sc[6][:], ALU.mult, **rwt)
    p.stt(sc[3][:], sc[1][:], -1.0, sc[5][:], ALU.mult, ALU.mult, **rwt)
    nre, nim, den, cre_, cim_, t1, t2 = (sv[:, i, :] for i in range(4, 11))
    rwv = dict(reads=[svk, nm('lamre'), nm('lamim')] + tk, writes=[svk])
    p.ts(nre, ep_re[:, :, 0], -1.0, None, ALU.add, **rwv)
    p.cp(nim, ep_im[:, :, 0], **rwv)
    p.tt(den, lre[:], lre[:], ALU.mult, **rwv)
    p.tt(t1, lim[:], lim[:], ALU.mult, **rwv)
    p.tt(den, den, t1, ALU.add, **rwv)
    p.add('vector', lambda e: e.reciprocal(den, den), **rwv)
    p.tt(t1, nre, lre[:], ALU.mult, **rwv)
    p.tt(t2, nim, lim[:], ALU.mult, **rwv)
    p.tt(t1, t1, t2, ALU.add, **rwv)
    p.tt(cre_, t1, den, ALU.mult, **rwv)
    p.tt(t1, nim, lre[:], ALU.mult, **rwv)
    p.tt(t2, nre, lim[:], ALU.mult, **rwv)
    p.tt(t1, t1, t2, ALU.subtract, **rwv)
    p.tt(cim_, t1, den, ALU.mult, **rwv)
    p.tt(S(4), S(2), bj(cre_), ALU.mult, **rwt)
    p.tt(S(5), S(3), bj(cim_), ALU.mult, **rwt)
    p.tt(F(ei_re), sc[4][:], sc[5][:], ALU.subtract, **rwt)
    p.tt(S(4), S(3), bj(cre_), ALU.mult, **rwt)
    p.tt(S(5), S(2), bj(cim_), ALU.mult, **rwt)
    p.tt(F(ei_im), sc[4][:], sc[5][:], ALU.add, **rwt)
    W['ep_re'], W['ep_im'], W['ei_re'], W['ei_im'] = ep_re, ep_im, ei_re, ei_im
    W['tk'] = tk
    p.dma(k.tabscr[l], k.tab[:].rearrange("p a j i -> p (a j i)"), reads=['tab'], writes=['tabscr'])
    W['hp_re'] = p.sb([128, 8], name=nm('hp_re'))
    W['hp_im'] = p.sb([128, 8], name=nm('hp_im'))
    p.memset(W['hp_re'][:], 0.0, writes=[nm('hp')])
    p.memset(W['hp_im'][:], 0.0, writes=[nm('hp')])
    W['carry'] = p.sb([128, NFF, 2], name=nm('carry'))
    p.memset(W['carry'][:], 0.0, writes=[nm('carry')])

    wa = p.sb([16, 256], name=nm('walpha'))
    p.dma(wa[:], I['gla_w_alpha'][l], writes=[nm('walpha')])
    W['walpha'] = wa
    nb = colvec('nbalpha', I['gla_b_alpha'][l].rearrange("(c q) -> q c", q=128), 2)
    p.ts(nb[:], nb[:], -1.0, None, ALU.mult, reads=[nm('nbalpha')], writes=[nm('nbalpha')])
    W['nbalpha'] = nb
    W['glaS'] = p.sb([128, 2, 64], name=nm('glaS'))
    W['glaSb'] = p.sb([128, 2, 64], BF16, name=nm('glaSb'))
    p.memset(W['glaS'][:], 0.0, writes=[nm('glaS')])
    p.memset(W['glaSb'][:], 0.0, writes=[nm('glaS')])

    W['mlCT'] = p.sb([128, 2, 65], name=nm('mlCT'))
    W['mlCTb'] = p.sb([128, 2, 65], BF16, name=nm('mlCTb'))
    W['mprev'] = p.sb([128, 4], name=nm('mprev'))
    p.memset(W['mlCT'][:], 0.0, writes=[nm('mlCT')])
    p.memset(W['mlCTb'][:], 0.0, writes=[nm('mlCT')])
    p.memset(W['mprev'][:], 0.0, writes=[nm('mprev')])

    for key, src, r0, r1 in (('rwWw', 'rw_w2', 0, 32), ('rwWa', 'rw_a2', 32, 64), ('rwWg', 'rw_g2', 64, 128)):
        t = p.sb([128, 256], name=nm(key))
        p.memset(t[:], 0.0, writes=[nm(key)])
        p.dma(t[r0:r1, :], I[src][l], writes=[nm(key)])
        W[key] = t
    W['rwA'] = p.sb([128, 2, 64], name=nm('rwA'))
    p.memset(W['rwA'][:], 0.0, writes=[nm('rwA')])
    W['rwAb'] = p.sb([128, 2, 64], BF16, name=nm('rwAb'))
    p.memset(W['rwAb'][:], 0.0, writes=[nm('rwA')])
    W['rcprev'] = p.sb([128, 896], name=nm('rcprev'))
    p.memset(W['rcprev'][:], 0.0, writes=[nm('rcprev')])
    return W


def wslice(k, name, l, r0, r1, c0, c1):
    key = 'ws_%s_%d_%d_%d' % (name, l, r0, c0)
    if key not in k.wblk:
        k.wblk[key] = k.nc.dram_tensor(key, [128, (r1 - r0) // 128, c1 - c0], BF16, kind="Internal").ap()
    return k.wblk[key], key


def convert_weights(k):
    p, I = k.p, k.I
    for l in range(DEPTH):
        blocks = []
        for c0 in range(0, DIN, 512):
            blocks.append(('w_in', 0, D, c0, min(c0 + 512, DIN)))
        for half in range(2):
            blocks.append(('w_out', 0, D, half * 512, half * 512 + 512))
        for jb in range(0, DFF, 512):
            w = min(512, DFF - jb)
            blocks.append(('ffn_w_up', 0, D, jb, jb + w))
            blocks.append(('ffn_w_up', 0, D, DFF + jb, DFF + jb + w))
        for half in range(2):
            for (k0, kc_n) in ((0, 8), (8, 8), (16, 6)):
                blocks.append(('ffn_w_down', k0 * 128, (k0 + kc_n) * 128, half * 512, half * 512 + 512))
        for name, r0, r1, c0, c1 in blocks:
            dst, key = wslice(k, name, l, r0, r1, c0, c1)
            p.dma(dst, I[name][l, r0:r1, c0:c1].rearrange("(k q) c -> q k c", q=128), writes=[key], eng='gpsimd')


def wblock(k, name, l, r0, r1, c0, c1):
    p = k.p
    i = k.wcnt % k.NB
    k.wcnt += 1
    key = 'wb%d' % i
    src, ckey = wslice(k, name, l, r0, r1, c0, c1)
    kc = (r1 - r0) // 128
    p.dma(k.wb[i][:, 0:kc, 0:c1 - c0], src, reads=[ckey], writes=[key])
    return k.wb[i], key


def rmsnorm_T(k, xt, xkey, n, gcol, gkey, tok0, outT, outkey, rstd_out=None, rstd_key=None):
    p = k.p
    sm = k.sm
    p.act(k.sc[7][:n, :], xt[:n, :], AF.Square, reads=[xkey], writes=['sc7', 'sm'], accum_out=sm[:n, 0:1])
    p.ts(sm[:n, 1:2], sm[:n, 0:1], 1.0 / D, EPS, ALU.mult, ALU.add, reads=['sm'], writes=['sm'])
    if rstd_out is not None:
        p.tt(rstd_out[:n, 0:1], sm[:n, 1:2], k.mhalf[:n, 0:1], ALU.pow, reads=['sm', 'mhalf'], writes=[rstd_key], eng='gpsimd')
        src, skey = xt, xkey
    else:
        p.tt(sm[:n, 2:3], sm[:n, 1:2], k.mhalf[:n, 0:1], ALU.pow, reads=['sm', 'mhalf'], writes=['sm'], eng='gpsimd')
        p.ts(k.sc[7][:n, :], xt[:n, :], sm[:n, 2:3], None, ALU.mult, reads=[xkey, 'sm'], writes=['sc7'])
        src, skey = k.sc[7], 'sc7'
    for kc in range(8):
        p.tr(k.pp[0][:, kc * n:(kc + 1) * n], src[:n, kc * 128:(kc + 1) * 128], k.ident[:n, :n],
             reads=[skey, 'ident'], writes=['b0', 'b1'])
    p.tt(outT[:, :, tok0:tok0 + n], v3(k.pp[0], n), gcol[:, :].unsqueeze(2).to_broadcast([128, 8, n]),
         ALU.mult, reads=['b0', 'b1', gkey], writes=[outkey])


def gelu(k, out, x, t1, t2, rw):
    p = k.p
    p.tt(t1, x, x, ALU.mult, **rw)
    p.ts(t1, t1, 0.044715, 1.0, ALU.mult, ALU.add, **rw)
    p.tt(t1, t1, x, ALU.mult, **rw)
    p.act(t2, t1, AF.Sigmoid, scale=GELU_C, **rw)
    p.tt(out, x, t2, ALU.mult, **rw)


def s5_tile(k, l, g, ti, n, tidx):
    p, W = k.p, k.W[l]
    nm = lambda s: '%s_l%d' % (s, l)
    Pt, Yt = k.P[ti], k.Y[ti]
    pk, yk = pkeys(ti, 0, 256), 'Y%d' % ti
    sc = k.sc
    prompt = g['kind'] == 'p'
    for c in range(2):
        p.tr(k.bank(7)[:, c * n:(c + 1) * n], Pt[:n, c * 128:(c + 1) * 128], k.ident[:n, :n],
             reads=[pk, 'ident'], writes=['b7'])
    uT = k.uTb[:, :, 0:n]
    p.cp(uT, k.bank(7)[:, 0:2 * n].rearrange("p (c i) -> p c i", i=n), reads=['b7'], writes=['sc0'], eng='scalar')
    for j in range(8):
        p.mm(k.pp[1][:, j * n:(j + 1) * n], W['bre'][:, j, :], uT[:, j // 4, :],
             reads=['sc0', nm('bre')], writes=['b2', 'b3'])
        p.mm(k.pp[2][:, j * n:(j + 1) * n], W['bim'][:, j, :], uT[:, j // 4, :],
             reads=['sc0', nm('bim')], writes=['b4', 'b5'])
    if prompt:
        tab = lambda t: t[:, :, :]
    else:
        tab = lambda t: t[:, :, 0:1].to_broadcast([128, 8, n])
    bre3, bim3 = v3(k.pp[1], n), v3(k.pp[2], n)
    S = lambda i: v3(sc[i], n)
    tk = W['tk']
    rw = dict(reads=['b2', 'b3', 'b4', 'b5', 'sc1', 'sc2', 'sc3', 'sc4', 'sc5', 'sc6'] + tk,
              writes=['sc1', 'sc2', 'sc3', 'sc4', 'sc5', 'sc6'])
    p.tt(S(1), bre3, tab(W['ei_re']), ALU.mult, **rw)
    p.tt(S(2), bim3, tab(W['ei_im']), ALU.mult, **rw)
    p.tt(S(3), S(1), S(2), ALU.subtract, **rw)
    p.tt(S(1), bim3, tab(W['ei_re']), ALU.mult, **rw)
    p.tt(S(2), bre3, tab(W['ei_im']), ALU.mult, **rw)
    p.tt(S(4), S(1), S(2), ALU.add, **rw)
    if prompt and STOP[0] == 11:
        return
    if prompt:
        for j in range(8):
            for zi, ci, hp in ((3, 5, 'hp_re'), (4, 6, 'hp_im')):
                p.add('vector', lambda e, zi=zi, ci=ci, hp=hp, j=j: e.tensor_tensor_scan(
                    sc[ci][:, j * n:(j + 1) * n], k.ones[:, 0:n], sc[zi][:, j * n:(j + 1) * n],
                    W[hp][:, j:j + 1], ALU.mult, ALU.add),
                    reads=['sc3', 'sc4', 'ones', nm('hp')], writes=['sc%d' % ci])
    else:
        for si, ci, src in ((3, 5, 'st_s5_re'), (4, 6, 'st_s5_im')):
            p.dma(sc[7][:n, :], k.I[src][l], writes=['sc7'])
            for j in range(8):
                p.tr(k.bank(0)[:, j * n:(j + 1) * n], sc[7][:n, j * 128:(j + 1) * 128], k.ident[:n, :n],
                     reads=['sc7', 'ident'], writes=['b0'])
            p.tt(S(ci), S(si), v3(k.bank(0), n), ALU.add, reads=['b0', 'sc%d' % si], writes=['sc%d' % ci])
    if prompt and STOP[0] == 12:
        return
    rw = dict(reads=['sc1', 'sc2', 'sc3', 'sc4', 'sc5', 'sc6'] + tk, writes=['sc1', 'sc2', 'sc3', 'sc4'])
    p.tt(S(3), S(5), tab(W['ep_re']), ALU.mult, **rw)
    p.tt(S(4), S(6), tab(W['ep_im']), ALU.mult, **rw)
    p.tt(S(1), S(3), S(4), ALU.subtract, **rw)
    p.tt(S(3), S(6), tab(W['ep_re']), ALU.mult, **rw)
    p.tt(S(4), S(5), tab(W['ep_im']), ALU.mult, **rw)
    p.tt(S(2), S(3), S(4), ALU.add, **rw)
    if prompt:
        p.cp(W['hp_re'][:, :], S(1)[:, :, n - 1], reads=['sc1'], writes=[nm('hp')])
        p.cp(W['hp_im'][:, :], S(2)[:, :, n - 1], reads=['sc2'], writes=[nm('hp')])
        if g['last'] and tidx == T // 128 - 1:
            for hp, dst in (('hp_re', 'p_s5_re'), ('hp_im', 'p_s5_im')):
                p.dma(k.O[dst][l].rearrange("(j q) -> q j", q=128), W[hp][:, :], reads=[nm('hp')],
                      is_out=True, allow_slow_non_contiguous=True)
    else:
        for hi, dst in ((1, 's_s5_re'), (2, 's_s5_im')):
            for j in range(8):
                p.tr(k.pp[0][:n, j * 128:(j + 1) * 128], sc[hi][:, j * n:(j + 1) * n], k.ident[:, :],
                     reads=['sc%d' % hi, 'ident'], writes=['b0', 'b1'] if j >= 4 else ['b0'])
            p.cp(sc[7][:n, :], k.pp[0][:n, :], reads=['b0', 'b1'], writes=['sc7'], eng='scalar')
            p.dma(k.O[dst][l], sc[7][:n, :], reads=['sc7'], is_out=True)
    yield
    if prompt and STOP[0] == 13:
        return
    for j in range(8):
        cc0 = ((2 * j) % 8) * 16
        p.mm(k.bank(6)[:n, 32 * j:32 * j + 32], S(1)[:, j, :], W['cre'][:, j // 4, cc0:cc0 + 32], start=True, stop=False,
             reads=['sc1', nm('cre')], writes=['b6'])
        p.mm(k.bank(6)[:n, 32 * j:32 * j + 32], S(2)[:, j, :], W['ncim'][:, j // 4, cc0:cc0 + 32], start=False, stop=True,
             reads=['sc2', nm('cim')], writes=['b6'])
    if prompt and STOP[0] == 14:
        return
    bcs = W['bc']
    rw = dict(reads=['b6', pk, 'bc', 'sc3', 'sc4', 'sc5'], writes=['sc3', 'sc4', 'sc5'])
    ys, t1, t2 = sc[3][:n, 0:256], sc[4][:n, 0:256], sc[5][:n, 0:256]
    p.tt(ys, Pt[:n, 0:256], bcs[:n, BO['s5_d']:BO['s5_d'] + 256], ALU.mult, **rw)
    p.tt(ys, ys, k.bank(6)[:n, 0:256], ALU.add, **rw)
    z = sc[3][:n, 256:512]
    gelu(k, z, ys, t1, t2, rw)
    yield
    for c in range(2):
        p.tr(k.bank(7)[:, c * n:(c + 1) * n], sc[3][:n, 256 + c * 128:256 + (c + 1) * 128], k.ident[:n, :n],
             reads=['sc3', 'ident'], writes=['b7'])
    p.cp(sc[4][:, 0:2 * n], k.bank(7)[:, 0:2 * n], reads=['b7'], writes=['sc4'], eng='scalar')
    zT = sc[4][:, 0:2 * n].rearrange("p (c i) -> p c i", i=n)
    for c in range(2):
        p.mm(k.bank(6)[:n, 0:256], zT[:, c, :], W['wglu'][:, c, :], start=(c == 0), stop=(c == 1),
             reads=['sc4', nm('wglu')], writes=['b6'])
    p.act(t2, k.bank(6)[:n, 0:256], AF.Sigmoid, reads=['b6'], writes=['sc5'])
    p.tt(Yt[:n, 0:256], z, t2, ALU.mult, reads=['sc3', 'sc5'], writes=[yk])


def head_post(k, o_ap, o_reads, n, gvec, gvkey, gate, gate_reads, out_ap, out_key, layernorm=False,
              extra=None, extra_reads=()):
    p = k.p
    sm = k.sm
    s7 = k.sc[7]
    h3 = lambda ap: ap.rearrange("p (h d) -> p h d", d=64)
    yv = s7[:n, 256:512]
    rw = dict(reads=list(o_reads) + ['sc7', 'sm'], writes=['sc7', 'sm'])
    if layernorm:
        p.red(sm[:n, 4:8], h3(o_ap), ALU.add, **rw)
        p.ts(sm[:n, 4:8], sm[:n, 4:8], -1.0 / 64, None, ALU.mult, **rw)
        p.tt(h3(yv), h3(o_ap), sm[:n, 4:8].unsqueeze(2).to_broadcast([n, 4, 64]), ALU.add, **rw)
        src = yv
        eps = GN_EPS
    else:
        src = o_ap
        eps = EPS
    p.act(s7[:n, 0:256], src, AF.Square, **rw)
    p.red(sm[:n, 8:12], h3(s7[:n, 0:256]), ALU.add, **rw)
    p.ts(sm[:n, 8:12], sm[:n, 8:12], 1.0 / 64, eps, ALU.mult, ALU.add, **rw)
    p.tt(sm[:n, 12:16], sm[:n, 8:12], k.mhalf[:n, 0:4], ALU.pow, reads=rw['reads'] + ['mhalf'], writes=rw['writes'], eng='gpsimd')
    p.tt(h3(yv), h3(src), sm[:n, 12:16].unsqueeze(2).to_broadcast([n, 4, 64]), ALU.mult, **rw)
    p.tt(yv, yv, gvec, ALU.mult, reads=['sc7', gvkey], writes=['sc7'])
    if extra is not None:
        p.tt(yv, yv, extra, ALU.add, reads=['sc7'] + list(extra_reads), writes=['sc7'])
    p.tt(out_ap, yv, gate, ALU.mult, reads=['sc7'] + list(gate_reads), writes=[out_key])


def gla_pre(k, l, ti, n, prompt):
    p, W = k.p, k.W[l]
    nm = lambda s: '%s_l%d' % (s, l)
    Pt, pk, sc = k.P[ti], pkeys(ti, 1288, 2328), k.sc
    p.tr(k.bank(2)[0:16, 0:n], Pt[:n, OFF['ga']:OFF['ga'] + 16], k.ident[:n, :n], reads=[pk, 'ident'], writes=['b2'])
    p.cp(sc[0][0:16, 0:n], k.bank(2)[0:16, 0:n], reads=['b2'], writes=['sc0'], eng='scalar')
    for c in range(2):
        p.mm(k.bank(3)[:, c * n:(c + 1) * n], W['walpha'][0:16, c * 128:(c + 1) * 128], sc[0][0:16, 0:n],
             reads=['sc0', nm('walpha')], writes=['b3'])
    for c in range(2):
        p.act(sc[1][:, c * n:(c + 1) * n], k.bank(3)[:, c * n:(c + 1) * n], AF.Exp, scale=-1.0,
              bias=W['nbalpha'][:, c:c + 1], reads=['b3', nm('nbalpha')], writes=['sc1'])
    p.act(sc[1][:, 0:2 * n], sc[1][:, 0:2 * n], AF.Ln, bias=k.onec[:], reads=['sc1', 'onec'], writes=['sc1'])
    if prompt:
        for c in range(2):
            p.add('vector', lambda e, c=c: e.tensor_tensor_scan(
                sc[2][:, c * n:(c + 1) * n], k.ones[:, 0:n], sc[1][:, c * n:(c + 1) * n], 0.0, ALU.mult, ALU.add),
                reads=['sc1', 'ones'], writes=['sc2'])
    else:
        p.cp(sc[2][:, 0:2 * n], sc[1][:, 0:2 * n], reads=['sc1'], writes=['sc2'])


def gla_prompt(k, l, g, ti, n, tidx):
    p, W = k.p, k.W[l]
    nm = lambda s: '%s_l%d' % (s, l)
    Pt, pk, sc = k.P[ti], pkeys(ti, 1288, 2328), k.sc
    gla_pre(k, l, ti, n, True)
    cum = sc[2][:, 0:2 * n]
    cl = cum.rearrange("p (c i) -> p c i", i=n)[:, :, n - 1]
    gs = k.gsm
    p.ts(gs[:, 0:2], cl, -1.0 / 16, None, ALU.mult, reads=['sc2'], writes=['gsm'])
    p.act(gs[:, 2:4], cl, AF.Exp, scale=-1.0 / 16, reads=['sc2'], writes=['gsm'])
    p.act(sc[3][:, 0:2 * n], cum, AF.Exp, scale=-1.0 / 16, reads=['sc2'], writes=['sc3'])
    p.act(sc[4][:, 0:2 * n], cum, AF.Exp, scale=1.0 / 16, reads=['sc2'], writes=['sc4'])
    for c in range(2):
        p.act(sc[5][:, c * n:(c + 1) * n], sc[2][:, c * n:(c + 1) * n], AF.Exp, scale=1.0 / 16,
              bias=gs[:, c:c + 1], reads=['sc2', 'gsm'], writes=['sc5'])
    for c in range(2):
        p.tr(k.bank(2)[:, c * n:(c + 1) * n], Pt[:n, OFF['gq'] + c * 128:OFF['gq'] + (c + 1) * 128],
             k.ident[:n, :n], reads=[pk, 'ident'], writes=['b2'])
        p.tr(k.bank(2)[:, (2 + c) * n:(3 + c) * n], Pt[:n, OFF['gk'] + c * 128:OFF['gk'] + (c + 1) * 128],
             k.ident[:n, :n], reads=[pk, 'ident'], writes=['b2'])
    for h in range(4):
        pr, hl = h // 2, h % 2
        p.stt(k.mqT[:, h, :n], k.bank(2)[:, pr * n:(pr + 1) * n], k.hmask8[:, hl:hl + 1],
              sc[3][:, pr * n:(pr + 1) * n], ALU.mult, ALU.mult, reads=['b2', 'hmask8', 'sc3'], writes=['mqT'])
        p.stt(k.mkT[:, h, :n], k.bank(2)[:, (2 + pr) * n:(3 + pr) * n], k.hmask[:, hl:hl + 1],
              sc[4][:, pr * n:(pr + 1) * n], ALU.mult, ALU.mult, reads=['b2', 'hmask', 'sc4'], writes=['mkT'])
    p.tt(sc[6][:, 0:2 * n], k.bank(2)[:, 2 * n:4 * n], sc[5][:, 0:2 * n], ALU.mult, reads=['b2', 'sc5'], writes=['sc6'])
    for h in range(4):
        p.mm(k.bank(3)[:n, h * n:(h + 1) * n], k.mkT[:, h, :n], k.mqT[:, h, :n], reads=['mkT', 'mqT'], writes=['b3'])
    p.tt(k.mAT[:n, :, :n], v3(k.bank(3), n, 4)[:n], k.triu[:n, :n].unsqueeze(1).to_broadcast([n, 4, n]), ALU.mult,
         reads=['b3', 'triu'], writes=['mAT'])
    p.cp(k.mV[:n, 0:256], Pt[:n, OFF['gv']:OFF['gv'] + 256], reads=[pk], writes=['mV'], eng='scalar')
    for h in range(4):
        p.mm(k.bank(4)[:n, h * 64:(h + 1) * 64], k.mAT[:n, h, :n], k.mV[:n, h * 64:(h + 1) * 64], start=True, stop=False,
             reads=['mAT', 'mV'], writes=['b4'])
        p.mm(k.bank(4)[:n, h * 64:(h + 1) * 64], k.mqT[:, h, :n], W['glaSb'][:, h // 2, :], start=False, stop=True,
             reads=['mqT', nm('glaS')], writes=['b4'])
    p.cp(sc[0][:n, 256:512], k.bank(4)[:n, 0:256], reads=['b4'], writes=['sc0'], eng='scalar')
    for c in range(2):
        p.tr(k.bank(5)[:n, c * 128:(c + 1) * 128], sc[6][:, c * n:(c + 1) * n], k.ident[:, :],
             reads=['sc6', 'ident'], writes=['b5'])
    p.cp(k.mKh[:n, 0:256], k.bank(5)[:n, 0:256], reads=['b5'], writes=['mKh'])
    for pr in range(2):
        p.mm(k.bank(5)[:, pr * 256:(pr + 1) * 256], k.mKh[:n, pr * 128:(pr + 1) * 128], k.mV[:n, 0:256],
             reads=['mKh', 'mV'], writes=['b5'])
    S = W['glaS']
    for pr in range(2):
        for hl in range(2):
            r0 = hl * 64
            c0 = pr * 256 + (2 * pr + hl) * 64
            p.stt(S[r0:r0 + 64, pr, :], S[r0:r0 + 64, pr, :], gs[r0:r0 + 64, 2 + pr:3 + pr], k.bank(5)[r0:r0 + 64, c0:c0 + 64],
                  ALU.mult, ALU.add, reads=['b5', 'gsm', nm('glaS')], writes=[nm('glaS')])
    p.cp(W['glaSb'][:], S[:], reads=[nm('glaS')], writes=[nm('glaS')])
    if g['last'] and tidx == T // 128 - 1:
        for pr in range(2):
            p.dma(k.O['p_gla_S'][l].rearrange("(pr q v) -> pr q v", pr=2, v=64)[pr], S[:, pr, :],
                  reads=[nm('glaS')], is_out=True)
    head_post(k, sc[0][:n, 256:512], ['sc0'], n, W['bc'][:n, BO['gla_norm']:BO['gla_norm'] + 256], 'bc',
              k.Y[ti][:n, 512:768], ['Y%d' % ti], k.Y[ti][:n, 512:768], 'Y%d' % ti)


def rep_gather(k, name, src_ap, src_reads, dst, width=64):
    p = k.p
    scr = k.scr[name]
    p.dma(scr, src_ap, reads=src_reads, writes=['scr_' + name])
    if width == 64:
        for vh in range(2):
            p.dma(dst[vh:128:2, :], scr.rearrange("i (h d) -> (i h) d", d=64), reads=['scr_' + name], writes=['rep'])
    else:
        p.dma(dst, scr.rearrange("i (q w) -> (i q) w", w=32), reads=['scr_' + name], writes=['rep'])


def rep_scatter(k, src, dst_sb, dst_key):
    p = k.p
    scr = k.scr['o']
    p.dma(scr.rearrange("i (q w) -> (i q) w", w=32), src, reads=['rep', 'repS'], writes=['scr_o'])
    p.dma(dst_sb, scr, reads=['scr_o'], writes=[dst_key])


def gla_sample(k, l, ti, n):
    p, W = k.p, k.W[l]
    nm = lambda s: '%s_l%d' % (s, l)
    Pt, pk, sc = k.P[ti], pkeys(ti, 1288, 2328), k.sc
    gla_pre(k, l, ti, n, False)
    p.act(sc[3][:, 0:2 * n], sc[2][:, 0:2 * n], AF.Exp, scale=-1.0 / 16, reads=['sc2'], writes=['sc3'])
    for c in range(2):
        p.tr(k.bank(2)[:n, c * 128:(c + 1) * 128], sc[3][:, c * n:(c + 1) * n], k.ident[:, :], reads=['sc3', 'ident'],
             writes=['b2'])
    p.cp(sc[4][:n, 0:256], k.bank(2)[:n, 0:256], reads=['b2'], writes=['sc4'], eng='scalar')
    rep = k.rep
    rep_gather(k, 'q', Pt[:n, OFF['gq']:OFF['gq'] + 256], [pk], rep[:, 0, :])
    rep_gather(k, 'k', Pt[:n, OFF['gk']:OFF['gk'] + 256], [pk], rep[:, 1, :])
    rep_gather(k, 'a', sc[4][:n, 0:256], ['sc4'], rep[:, 2, :])
    rep_gather(k, 'v', Pt[:n, OFF['gv']:OFF['gv'] + 256], [pk], rep[:, 3, 0:32], width=32)
    S = k.repS
    S3 = S[:].rearrange("p (d v) -> p d v", v=32)
    src = k.I['st_gla_S'][l].rearrange("i (h d vh v) -> i h vh d v", h=4, d=64, vh=2, v=32)
    for h in range(4):
        for vh in range(2):
            q = h * 2 + vh
            p.dma(S[q:128:8, :].rearrange("p (d v) -> p d v", v=32), src[:, h, vh, :, :], writes=['repS'])
    A = k.scd[0][:].rearrange("p (d v) -> p d v", v=32)
    B = k.scd[1][:].rearrange("p (d v) -> p d v", v=32)
    bd = lambda ap: ap.unsqueeze(2).to_broadcast([128, 64, 32])
    bv = lambda ap: ap.unsqueeze(1).to_broadcast([128, 64, 32])
    rw = dict(reads=['rep', 'repS', 'sc0', 'sc1', 'sc2', 'sc3'], writes=['sc0', 'sc1', 'sc2', 'sc3'])
    p.tt(A, S3, bd(rep[:, 2, :]), ALU.mult, **rw)
    p.tt(B, bd(rep[:, 1, :]), bv(rep[:, 3, 0:32]), ALU.mult, **rw)
    p.tt(S3, A, B, ALU.add, reads=['sc0', 'sc1', 'sc2', 'sc3'], writes=['repS'])
    dst = k.O['s_gla_S'][l].rearrange("i (h d vh v) -> i h vh d v", h=4, d=64, vh=2, v=32)
    for h in range(4):
        for vh in range(2):
            q = h * 2 + vh
            p.dma(dst[:, h, vh, :, :], S[q:128:8, :].rearrange("p (d v) -> p d v", v=32), reads=['repS'], is_out=True)
    p.tt(A, S3, bd(rep[:, 0, :]), ALU.mult, reads=['rep', 'repS'], writes=['sc0', 'sc1'])
    p.red(rep[:, 4, 0:32], k.scd[0][:].rearrange("p (d v) -> p v d", v=32), ALU.add, reads=['sc0', 'sc1'], writes=['rep'])
    p.ts(rep[:, 4, 0:32], rep[:, 4, 0:32], 0.125, None, ALU.mult, reads=['rep'], writes=['rep'])
    rep_scatter(k, rep[:, 4, 0:32], sc[0][:n, 256:512], 'sc0')
    head_post(k, sc[0][:n, 256:512], ['sc0'], n, W['bc'][:n, BO['gla_norm']:BO['gla_norm'] + 256], 'bc',
              k.Y[ti][:n, 512:768], ['Y%d' % ti], k.Y[ti][:n, 512:768], 'Y%d' % ti)


def ml_gates(k, l, ti, n):
    p, W = k.p, k.W[l]
    nm = lambda s: '%s_l%d' % (s, l)
    Pt, pk = k.P[ti], pkeys(ti, 256, 1288)
    m = k.msm
    bcs = W['bc']
    rw = dict(reads=[pk, 'bc', 'msm', 'onec'], writes=['msm'])
    p.tt(m[:n, 0:4], Pt[:n, OFF['mi']:OFF['mi'] + 4], bcs[:n, 3200:3204], ALU.add, **rw)
    p.tt(m[:n, 4:8], Pt[:n, OFF['mf']:OFF['mf'] + 4], bcs[:n, 3204:3208], ALU.add, **rw)
    p.act(m[:n, 4:8], m[:n, 4:8], AF.Exp, scale=-1.0, **rw)
    p.act(m[:n, 4:8], m[:n, 4:8], AF.Ln, bias=k.onec[:n, :], **rw)


def ml_prompt(k, l, g, ti, n, tidx):
    p, W = k.p, k.W[l]
    nm = lambda s: '%s_l%d' % (s, l)
    Pt, pk, sc = k.P[ti], pkeys(ti, 256, 1288), k.sc
    m, m2, m3 = k.msm, k.msm2, k.msm3
    ml_gates(k, l, ti, n)
    rw = dict(reads=['msm', 'msm2', 'msm3', 'b2', nm('mprev')], writes=['msm', 'msm2', 'msm3'])
    p.mm(k.bank(2)[:n, 0:4], k.triu[:n, :n], m[:n, 4:8], reads=['triu', 'msm'], writes=['b2'])
    p.ts(m[:n, 8:12], k.bank(2)[:n, 0:4], -1.0, None, ALU.mult, **rw)
    p.cp(m[:n, 28:32], m[:n, 8:12], **rw)
    p.tt(m[:n, 12:16], m[:n, 0:4], m[:n, 8:12], ALU.subtract, **rw)
    for h in range(4):
        p.ts(sc[1][:n, h * 128:(h + 1) * 128], k.ident[:n, :n], m[:n, 12 + h:13 + h], None, ALU.mult,
             reads=['ident', 'msm'], writes=['sc1'])
    for h in range(4):
        p.mm(k.bank(2)[:n, h * 128:(h + 1) * 128], k.ones[:n, :n], sc[1][:n, h * 128:(h + 1) * 128],
             reads=['ones', 'sc1'], writes=['b2'])
    for h in range(4):
        p.stt(sc[2][:n, h * 128:(h + 1) * 128], k.bank(2)[:n, h * 128:(h + 1) * 128], m[:n, 8 + h:9 + h],
              k.maskadd[:n, :n], ALU.add, ALU.add, reads=['b2', 'msm', 'maskadd'], writes=['sc2'])
    p.red(m[:n, 16:20], sc[2][:n, 0:512].rearrange("p (h s) -> p h s", s=128), ALU.max, reads=['sc2', 'msm'], writes=['msm'])
    p.tt(m[:n, 20:24], m[:n, 8:12], W['mprev'][:n, :], ALU.add, **rw)
    p.tt(m[:n, 24:28], m[:n, 20:24], m[:n, 16:20], ALU.max, **rw)
    p.ts(m2[:n, 20:24], m[:n, 24:28], -1.0, None, ALU.mult, **rw)
    for h in range(4):
        p.act(sc[3][:n, h * 128:(h + 1) * 128], sc[2][:n, h * 128:(h + 1) * 128], AF.Exp, bias=m2[:n, 20 + h:21 + h],
              reads=['sc2', 'msm2'], writes=['sc3'])
    p.tt(m2[:n, 0:4], m[:n, 20:24], m2[:n, 20:24], ALU.add, **rw)
    p.act(m2[:n, 0:4], m2[:n, 0:4], AF.Exp, **rw)
    p.act(m2[:n, 12:16], m2[:n, 20:24], AF.Exp, **rw)
    for c in range(2):
        p.tr(k.bank(3)[:, c * n:(c + 1) * n], Pt[:n, OFF['mq'] + c * 128:OFF['mq'] + (c + 1) * 128],
             k.ident[:n, :n], reads=[pk, 'ident'], writes=['b3'])
        p.tr(k.bank(3)[:, (2 + c) * n:(3 + c) * n], Pt[:n, OFF['mk'] + c * 128:OFF['mk'] + (c + 1) * 128],
             k.ident[:n, :n], reads=[pk, 'ident'], writes=['b3'])
    for h in range(4):
        pr, hl = h // 2, h % 2
        p.ts(k.mqT[:, h, :n], k.bank(3)[:, pr * n:(pr + 1) * n], k.hmask[:, hl:hl + 1], None, ALU.mult,
             reads=['b3', 'hmask'], writes=['mqT'])
        p.ts(k.mkT[:, h, :n], k.bank(3)[:, (2 + pr) * n:(3 + pr) * n], k.hmask8[:, hl:hl + 1], None, ALU.mult,
             reads=['b3', 'hmask8'], writes=['mkT'])
    for h in range(4):
        p.mm(k.bank(4)[:n, h * n:(h + 1) * n], k.mqT[:, h, :n], k.mkT[:, h, :n], reads=['mqT', 'mkT'], writes=['b4'])
    p.tt(sc[4][:n, 0:512], sc[3][:n, 0:512], k.bank(4)[:n, 0:512], ALU.mult, reads=['sc3', 'b4'], writes=['sc4'])
    for h in range(4):
        p.tr(k.bank(5)[:n, h * n:(h + 1) * n], sc[4][:n, h * 128:(h + 1) * 128], k.ident[:n, :n],
             reads=['sc4', 'ident'], writes=['b5'])
    p.cp(k.mAT[:n, :, :n], v3(k.bank(5), n, 4)[:n], reads=['b5'], writes=['mAT'], eng='scalar')
    p.cp(k.mVa[:n, :, 0:64], Pt[:n, OFF['mv']:OFF['mv'] + 256].rearrange("p (h d) -> p h d", d=64), reads=[pk],
         writes=['mVa'], eng='scalar')
    for h in range(4):
        p.mm(k.bank(6)[:n, h * 65:(h + 1) * 65], k.mAT[:n, h, :n], k.mVa[:n, h, :], reads=['mAT', 'mVa'], writes=['b6'])
        p.mm(k.bank(7)[:n, h * 65:(h + 1) * 65], k.mqT[:, h, :n], W['mlCTb'][:, h // 2, :], reads=['mqT', nm('mlCT')],
             writes=['b7'])
    nd = sc[5][:n, 0:260].rearrange("p (h e) -> p h e", e=65)
    p.cp(sc[5][:n, 0:260], k.bank(6)[:n, 0:260], reads=['b6'], writes=['sc5'], eng='scalar')
    p.tt(sc[6][:n, 0:260].rearrange("p (h e) -> p h e", e=65), k.bank(7)[:n, 0:260].rearrange("p (h e) -> p h e", e=65),
         m2[:n, 0:4].unsqueeze(2).to_broadcast([n, 4, 65]), ALU.mult, reads=['b7', 'msm2'], writes=['sc6'])
    p.tt(sc[5][:n, 0:260], sc[5][:n, 0:260], sc[6][:n, 0:260], ALU.add, reads=['sc5', 'sc6'], writes=['sc5'])
    p.ts(m2[:n, 4:8], nd[:, :, 64], -1.0, None, ALU.mult, reads=['sc5', 'msm2'], writes=['msm2'])
    p.tt(m2[:n, 8:12], nd[:, :, 64], m2[:n, 4:8], ALU.max, reads=['sc5', 'msm2'], writes=['msm2'])
    p.tt(m2[:n, 8:12], m2[:n, 8:12], m2[:n, 12:16], ALU.max, **rw)
    p.add('vector', lambda e: e.reciprocal(m2[:n, 16:20], m2[:n, 8:12]), **rw)
    p.tt(sc[0][:n, 256:512].rearrange("p (h d) -> p h d", d=64), nd[:, :, 0:64],
         m2[:n, 16:20].unsqueeze(2).to_broadcast([n, 4, 64]), ALU.mult, reads=['sc5', 'msm2'], writes=['sc0'])
    p.mm(k.bank(2)[:, 0:8], k.sellast[:n, :], m[:n, 24:32], reads=['sellast', 'msm'], writes=['b2'])
    p.cp(m3[:, 0:8], k.bank(2)[:, 0:8], **rw)
    p.tt(m3[:, 12:16], m3[:, 4:8], m3[:, 0:4], ALU.subtract, **rw)
    p.tt(m3[:, 8:12], m3[:, 12:16], W['mprev'][:, :], ALU.add, **rw)
    p.act(m3[:, 8:12], m3[:, 8:12], AF.Exp, **rw)
    p.tt(m2[:n, 24:28], m[:n, 12:16], m3[:n, 12:16], ALU.add, **rw)
    p.act(m2[:n, 24:28], m2[:n, 24:28], AF.Exp, **rw)
    p.cp(W['mprev'][:, :], m3[:, 0:4], reads=['msm3'], writes=[nm('mprev')])
    p.stt(k.mKh[:n, 0:256].rearrange("p (h d) -> p h d", d=64),
          Pt[:n, OFF['mk']:OFF['mk'] + 256].rearrange("p (h d) -> p h d", d=64), 0.125,
          m2[:n, 24:28].unsqueeze(2).to_broadcast([n, 4, 64]), ALU.mult, ALU.mult, reads=[pk, 'msm2'], writes=['mKh'])
    CT = W['mlCT']
    for pr in range(2):
        p.mm(k.bank(2 + pr)[:, 0:260], k.mKh[:n, pr * 128:(pr + 1) * 128], k.mVa[:n].rearrange("p h e -> p (h e)"),
             reads=['mKh', 'mVa'], writes=['b%d' % (2 + pr)])
        for hl in range(2):
            r0 = hl * 64
            h = 2 * pr + hl
            p.stt(CT[r0:r0 + 64, pr, :], CT[r0:r0 + 64, pr, :], m3[r0:r0 + 64, 8 + h:9 + h],
                  k.bank(2 + pr)[r0:r0 + 64, h * 65:(h + 1) * 65], ALU.mult, ALU.add,
                  reads=['b%d' % (2 + pr), 'msm3', nm('mlCT')], writes=[nm('mlCT')])
    p.cp(W['mlCTb'][:], CT[:], reads=[nm('mlCT')], writes=[nm('mlCT')])
    if g['last'] and tidx == T // 128 - 1:
        for pr in range(2):
            p.tr(k.bank(4)[0:64, pr * 128:(pr + 1) * 128], CT[:, pr, 0:64], k.ident[:, :], reads=[nm('mlCT'), 'ident'],
                 writes=['b4'])
        p.cp(sc[1][0:64, 0:256], k.bank(4)[0:64, 0:256], reads=['b4'], writes=['sc1'])
        p.dma(k.O['p_ml_C'][l].rearrange("(h v d) -> v h d", h=4, v=64), sc[1][0:64, 0:256].rearrange("p (h d) -> p h d", d=64),
              reads=['sc1'], is_out=True)
        p.dma(k.O['p_ml_n'][l].rearrange("(pr q) -> q pr", q=128), CT[:, :, 64], reads=[nm('mlCT')], is_out=True,
              allow_slow_non_contiguous=True)
        p.dma(k.O['p_ml_m'][l:l + 1, :], W['mprev'][0:1, :], reads=[nm('mprev')], is_out=True)
    head_post(k, sc[0][:n, 256:512], ['sc0'], n, W['bc'][:n, BO['ml_norm']:BO['ml_norm'] + 256], 'bc',
              k.Y[ti][:n, 256:512], ['Y%d' % ti], k.Y[ti][:n, 256:512], 'Y%d' % ti)


def ml_sample(k, l, ti, n):
    p, W = k.p, k.W[l]
    nm = lambda s: '%s_l%d' % (s, l)
    Pt, pk, sc = k.P[ti], pkeys(ti, 256, 1288), k.sc
    m, m2, m3 = k.msm, k.msm2, k.msm3
    ml_gates(k, l, ti, n)
    rw = dict(reads=['msm', 'msm2', 'msm3'], writes=['msm', 'msm2', 'msm3'])
    p.dma(m3[:n, 0:4], k.I['st_ml_m'][l], writes=['msm3'])
    p.tt(m[:n, 8:12], m3[:n, 0:4], m[:n, 4:8], ALU.subtract, **rw)
    p.tt(m[:n, 12:16], m[:n, 8:12], m[:n, 0:4], ALU.max, **rw)
    p.dma(k.O['s_ml_m'][l], m[:n, 12:16], reads=['msm'], is_out=True)
    pk4 = m2[:n, 0:16].rearrange("p (h w) -> p h w", w=4)
    p.tt(pk4[:, :, 0], m[:n, 8:12], m[:n, 12:16], ALU.subtract, **rw)
    p.tt(pk4[:, :, 1], m[:n, 0:4], m[:n, 12:16], ALU.subtract, **rw)
    p.ts(pk4[:, :, 2], m[:n, 12:16], -1.0, None, ALU.mult, **rw)
    p.act(m2[:n, 0:16], m2[:n, 0:16], AF.Exp, **rw)
    rep = k.rep
    p.dma(k.scr['s16'], m2[:n, 0:16], reads=['msm2'], writes=['scr_s16'])
    for vh in range(2):
        p.dma(rep[vh:128:2, 5, 0:4], k.scr['s16'].rearrange("i (h w) -> (i h) w", w=4), reads=['scr_s16'], writes=['rep'])
        p.dma(rep[vh:128:2, 2, :], k.I['st_ml_n'][l].rearrange("i (h d) -> (i h) d", d=64), writes=['rep'])
    rep_gather(k, 'q', Pt[:n, OFF['mq']:OFF['mq'] + 256], [pk], rep[:, 0, :])
    rep_gather(k, 'k', Pt[:n, OFF['mk']:OFF['mk'] + 256], [pk], rep[:, 1, :])
    rep_gather(k, 'v', Pt[:n, OFF['mv']:OFF['mv'] + 256], [pk], rep[:, 3, 0:32], width=32)
    C = k.repS
    C3 = C[:].rearrange("p (v d) -> p v d", d=64)
    p.dma(C[:], k.I['st_ml_C'][l].rearrange("i (q f) -> (i q) f", f=2048), writes=['repS'])
    A = k.scd[0][:].rearrange("p (v d) -> p v d", d=64)
    bq = lambda ap: ap.unsqueeze(1).to_broadcast([128, 32, 64])
    bvv = lambda ap: ap.unsqueeze(2).to_broadcast([128, 32, 64])
    w0c, wc, enc = rep[:, 5, 0:1], rep[:, 5, 1:2], rep[:, 5, 2:3]
    rr = dict(reads=['rep', 'repS', 'sc0', 'sc1', 'sc2', 'sc3'], writes=['rep', 'sc0', 'sc1', 'sc2', 'sc3'])
    t6 = rep[:, 6, :]
    sA = rep[:, 7, :]
    p.tt(t6, rep[:, 0, :], rep[:, 1, :], ALU.mult, **rr)
    p.red(sA[:, 0:1], t6, ALU.add, **rr)
    p.tt(t6, rep[:, 0, :], rep[:, 2, :], ALU.mult, **rr)
    p.red(sA[:, 1:2], t6, ALU.add, **rr)
    p.stt(sA[:, 2:3], sA[:, 0:1], 0.125, wc, ALU.mult, ALU.mult, **rr)
    p.stt(sA[:, 3:4], sA[:, 1:2], w0c, sA[:, 2:3], ALU.mult, ALU.add, **rr)
    p.ts(sA[:, 4:5], sA[:, 3:4], -1.0, None, ALU.mult, **rr)
    p.tt(sA[:, 5:6], sA[:, 3:4], sA[:, 4:5], ALU.max, **rr)
    p.tt(sA[:, 5:6], sA[:, 5:6], enc, ALU.max, **rr)
    p.add('vector', lambda e: e.reciprocal(sA[:, 5:6], sA[:, 5:6]), **rr)
    p.tt(A, C3, bq(rep[:, 0, :]), ALU.mult, **rr)
    p.red(rep[:, 4, 0:32], A, ALU.add, **rr)
    p.ts(rep[:, 4, 0:32], rep[:, 4, 0:32], w0c, None, ALU.mult, **rr)
    p.stt(rep[:, 4, 0:32], rep[:, 3, 0:32], sA[:, 2:3], rep[:, 4, 0:32], ALU.mult, ALU.add, **rr)
    p.ts(rep[:, 4, 0:32], rep[:, 4, 0:32], sA[:, 5:6], None, ALU.mult, **rr)
    rep_scatter(k, rep[:, 4, 0:32], sc[6][:n, 256:512], 'sc6')
    p.ts(t6, rep[:, 1, :], wc, 0.125, ALU.mult, ALU.mult, **rr)
    p.tt(A, bvv(rep[:, 3, 0:32]), bq(t6), ALU.mult, **rr)
    p.stt(C3, C3, w0c, A, ALU.mult, ALU.add, reads=['rep', 'repS', 'sc0', 'sc1'], writes=['repS'])
    p.dma(k.O['s_ml_C'][l].rearrange("i (q f) -> (i q) f", f=2048), C[:], reads=['repS'], is_out=True)
    p.stt(rep[:, 2, :], rep[:, 2, :], w0c, t6, ALU.mult, ALU.add, **rr)
    p.dma(k.O['s_ml_n'][l].rearrange("i (h d) -> (i h) d", d=64), rep[0:128:2, 2, :], reads=['rep'], is_out=True)
    head_post(k, sc[6][:n, 256:512], ['sc6'], n, W['bc'][:n, BO['ml_norm']:BO['ml_norm'] + 256], 'bc',
              k.Y[ti][:n, 256:512], ['Y%d' % ti], k.Y[ti][:n, 256:512], 'Y%d' % ti)


RWS = 0.6065306597126334


def rw_pre(k, l, g, ti, n, prompt):
    p, W = k.p, k.W[l]
    nm = lambda s: '%s_l%d' % (s, l)
    Pt, pk, sc = k.P[ti], pkeys(ti, 2328, 3224), k.sc
    bcs = W['bc']
    rc0 = OFF['rc']
    B = lambda name, w=256: bcs[:n, BO[name]:BO[name] + w]
    if prompt:
        for half in range(2):
            c0 = half * 448
            p.mm(k.bank(2 + half)[:n, 0:448], k.shm[:n, :n], Pt[:n, rc0 + c0:rc0 + c0 + 448], start=True, stop=False,
                 reads=[pk, 'shm'], writes=['b%d' % (2 + half)])
            p.mm(k.bank(2 + half)[:n, 0:448], k.shprev[:, :n], W['rcprev'][:, c0:c0 + 448], start=False, stop=True,
                 reads=[nm('rcprev'), 'shprev'], writes=['b%d' % (2 + half)])
            p.tt(sc[0][:n, c0:c0 + 448], k.bank(2 + half)[:n, 0:448], Pt[:n, rc0 + c0:rc0 + c0 + 448], ALU.subtract,
                 reads=['b%d' % (2 + half), pk], writes=['sc0'])
        p.cp(W['rcprev'][:, :], Pt[:, rc0:rc0 + 896], reads=[pk], writes=[nm('rcprev')], eng='gpsimd')
    else:
        p.dma(sc[0][:n, 0:896], k.I['st_rw_shift'][l], writes=['sc0'])
        p.tt(sc[0][:n, 0:896], sc[0][:n, 0:896], Pt[:n, rc0:rc0 + 896], ALU.subtract, reads=['sc0', pk], writes=['sc0'])
    p.tt(sc[0][:n, 0:896], sc[0][:n, 0:896], B('rw_mu', 896), ALU.mult, reads=['sc0', 'bc'], writes=['sc0'])
    p.tt(sc[0][:n, 0:896], sc[0][:n, 0:896], Pt[:n, rc0:rc0 + 896], ALU.add, reads=['sc0', pk], writes=['sc0'])
    rr, rk, rv = sc[0][:n, 0:256], sc[0][:n, 256:512], sc[0][:n, 512:768]
    p.act(sc[1][:n, 0:32], sc[0][:n, 768:800], AF.Tanh, reads=['sc0'], writes=['sc1'])
    p.cp(sc[1][:n, 32:64], sc[0][:n, 800:832], reads=['sc0'], writes=['sc1'])
    p.act(sc[1][:n, 64:128], sc[0][:n, 832:896], AF.Sigmoid, reads=['sc0'], writes=['sc1'])
    p.tr(k.bank(4)[:, 0:n], sc[1][:n, 0:128], k.ident[:n, :n], reads=['sc1', 'ident'], writes=['b4'])
    p.cp(sc[1][:, 128:128 + n], k.bank(4)[:, 0:n], reads=['b4'], writes=['sc1'], eng='scalar')
    T3T = sc[1][:, 128:128 + n]
    p.mm(k.bank(2)[:n, 0:256], T3T, W['rwWw'][:, :], reads=['sc1', nm('rwWw')], writes=['b2'])
    p.mm(k.bank(2)[:n, 256:512], T3T, W['rwWa'][:, :], reads=['sc1', nm('rwWa')], writes=['b2'])
    p.mm(k.bank(3)[:n, 0:256], T3T, W['rwWg'][:, :], reads=['sc1', nm('rwWg')], writes=['b3'])
    r2 = dict(reads=['b2', 'b3', 'sc0', 'sc2', 'sc3', 'sc4', 'sm', 'bc'], writes=['sc2', 'sc3', 'sc4', 'sm'])
    sgw, a_, g_ = sc[2][:n, 0:256], sc[2][:n, 256:512], sc[2][:n, 512:768]
    p.tt(sgw, k.bank(2)[:n, 0:256], B('rw_w0'), ALU.add, **r2)
    p.act(sgw, sgw, AF.Sigmoid, **r2)
    p.tt(a_, k.bank(2)[:n, 256:512], B('rw_a0'), ALU.add, **r2)
    p.act(a_, a_, AF.Sigmoid, **r2)
    p.cp(g_, k.bank(3)[:n, 0:256], **r2)
    kk, kt, be, tmp = sc[3][:n, 0:256], sc[3][:n, 256:512], sc[3][:n, 512:768], sc[3][:n, 768:1024]
    sm = k.sm
    h3 = lambda ap: ap.rearrange("p (h d) -> p h d", d=64)
    p.tt(kk, rk, B('rw_k_k'), ALU.mult, **r2)
    p.tt(tmp, kk, kk, ALU.mult, **r2)
    p.red(sm[:n, 16:20], h3(tmp), ALU.add, **r2)
    p.ts(sm[:n, 16:20], sm[:n, 16:20], 1e-24, None, ALU.max, **r2)
    p.tt(sm[:n, 20:24], sm[:n, 16:20], k.mhalf[:n, 0:4], ALU.pow, reads=r2['reads'] + ['mhalf'], writes=r2['writes'], eng='gpsimd')
    p.tt(h3(kk), h3(kk), sm[:n, 20:24].unsqueeze(2).to_broadcast([n, 4, 64]), ALU.mult, **r2)
    p.stt(tmp, a_, -1.0, B('rw_k_a'), ALU.add, ALU.mult, **r2)
    p.stt(kt, tmp, 1.0, rk, ALU.add, ALU.mult, **r2)
    p.tt(be, a_, kk, ALU.mult, **r2)
    p.tt(tmp, rr, kt, ALU.mult, **r2)
    p.tt(tmp, tmp, B('rw_r_k'), ALU.mult, **r2)
    p.red(sm[:n, 24:28], h3(tmp), ALU.add, **r2)
    p.tt(h3(sc[4][:n, 0:256]), h3(rv), sm[:n, 24:28].unsqueeze(2).to_broadcast([n, 4, 64]), ALU.mult, **r2)
    p.cp(sc[4][:n, 256:512], rv, **r2)


def rw_post(k, l, ti, n, o_ap, o_reads, g_ap, g_reads, bv_ap, bv_reads):
    W = k.W[l]
    nm = lambda s: '%s_l%d' % (s, l)
    head_post(k, o_ap, o_reads, n, W['bc'][:n, BO['rw_norm']:BO['rw_norm'] + 256], 'bc', g_ap, g_reads,
              k.Y[ti][:n, 768:1024], 'Y%d' % ti, layernorm=True, extra=bv_ap, extra_reads=bv_reads)


def rw_prompt(k, l, g, ti, n, tidx):
    p, W = k.p, k.W[l]
    nm = lambda s: '%s_l%d' % (s, l)
    Pt, pk, sc = k.P[ti], pkeys(ti, 2328, 3224), k.sc
    rw_pre(k, l, g, ti, n, True)
    if STOP[0] == 21:
        return
    rr = sc[0][:n, 0:256]
    sgw = sc[2][:n, 0:256]
    kk, kt, be = sc[3][:n, 0:256], sc[3][:n, 256:512], sc[3][:n, 512:768]
    V = sc[4][:n, 256:512]
    gs = k.gsm
    p.mm(k.bank(3)[:n, 256:512], k.triu[:n, :n], sgw, reads=['triu', 'sc2'], writes=['b3'])
    cws = k.bank(3)[:n, 256:512]
    eP, eN, ePm1 = sc[5][:n, 0:256], sc[5][:n, 256:512], sc[5][:n, 512:768]
    p.act(eP, cws, AF.Exp, scale=-RWS, reads=['b3'], writes=['sc5'])
    p.act(eN, cws, AF.Exp, scale=RWS, reads=['b3'], writes=['sc5'])
    p.tt(ePm1, cws, sgw, ALU.subtract, reads=['b3', 'sc2'], writes=['sc5'])
    p.act(ePm1, ePm1, AF.Exp, scale=-RWS, reads=['sc5'], writes=['sc5'])
    p.cp(sc[5][:n, 768:1024], cws, reads=['b3'], writes=['sc5'])
    for pr in range(2):
        p.mm(k.bank(4)[:, 256 + 8 * pr:264 + 8 * pr], sc[5][:n, 768 + pr * 128:768 + (pr + 1) * 128], k.sellast[:n, 0:8],
             reads=['sc5', 'sellast'], writes=['b4'])
    p.act(gs[:, 4:6], k.bank(4)[:, 256:272:8], AF.Exp, scale=-RWS, reads=['b4'], writes=['gsm'])
    r6 = dict(reads=['sc0', 'sc3', 'sc5', 'sc6'], writes=['sc6'])
    p.tt(sc[6][:n, 0:256], kk, ePm1, ALU.mult, **r6)
    p.tt(sc[6][:n, 256:512], be, eN, ALU.mult, **r6)
    p.tt(sc[6][:n, 512:768], kt, eN, ALU.mult, **r6)
    p.tt(sc[6][:n, 768:1024], rr, eP, ALU.mult, **r6)
    if STOP[0] == 22:
        return
    for X in range(4):
        for c in range(2):
            p.tr(k.pp[3][:, (X * 2 + c) * n:(X * 2 + c + 1) * n], sc[6][:n, X * 256 + c * 128:X * 256 + (c + 1) * 128],
                 k.ident[:n, :n], reads=['sc6', 'ident'], writes=['b6', 'b7'])
    if STOP[0] == 27:
        return
    kaTm = k.rwT[:, 0:4, :]
    rTm = k.rwT[:, 4:8, :]
    for h in range(4):
        pr, hl = h // 2, h % 2
        p.ts(kaTm[:, h, :n], k.pp[3][:, pr * n:(pr + 1) * n], k.hmask[:, hl:hl + 1], None, ALU.mult,
             reads=['b6', 'b7', 'hmask'], writes=['rwT'])
        p.ts(rTm[:, h, :n], k.pp[3][:, (6 + pr) * n:(7 + pr) * n], k.hmask[:, hl:hl + 1], None, ALU.mult,
             reads=['b6', 'b7', 'hmask'], writes=['rwT'])
    if STOP[0] == 28:
        return
    beT = k.rwT[:, 8:10, :]
    ktT = k.rwT[:, 10:12, :]
    p.cp(beT, k.pp[3][:, 2 * n:4 * n].rearrange("p (c i) -> p c i", i=128), reads=['b6', 'b7'], writes=['rwT'], eng='scalar')
    p.cp(ktT, k.pp[3][:, 4 * n:6 * n].rearrange("p (c i) -> p c i", i=128), reads=['b6', 'b7'], writes=['rwT'], eng='scalar')
    p.ts(k.rwb2[:n, 3, :], sc[6][:n, 256:512], -1.0, None, ALU.mult, reads=['sc6'], writes=['rwTok'])
    p.cp(k.rwb2[:n, 4, :], sc[6][:n, 512:768], reads=['sc6'], writes=['rwTok'], eng='gpsimd')
    if STOP[0] == 23:
        return
    H4 = lambda ap: ap.rearrange("p (h i) -> p h i", i=128)
    mb = lambda m_: m_[:n, :n].unsqueeze(1).to_broadcast([n, 4, n])
    rb = k.rwb
    LA, LAT, MT, QKT, nQBT, TT, LB, LBT = (rb[:n, i, :] for i in range(8))
    Vb, RHSb, Ub = k.rwb2[:n, 0, :], k.rwb2[:n, 1, :], k.rwb2[:n, 2, :]
    for h in range(4):
        pr = h // 2
        p.mm(k.bank(2)[:n, h * n:(h + 1) * n], kaTm[:, h, :n], beT[:, pr, :n], reads=['rwT'], writes=['b2'])
        p.mm(k.bank(3)[:n, h * n:(h + 1) * n], beT[:, pr, :n], kaTm[:, h, :n], reads=['rwT'], writes=['b3'])
        p.mm(k.bank(4)[:n, h * n:(h + 1) * n], ktT[:, pr, :n], kaTm[:, h, :n], reads=['rwT'], writes=['b4'])
        p.mm(k.bank(5)[:n, h * n:(h + 1) * n], ktT[:, pr, :n], rTm[:, h, :n], reads=['rwT'], writes=['b5'])
    p.stt(H4(LA), H4(k.bank(2)[:n, 0:512]), -1.0, mb(k.masksl), ALU.mult, ALU.mult, reads=['b2', 'masksl'], writes=['rwLA'])
    p.stt(H4(LAT), H4(k.bank(3)[:n, 0:512]), -1.0, mb(k.masksu), ALU.mult, ALU.mult, reads=['b3', 'masksu'], writes=['rwLAT'])
    p.tt(H4(MT), H4(k.bank(4)[:n, 0:512]), mb(k.masksu), ALU.mult, reads=['b4', 'masksu'], writes=['rwMT'])
    p.tt(H4(QKT), H4(k.bank(5)[:n, 0:512]), mb(k.triu), ALU.mult, reads=['b5', 'triu'], writes=['rwQKT'])
    for h in range(4):
        pr = h // 2
        p.mm(k.bank(2)[:n, h * n:(h + 1) * n], beT[:, pr, :n], rTm[:, h, :n], reads=['rwT'], writes=['b2'])
    p.stt(H4(nQBT), H4(k.bank(2)[:n, 0:512]), -1.0, mb(k.triu), ALU.mult, ALU.mult, reads=['b2', 'triu'], writes=['rwQBT'])
    if STOP[0] == 24:
        return
    p.tt(H4(TT), H4(LAT), mb(k.ident), ALU.add, reads=['rwLAT', 'ident'], writes=['rwTT'])
    p.cp(Vb, V, reads=['sc4'], writes=['rwVb'], eng='gpsimd')
    cur = (LA, LAT, 'rwLA', 'rwLAT')
    nxt = (LB, LBT, 'rwLB', 'rwLBT')
    for lvl in range(1, 7):
        cL, cLT, ckL, ckLT = cur
        nL, nLT, nkL, nkLT = nxt
        last = lvl == 6
        for h in range(4):
            sl = slice(h * n, (h + 1) * n)
            p.mm(k.bank(2)[:n, sl], cLT[:, sl], cL[:, sl], reads=[ckL, ckLT], writes=['b2'])
            if not last:
                p.mm(k.bank(3)[:n, sl], cL[:, sl], cLT[:, sl], reads=[ckL, ckLT], writes=['b3'])
        p.cp(nL, k.bank(2)[:n, 0:512], reads=['b2', nkL], writes=[nkL])
        if not last:
            p.cp(nLT, k.bank(3)[:n, 0:512], reads=['b3', nkLT], writes=[nkLT], eng='scalar')
        for h in range(4):
            sl = slice(h * n, (h + 1) * n)
            p.mm(k.bank(4)[:n, sl], nL[:, sl], TT[:, sl], reads=[nkL, 'rwTT'], writes=['b4'])
        p.tt(TT, TT, k.bank(4)[:n, 0:512], ALU.add, reads=['b4', 'rwTT'], writes=['rwTT'])
        cur, nxt = nxt, cur
    if STOP[0] == 25:
        return
    A = W['rwA']
    RHS, U, Osb = sc[1][:n, 768:1024], sc[6][:n, 0:256], sc[6][:n, 768:1024]
    for h in range(4):
        pr = h // 2
        hs = slice(h * 64, (h + 1) * 64)
        p.mm(k.bank(5)[:n, hs], kaTm[:, h, :n], W['rwAb'][:, pr, :], start=True, stop=False, reads=['rwT', nm('rwA')], writes=['b5'])
        p.mm(k.bank(5)[:n, hs], H4(MT)[:, h, :], Vb[:, hs], start=False, stop=True, reads=['rwMT', 'rwVb'], writes=['b5'])
    p.cp(RHSb, k.bank(5)[:n, 0:256], reads=['b5'], writes=['rwRHS'])
    for h in range(4):
        hs = slice(h * 64, (h + 1) * 64)
        p.mm(k.bank(6)[:n, hs], H4(TT)[:, h, :], RHSb[:, hs], reads=['rwTT', 'rwRHS'], writes=['b6'])
    p.cp(U, k.bank(6)[:n, 0:256], reads=['b6'], writes=['sc6'])
    p.cp(Ub, k.bank(6)[:n, 0:256], reads=['b6'], writes=['rwUb'], eng='scalar')
    for h in range(4):
        pr = h // 2
        hs = slice(h * 64, (h + 1) * 64)
        p.mm(k.bank(7)[:n, hs], rTm[:, h, :n], W['rwAb'][:, pr, :], start=True, stop=False, reads=['rwT', nm('rwA')], writes=['b7'])
        p.mm(k.bank(7)[:n, hs], H4(QKT)[:, h, :], Vb[:, hs], start=False, stop=False, reads=['rwQKT', 'rwVb'], writes=['b7'])
        p.mm(k.bank(7)[:n, hs], H4(nQBT)[:, h, :], Ub[:, hs], start=False, stop=True, reads=['rwQBT', 'rwUb'], writes=['b7'])
    p.cp(Osb, k.bank(7)[:n, 0:256], reads=['b7'], writes=['sc6'], eng='scalar')
    if STOP[0] == 26:
        return
    for pr in range(2):
        ps = slice(pr * 128, (pr + 1) * 128)
        p.mm(k.bank(5)[:, 256 + pr * 128:256 + (pr + 1) * 128], k.rwb2[:n, 4, ps], Vb[:, ps],
             start=True, stop=False, reads=['rwTok', 'rwVb'], writes=['b5'])
        p.mm(k.bank(5)[:, 256 + pr * 128:256 + (pr + 1) * 128], k.rwb2[:n, 3, ps], Ub[:, ps],
             start=False, stop=True, reads=['rwTok', 'rwUb'], writes=['b5'])
        for hl in range(2):
            r0 = hl * 64
            c0 = 256 + pr * 128 + hl * 64
            p.tt(A[r0:r0 + 64, pr, :], A[r0:r0 + 64, pr, :], k.bank(5)[r0:r0 + 64, c0:c0 + 64], ALU.add,
                 reads=['b5', nm('rwA')], writes=[nm('rwA')])
            p.ts(A[r0:r0 + 64, pr, :], A[r0:r0 + 64, pr, :], gs[r0:r0 + 64, 4 + pr:5 + pr], None, ALU.mult,
                 reads=['gsm', nm('rwA')], writes=[nm('rwA')])
    p.cp(W['rwAb'][:], A[:], reads=[nm('rwA')], writes=[nm('rwA')])
    if g['last'] and tidx == T // 128 - 1:
        for pr in range(2):
            p.tr(k.bank(4)[0:64, pr * 128:(pr + 1) * 128], A[:, pr, :], k.ident[:, :], reads=[nm('rwA'), 'ident'], writes=['b4'])
        p.cp(sc[1][0:64, 0:256], k.bank(4)[0:64, 0:256], reads=['b4'], writes=['sc1'])
        p.dma(k.O['p_rw_S'][l].rearrange("(h v d) -> v h d", h=4, v=64), sc[1][0:64, 0:256].rearrange("p (h d) -> p h d", d=64),
              reads=['sc1'], is_out=True)
    rw_post(k, l, ti, n, Osb, ['sc6'], sc[2][:n, 512:768], ['sc2'], sc[4][:n, 0:256], ['sc4'])


def rw_sample(k, l, g, ti, n):
    p, W = k.p, k.W[l]
    nm = lambda s: '%s_l%d' % (s, l)
    Pt, pk, sc = k.P[ti], pkeys(ti, 2328, 3224), k.sc
    rw_pre(k, l, g, ti, n, False)
    p.act(sc[2][:n, 0:256], sc[2][:n, 0:256], AF.Exp, scale=-RWS, reads=['sc2'], writes=['sc2'])
    p.cp(sc[6][:n, 0:256], sc[2][:n, 512:768], reads=['sc2'], writes=['sc6'])
    p.cp(sc[6][:n, 256:512], sc[4][:n, 0:256], reads=['sc4'], writes=['sc6'])
    rep = k.rep
    rep_gather(k, 'q', sc[0][:n, 0:256], ['sc0'], rep[:, 0, :])
    rep_gather(k, 'k', sc[3][:n, 256:512], ['sc3'], rep[:, 1, :])
    rep_gather(k, 'a', sc[3][:n, 0:256], ['sc3'], rep[:, 2, :])
    rep_gather(k, 'b', sc[3][:n, 512:768], ['sc3'], rep[:, 5, :])
    rep_gather(k, 'c', sc[2][:n, 0:256], ['sc2'], rep[:, 6, :])
    rep_gather(k, 'v', sc[0][:n, 512:768], ['sc0'], rep[:, 3, 0:32], width=32)
    S = k.repS
    S3 = S[:].rearrange("p (v d) -> p v d", d=64)
    p.dma(S[:], k.I['st_rw_S'][l].rearrange("i (q f) -> (i q) f", f=2048), writes=['repS'])
    A = k.scd[0][:].rearrange("p (v d) -> p v d", d=64)
    Bt = k.scd[1][:].rearrange("p (v d) -> p v d", d=64)
    bq = lambda ap: ap.unsqueeze(1).to_broadcast([128, 32, 64])
    bvv = lambda ap: ap.unsqueeze(2).to_broadcast([128, 32, 64])
    rr_ = dict(reads=['rep', 'repS', 'sc0', 'sc1', 'sc2', 'sc3'], writes=['rep', 'sc0', 'sc1', 'sc2', 'sc3'])
    p.tt(A, S3, bq(rep[:, 2, :]), ALU.mult, **rr_)
    p.red(rep[:, 4, 0:32], A, ALU.add, **rr_)
    p.tt(A, S3, bq(rep[:, 6, :]), ALU.mult, **rr_)
    p.tt(Bt, bvv(rep[:, 4, 0:32]), bq(rep[:, 5, :]), ALU.mult, **rr_)
    p.tt(A, A, Bt, ALU.subtract, **rr_)
    p.tt(Bt, bvv(rep[:, 3, 0:32]), bq(rep[:, 1, :]), ALU.mult, **rr_)
    p.tt(S3, A, Bt, ALU.add, reads=['sc0', 'sc1', 'sc2', 'sc3', 'repS'], writes=['repS'])
    p.dma(k.O['s_rw_S'][l].rearrange("i (q f) -> (i q) f", f=2048), S[:], reads=['repS'], is_out=True)
    p.tt(A, S3, bq(rep[:, 0, :]), ALU.mult, **rr_)
    p.red(rep[:, 4, 32:64], A, ALU.add, **rr_)
    rep_scatter(k, rep[:, 4, 32:64], sc[6][:n, 768:1024], 'sc6')
    rw_post(k, l, ti, n, sc[6][:n, 768:1024], ['sc6'], sc[6][:n, 0:256], ['sc6'], sc[6][:n, 256:512], ['sc6'])


def ptsel(k, l, src_dbg, name, ap, reads):
    if name in k.DBG:
        k.p.dma(k.DBG[name], ap, reads=reads, is_out=True)


def run_group(k, g):
    p, I, O = k.p, k.I, k.O
    prompt = g['kind'] == 'p'
    tiles = g['tiles']
    ntok = sum(n for _, n in tiles)
    for ti, (tidx, n) in enumerate(tiles):
        src = I['xp'][tidx * 128:tidx * 128 + n, :] if prompt else I['xs']
        p.dma(k.X[ti][:n, :], src, writes=['X%d' % ti])
    for l in range(DEPTH):
        W = k.W[l]
        nm = lambda s: '%s_l%d' % (s, l)
        p.dma(k.bcs[:], k.bcscr[l:l + 1, :].partition_broadcast(128), reads=['bcscr'], writes=['bc'])
        p.dma(k.tab[:].rearrange("p a j i -> p (a j i)"), k.tabscr[l], reads=['tabscr'], writes=['tab'])
        tok0 = 0
        for ti, (tidx, n) in enumerate(tiles):
            rmsnorm_T(k, k.X[ti], 'X%d' % ti, n, W['gmix'], nm('gmix'), tok0, k.hT, 'hT',
                      rstd_out=k.rstdm[:, ti:ti + 1], rstd_key='rstd%d' % ti)
            tok0 += n
        if len(MIXERS) < 3:
            for ti, (tidx, n) in enumerate(tiles):
                p.memset(k.Y[ti][:, 256:1024], 0.0, writes=['Y%d' % ti])

        s5g = [s5_tile(k, l, g, ti, n, tidx) for ti, (tidx, n) in enumerate(tiles)]
        if STOP[0] <= 1 and prompt:
            s5g = []
        sched = {0: [0], 3: [0], 4: [0, 1], 6: [1]} if len(s5g) == 2 else {0: [0], 3: [0], 4: [0]}
        for c0 in range(0, DIN, 512):
            w = min(512, DIN - c0)
            wt, wk = wblock(k, 'w_in', l, 0, D, c0, c0 + w)
            tok0 = 0
            for ti, (tidx, n) in enumerate(tiles):
                bi = ((c0 // 512) * len(tiles) + ti) % 2
                for kc in range(8):
                    p.mm(k.bank(bi)[:n, 0:w], k.hT[:, kc, tok0:tok0 + n], wt[:, kc, 0:w],
                         start=(kc == 0), stop=(kc == 7), reads=['hT', wk], writes=['b%d' % bi])
                p.act(k.P[ti][:n, c0:c0 + w], k.bank(bi)[:n, 0:w], AF.Identity, scale=k.rstdm[:n, ti:ti + 1],
                      reads=['b%d' % bi, 'rstd%d' % ti],
                      writes=['P%d_b%d' % (ti, c0 // 512)] + (['rep', 'repS'] if ti == 1 else []))
                tok0 += n
            for gi in sched.get(c0 // 512, []):
                if gi < len(s5g):
                    next(s5g[gi], None)
        for gen in s5g:
            for _ in gen:
                pass
        if STOP[0] <= 1 and prompt:
            return
        for ti, (tidx, n) in enumerate(tiles):
            yk = 'Y%d' % ti
            if 'ml' in MIXERS:
                p.act(k.Y[ti][:n, 256:512], k.P[ti][:n, OFF['mo']:OFF['mo'] + 256], AF.Sigmoid,
                      reads=pkeys(ti, OFF['mo'], OFF['mo'] + 256), writes=[yk])
            if 'gla' in MIXERS:
                p.act(k.Y[ti][:n, 512:768], k.P[ti][:n, OFF['gg']:OFF['gg'] + 256], AF.Sigmoid,
                      reads=pkeys(ti, OFF['gg'], OFF['gg'] + 256), writes=[yk])
                p.tt(k.Y[ti][:n, 512:768], k.Y[ti][:n, 512:768], k.P[ti][:n, OFF['gg']:OFF['gg'] + 256], ALU.mult,
                     reads=pkeys(ti, OFF['gg'], OFF['gg'] + 256) + [yk], writes=[yk])
        for ti, (tidx, n) in enumerate(tiles):
            yk = 'Y%d' % ti
            if 'ml' in MIXERS:
                if prompt:
                    ml_prompt(k, l, g, ti, n, tidx)
                else:
                    ml_sample(k, l, ti, n)
            if 'gla' in MIXERS:
                if prompt:
                    gla_prompt(k, l, g, ti, n, tidx)
                else:
                    gla_sample(k, l, ti, n)
            if 'rw' in MIXERS:
                if prompt:
                    rw_prompt(k, l, g, ti, n, tidx)
                else:
                    rw_sample(k, l, g, ti, n)
            if prompt and g['last'] and tidx == T // 128 - 1:
                p.dma(O['p_rw_shift'][l:l + 1, :], k.P[ti][n - 1:n, OFF['rc']:OFF['rc'] + 896], reads=pkeys(ti, 2328, 3224),
                      is_out=True)
            if not prompt:
                p.dma(O['s_rw_shift'][l], k.P[ti][:n, OFF['rc']:OFF['rc'] + 896], reads=pkeys(ti, 2328, 3224), is_out=True)
            if l == 0 and prompt and tidx == 0:
                ptsel(k, l, None, 'dbg_P', k.P[ti][:, :], pkeys(ti, 0, DIN))
                ptsel(k, l, None, 'dbg_Y', k.Y[ti][:, :], [yk])
        if (STOP[0] <= 2 or 10 < STOP[0] < 20) and prompt:
            return
        tok0 = 0
        for ti, (tidx, n) in enumerate(tiles):
            yt = k.Y[ti]
            for kc in range(8):
                p.tr(k.pp[0][:, kc * n:(kc + 1) * n], yt[:n, kc * 128:(kc + 1) * 128], k.ident[:n, :n],
                     reads=['Y%d' % ti, 'ident'], writes=['b0', 'b1'])
            p.cp(k.hT[:, :, tok0:tok0 + n], v3(k.pp[0], n), reads=['b0', 'b1'], writes=['hT'])
            tok0 += n
        for half in range(2):
            wt, wk = wblock(k, 'w_out', l, 0, D, half * 512, half * 512 + 512)
            tok0 = 0
            for ti, (tidx, n) in enumerate(tiles):
                bi = 2 * half + ti % 2
                for kc in range(8):
                    p.mm(k.bank(bi)[:n, :], k.hT[:, kc, tok0:tok0 + n], wt[:, kc, :],
                         start=(kc == 0), stop=(kc == 7), reads=['hT', wk], writes=['b%d' % bi])
                xs_ = k.X[ti][:n, half * 512:half * 512 + 512]
                p.tt(xs_, xs_, k.bank(bi)[:n, :], ALU.add, reads=['b%d' % bi, 'X%d' % ti], writes=['X%d' % ti])
                tok0 += n
        if STOP[0] <= 3 and prompt:
            return
        tok0 = 0
        for ti, (tidx, n) in enumerate(tiles):
            rmsnorm_T(k, k.X[ti], 'X%d' % ti, n, W['gffn'], nm('gffn'), tok0, k.hT, 'hT')
            tok0 += n
        N = ntok
        if not prompt:
            taps = []
            for j in range(2):
                tp = k.sc[1 + j]
                p.dma(k.sc[0][:N, 0:1024], I['st_ffn_conv'][l, :, j, 0:1024], writes=['sc0'])
                p.dma(k.sc[7][:N, 0:1024], I['st_ffn_conv'][l, :, j, 1024:2048], writes=['sc7'])
                p.dma(k.sc[3][:N, 0:768], I['st_ffn_conv'][l, :, j, 2048:2816], writes=['sc3'])
                for c in range(NFF):
                    srct, sk_ = ((k.sc[0], 'sc0'), (k.sc[7], 'sc7'), (k.sc[3], 'sc3'))[c // 8]
                    cc = c % 8
                    p.tr(k.bank(2 + j)[:, c * N:(c + 1) * N], srct[:N, cc * 128:(cc + 1) * 128],
                         k.ident[:N, :N], reads=[sk_, 'ident'], writes=['b%d' % (2 + j)])
                p.cp(tp[:, 0:NFF * N], k.bank(2 + j)[:, 0:NFF * N], reads=['b%d' % (2 + j)],
                     writes=['sc%d' % (1 + j)], eng='scalar')
                taps.append(tp)
            p.dma(O['s_ffn_conv'][l, :, 0, :], I['st_ffn_conv'][l, :, 1, :], is_out=True)
        ugb = k.ugb
        cw, cb = W['convw'], W['convb']
        for jb in range(0, DFF, 512):
            w = min(512, DFF - jb)
            nch = w // 128
            c0 = jb // 128
            wtg, wkg = wblock(k, 'ffn_w_up', l, 0, D, jb, jb + w)
            wtv, wkv = wblock(k, 'ffn_w_up', l, 0, D, DFF + jb, DFF + jb + w)
            par = (jb // 512) % 2
            ugP, uvP = k.pp[2 * par], k.pp[2 * par + 1]
            ugk = ['b%d' % (4 * par), 'b%d' % (4 * par + 1)]
            uvk = ['b%d' % (4 * par + 2), 'b%d' % (4 * par + 3)]
            for cl in range(nch):
                for kc in range(8):
                    p.mm(ugP[:, cl * N:(cl + 1) * N], wtg[:, kc, cl * 128:(cl + 1) * 128], k.hT[:, kc, 0:N],
                         start=(kc == 0), stop=(kc == 7), reads=['hT', wkg], writes=ugk)
            for cl in range(nch):
                for kc in range(8):
                    p.mm(uvP[:, cl * N:(cl + 1) * N], wtv[:, kc, cl * 128:(cl + 1) * 128], k.hT[:, kc, 0:N],
                         start=(kc == 0), stop=(kc == 7), reads=['hT', wkv], writes=uvk)
            V3 = lambda ap: ap[:, 0:nch * N].rearrange("p (c i) -> p c i", i=N)
            ug3, uv3 = V3(ugP), V3(uvP)
            if prompt and par == 1:
                fsi = (0, 1, 2, 5)
            else:
                fsi = (3, 4, 6, 7)
            c1, t1, t2, t3 = (V3(k.sc[i]) for i in fsi)
            kc1, kt1, kt2, kt3 = ('sc%d' % i for i in fsi)
            ugb = k.ugb2 if (prompt and par == 1) else k.ugb
            ugbk = 'ugb'
            wb_ = lambda j: cw[:, j, c0:c0 + nch].unsqueeze(2).to_broadcast([128, nch, N])
            bb_ = cb[:, c0:c0 + nch].unsqueeze(2).to_broadcast([128, nch, N])
            wk_ = [nm('convw'), nm('convb')]
            if prompt:
                p.cp(ugb[:, 0:nch, 0:2], W['carry'][:, c0:c0 + nch, :], reads=[nm('carry')], writes=[ugbk])
                p.cp(ugb[:, 0:nch, 2:2 + N], ug3, reads=ugk, writes=[ugbk], eng='scalar')
                p.cp(W['carry'][:, c0:c0 + nch, :], ugb[:, 0:nch, N:N + 2], reads=[ugbk], writes=[nm('carry')])
                u0, u1, u2 = ugb[:, 0:nch, 0:N], ugb[:, 0:nch, 1:1 + N], ugb[:, 0:nch, 2:2 + N]
                ukeys = [ugbk]
            else:
                p.cp(ugb[:, 0:nch, 2:2 + N], ug3, reads=ugk, writes=[ugbk], eng='scalar')
                u0 = taps[0][:, c0 * N:(c0 + nch) * N].rearrange("p (c i) -> p c i", i=N)
                u1 = taps[1][:, c0 * N:(c0 + nch) * N].rearrange("p (c i) -> p c i", i=N)
                u2 = ugb[:, 0:nch, 2:2 + N]
                ukeys = [ugbk, 'sc1', 'sc2']
                p.cp(k.sc[5][:, c0 * N:(c0 + nch) * N].rearrange("p (c i) -> p c i", i=N), u2, reads=[ugbk], writes=['sc5'])
            for cl in range(nch):
                c = c0 + cl
                a_ = c1[:, cl, :]
                p.ts(a_, ug3[:, cl, :], cw[:, 2, c:c + 1], cb[:, c:c + 1], ALU.mult, ALU.add,
                     reads=ugk + wk_, writes=[kc1])
                p.stt(a_, u1[:, cl, :], cw[:, 1, c:c + 1], a_, ALU.mult, ALU.add, reads=ukeys + wk_ + [kc1], writes=[kc1])
                p.stt(a_, u0[:, cl, :], cw[:, 0, c:c + 1], a_, ALU.mult, ALU.add, reads=ukeys + wk_ + [kc1], writes=[kc1])
            p.act(t3, c1, AF.Gelu_apprx_tanh, reads=[kc1], writes=[kt3])
            p.tt(k.actT[:, c0:c0 + nch, 0:N], t3, uv3, ALU.mult, reads=[kt3] + uvk, writes=['actT'])
        if prompt and g['last']:
            for j in range(2):
                p.dma(O['p_ffn_conv'][l, j].rearrange("(c q) -> q c", q=128), W['carry'][:, :, j],
                      reads=[nm('carry')], is_out=True, allow_slow_non_contiguous=True)
        if not prompt:
            for c in range(NFF):
                p.tr(k.pp[1 + c // 8][:N, (c % 8) * 128:(c % 8 + 1) * 128], k.sc[5][:, c * N:(c + 1) * N],
                     k.ident[:, :], reads=['sc5', 'ident'], writes=['b%d' % (2 + 2 * (c // 8)), 'b%d' % (3 + 2 * (c // 8))])
            for q in range(3):
                wq = 1024 if q < 2 else 768
                p.cp(k.sc[0][:N, 0:wq], k.pp[1 + q][:N, 0:wq], reads=['b%d' % (2 + 2 * q), 'b%d' % (3 + 2 * q)],
                     writes=['sc0'], eng='scalar')
                p.dma(O['s_ffn_conv'][l, :, 1, q * 1024:q * 1024 + wq], k.sc[0][:N, 0:wq], reads=['sc0'], is_out=True)
        if STOP[0] <= 4 and prompt:
            return
        kgs = [(0, 8), (8, 8), (16, 6)]
        for half in range(2):
            for gi, (k0, kc_n) in enumerate(kgs):
                wt, wk = wblock(k, 'ffn_w_down', l, k0 * 128, (k0 + kc_n) * 128, half * 512, half * 512 + 512)
                tok0 = 0
                for ti, (tidx, n) in enumerate(tiles):
                    bi = 2 * half + ti % 2
                    for kc in range(kc_n):
                        p.mm(k.bank(bi)[:n, :], k.actT[:, k0 + kc, tok0:tok0 + n], wt[:, kc, :],
                             start=(gi == 0 and kc == 0), stop=(gi == 2 and kc == kc_n - 1),
                             reads=['actT', wk], writes=['b%d' % bi])
                    if gi == 2:
                        xs_ = k.X[ti][:n, half * 512:half * 512 + 512]
                        p.tt(xs_, xs_, k.bank(bi)[:n, :], ALU.add, reads=['b%d' % bi, 'X%d' % ti],
                             writes=['X%d' % ti])
                    tok0 += n
    p.dma(k.sc[6][:, :], I['norm_final'].partition_broadcast(128), writes=['sc6'])
    for ti, (tidx, n) in enumerate(tiles):
        sm = k.sm
        xt, xkey = k.X[ti], 'X%d' % ti
        p.act(k.sc[7][:n, :], xt[:n, :], AF.Square, reads=[xkey], writes=['sc7', 'sm'], accum_out=sm[:n, 0:1])
        p.ts(sm[:n, 1:2], sm[:n, 0:1], 1.0 / D, EPS, ALU.mult, ALU.add, reads=['sm'], writes=['sm'])
        p.tt(sm[:n, 2:3], sm[:n, 1:2], k.mhalf[:n, 0:1], ALU.pow, reads=['sm', 'mhalf'], writes=['sm'], eng='gpsimd')
        p.stt(k.sc[7][:n, :], xt[:n, :], sm[:n, 2:3], k.sc[6][:n, :], ALU.mult, ALU.mult,
              reads=[xkey, 'sm', 'sc6'], writes=['sc7'])
        dst = O['y_p'][tidx * 128:tidx * 128 + n, :] if prompt else O['y_s']
        p.dma(dst, k.sc[7][:n, :], reads=['sc7'], is_out=True)


_CACHE = {}


def make_in_maps(inputs):
    f = lambda a: np.ascontiguousarray(np.asarray(a, dtype=np.float32))
    maps = []
    for c in range(NCORES):
        r = slice(NS * c, NS * (c + 1))
        m = {}
        m['xp'] = f(inputs['x_prompt'][c])
        m['xs'] = f(inputs['x_sample'][r, 0, :])
        m['st_s5_re'] = f(inputs['state_s5_re'][:, r].reshape(DEPTH, NS, 1024))
        m['st_s5_im'] = f(inputs['state_s5_im'][:, r].reshape(DEPTH, NS, 1024))
        m['st_ml_C'] = f(inputs['state_mlstm_C'][:, r].reshape(DEPTH, NS, 16384))
        m['st_ml_n'] = f(inputs['state_mlstm_n'][:, r].reshape(DEPTH, NS, 256))
        m['st_ml_m'] = f(inputs['state_mlstm_m'][:, r])
        m['st_gla_S'] = f(inputs['state_gla_S'][:, r].reshape(DEPTH, NS, 16384))
        m['st_rw_S'] = f(inputs['state_rwkv_S'][:, r].reshape(DEPTH, NS, 16384))
        m['st_rw_shift'] = f(inputs['state_rwkv_shift'][:, r])
        m['st_ffn_conv'] = f(inputs['state_ffn_conv'][:, r])
        for name, shp in IN_SPECS[11:]:
            m[name] = f(np.asarray(inputs[name]).reshape(shp))
        maps.append(m)
    return maps


def assemble(results):
    g = lambda name: [np.asarray(r[name]) for r in results]
    B = NCORES
    out = []
    out.append(np.stack(g('y_p'), 0))
    out.append(np.concatenate(g('y_s'), 0).reshape(B * NS, 1, D))
    pst = lambda name, shp: np.stack(g(name), 1).reshape((DEPTH, B) + shp)
    out.append(pst('p_s5_re', (16, 64)))
    out.append(pst('p_s5_im', (16, 64)))
    out.append(pst('p_ml_C', (4, 64, 64)))
    out.append(pst('p_ml_n', (4, 64)))
    out.append(pst('p_ml_m', (4,)))
    out.append(pst('p_gla_S', (4, 64, 64)))
    out.append(pst('p_rw_S', (4, 64, 64)))
    out.append(pst('p_rw_shift', (896,)))
    out.append(pst('p_ffn_conv', (2, DFF)))
    sst = lambda name, shp: np.concatenate(g(name), 1).reshape((DEPTH, B * NS) + shp)
    out.append(sst('s_s5_re', (16, 64)))
    out.append(sst('s_s5_im', (16, 64)))
    out.append(sst('s_ml_C', (4, 64, 64)))
    out.append(sst('s_ml_n', (4, 64)))
    out.append(sst('s_ml_m', (4,)))
    out.append(sst('s_gla_S', (4, 64, 64)))
    out.append(sst('s_rw_S', (4, 64, 64)))
    out.append(sst('s_rw_shift', (896,)))
    out.append(sst('s_ffn_conv', (2, DFF)))
    return tuple(np.ascontiguousarray(o, dtype=np.float32) for o in out)


def kernel(**inputs):
    if 'nc' not in _CACHE:
        _CACHE['nc'] = build()
    nc = _CACHE['nc']
    maps = make_in_maps(inputs)
    res = run_bass_kernel_spmd(nc, maps, core_ids=list(range(NCORES)))
    return assemble(res.results)
```

```python
import contextlib
import math
import numpy as np
import concourse.bass as bass
import concourse.mybir as mybir
from concourse.alu_op_type import AluOpType as ALU
from concourse.bass_utils import run_bass_kernel_spmd

AF = mybir.ActivationFunctionType
AX = mybir.AxisListType
F32 = mybir.dt.float32
BF16 = mybir.dt.bfloat16
I32 = mybir.dt.int32

ENGS = ['tensor', 'vector', 'scalar', 'gpsimd', 'sync']

NCORES = 8
D = 1024
DIN = 3224
DFF = 2816
NFF = 22
T = 2048
NS = 16
DEPTH = 2
EPS = 1e-6
GN_EPS = 64e-5
OFF = dict(u=0, mq=256, mk=512, mv=768, mi=1024, mf=1028, mo=1032, gq=1288, gk=1544, gv=1800,
           ga=2056, gg=2072, rc=2328)
BO = dict(s5_d=0, ml_norm=256, gla_norm=512, rw_norm=768, rw_w0=1024, rw_a0=1280, rw_k_k=1536,
          rw_k_a=1792, rw_r_k=2048, rw_mu=2304, ml_gate_bias=3200)
BCW = 3208
GELU_C = 1.5957691216057308
NT = 2
STOP = [99]
SAME_ENGINE_SYNC = True
MIXERS = ['gla', 'ml', 'rw']


class Prog:
    def __init__(self, nc, n_dma_sems=32):
        self.nc = nc
        self.stack = contextlib.ExitStack()
        self.ops = {e: [] for e in ENGS}
        self.cnt = {e: 0 for e in ENGS}
        self.known = {e: {} for e in ENGS}
        self.res = {}
        self.n_dma = n_dma_sems
        self.dma_rr = 0
        self.dma_use = [0] * n_dma_sems
        self.sems = {}
        self.uid = 0
        self.out_points = []

    def sb(self, shape, dtype=F32, name=None):
        self.uid += 1
        name = name or ('t%d' % self.uid)
        return self.stack.enter_context(self.nc.sbuf_tensor(name, list(shape), dtype))

    def ps(self, shape, dtype=F32, name=None):
        self.uid += 1
        name = name or ('p%d' % self.uid)
        return self.stack.enter_context(self.nc.psum_tensor(name, list(shape), dtype))

    def _sem(self, name):
        if name not in self.sems:
            self.sems[name] = self.stack.enter_context(self.nc.semaphore('s_' + name))
        return self.sems[name]

    def add(self, eng, fn, reads=(), writes=(), dma=False, is_out=False):
        def flat(ks):
            out = []
            for x in ks:
                if isinstance(x, (list, tuple)):
                    out.extend(flat(x))
                else:
                    out.append(x)
            return out
        reads, writes = flat(reads), flat(writes)
        waits = {}

        def need(sp):
            if sp is None:
                return
            s, v = sp
            if waits.get(s, 0) < v:
                waits[s] = v

        for k in reads:
            st = self.res.get(k)
            if st is not None:
                need(st['w'])
                if isinstance(k, str) and len(k) == 2 and k[0] == 'b':
                    for s, v in st['r'].items():
                        if s != eng:
                            need((s, v))
        for k in writes:
            st = self.res.get(k)
            if st is not None:
                need(st['w'])
                for s, v in st['r'].items():
                    need((s, v))
        if dma:
            j = self.dma_rr
            self.dma_rr = (j + 1) % self.n_dma
            if self.dma_use[j] > 0:
                need(('d%d' % j, 16 * self.dma_use[j]))
            self.dma_use[j] += 1
            sp = ('d%d' % j, 16 * self.dma_use[j])
            inc = 16
        else:
            self.cnt[eng] += 1
            sp = (eng, self.cnt[eng])
            inc = 1
        kn = self.known[eng]
        fw = []
        for s, v in waits.items():
            if s == eng and (eng == 'tensor' or not SAME_ENGINE_SYNC):
                continue
            if kn.get(s, 0) >= v:
                continue
            kn[s] = v
            fw.append((s, v))
        for s, _ in fw:
            self._sem(s)
        self._sem(sp[0])
        self.ops[eng].append((fw, fn, sp[0], inc))
        for k in writes:
            self.res[k] = {'w': sp, 'r': {}}
        for k in reads:
            st = self.res.setdefault(k, {'w': None, 'r': {}})
            if st['r'].get(sp[0], 0) < sp[1]:
                st['r'][sp[0]] = sp[1]
        if is_out:
            self.out_points.append(sp)
        return sp

    def dma(self, out, in_, reads=(), writes=(), eng='sync', is_out=False, **kw):
        return self.add(eng, lambda e: e.dma_start(out=out, in_=in_, **kw),
                        reads, writes, dma=True, is_out=is_out)

    def mm(self, out, lhsT, rhs, start=True, stop=True, reads=(), writes=()):
        assert (lhsT.dtype == F32) == (rhs.dtype == F32), (lhsT.dtype, rhs.dtype)
        return self.add('tensor', lambda e: e.matmul(out, lhsT, rhs, start=start, stop=stop),
                        reads, writes)

    def tr(self, out, in_, ident, reads=(), writes=()):
        return self.add('tensor', lambda e: e.transpose(out, in_, ident), reads, writes)

    def act(self, out, in_, func, reads=(), writes=(), eng='scalar', **kw):
        return self.add(eng, lambda e: e.activation(out, in_, func, **kw), reads, writes)

    def tt(self, out, a, b, op, reads=(), writes=(), eng='vector'):
        return self.add(eng, lambda e: e.tensor_tensor(out, a, b, op), reads, writes)

    def ts(self, out, a, s1, s2, op0, op1=None, reads=(), writes=(), eng='vector'):
        if op1 is None:
            return self.add(eng, lambda e: e.tensor_scalar(out, a, s1, None, op0), reads, writes)
        return self.add(eng, lambda e: e.tensor_scalar(out, a, s1, s2, op0, op1), reads, writes)

    def stt(self, out, a, s, b, op0, op1, reads=(), writes=(), eng='vector'):
        return self.add(eng, lambda e: e.scalar_tensor_tensor(out, a, s, b, op0, op1), reads, writes)

    def cp(self, out, in_, reads=(), writes=(), eng='vector'):
        if eng == 'scalar':
            return self.add(eng, lambda e: e.copy(out, in_), reads, writes)
        return self.add(eng, lambda e: e.tensor_copy(out, in_), reads, writes)

    def memset(self, ap, val, writes=(), eng='vector'):
        return self.add(eng, lambda e: e.memset(ap, val), (), writes)

    def red(self, out, in_, op, reads=(), writes=(), axis=None):
        axis = axis or AX.X
        return self.add('vector', lambda e: e.tensor_reduce(out, in_, axis, op), reads, writes)

    def emit(self):
        nc = self.nc
        final = {}
        for s, v in self.out_points:
            if final.get(s, 0) < v:
                final[s] = v
        with nc.Block() as block:
            def replay(name, e):
                for fw, fn, s, inc in self.ops[name]:
                    for ws, wv in fw:
                        e.wait_ge(self.sems[ws], wv)
                    fn(e).then_inc(self.sems[s], inc)
                if name == 'sync':
                    for fs, fv in final.items():
                        e.wait_ge(self.sems[fs], fv)

            @block.tensor
            def _(e):
                replay('tensor', e)

            @block.vector
            def _(e):
                replay('vector', e)

            @block.scalar
            def _(e):
                replay('scalar', e)

            @block.gpsimd
            def _(e):
                replay('gpsimd', e)

            @block.sync
            def _(e):
                replay('sync', e)

    def close(self):
        self.stack.close()


IN_SPECS = [
    ('xp', [T, D]), ('xs', [NS, D]),
    ('st_s5_re', [DEPTH, NS, 1024]), ('st_s5_im', [DEPTH, NS, 1024]),
    ('st_ml_C', [DEPTH, NS, 16384]), ('st_ml_n', [DEPTH, NS, 256]), ('st_ml_m', [DEPTH, NS, 4]),
    ('st_gla_S', [DEPTH, NS, 16384]), ('st_rw_S', [DEPTH, NS, 16384]),
    ('st_rw_shift', [DEPTH, NS, 896]), ('st_ffn_conv', [DEPTH, NS, 2, DFF]),
    ('norm_mix', [DEPTH, D]), ('w_in', [DEPTH, D, DIN]),
    ('s5_lam_re', [DEPTH, 16, 64]), ('s5_lam_im', [DEPTH, 16, 64]), ('s5_log_dt', [DEPTH, 16]),
    ('s5_b_re', [DEPTH, 16, 64, 16]), ('s5_b_im', [DEPTH, 16, 64, 16]),
    ('s5_c_re', [DEPTH, 16, 16, 64]), ('s5_c_im', [DEPTH, 16, 16, 64]),
    ('s5_d', [DEPTH, 256]), ('s5_w_glu', [DEPTH, 256, 256]),
    ('ml_gate_bias', [DEPTH, 8]), ('ml_norm', [DEPTH, 256]),
    ('gla_w_alpha', [DEPTH, 16, 256]), ('gla_b_alpha', [DEPTH, 256]), ('gla_norm', [DEPTH, 256]),
    ('rw_mu', [DEPTH, 896]), ('rw_w0', [DEPTH, 256]), ('rw_w2', [DEPTH, 32, 256]),
    ('rw_a0', [DEPTH, 256]), ('rw_a2', [DEPTH, 32, 256]), ('rw_g2', [DEPTH, 64, 256]),
    ('rw_k_k', [DEPTH, 256]), ('rw_k_a', [DEPTH, 256]), ('rw_r_k', [DEPTH, 256]),
    ('rw_norm', [DEPTH, 256]), ('w_out', [DEPTH, D, D]), ('norm_ffn', [DEPTH, D]),
    ('ffn_w_up', [DEPTH, D, 2 * DFF]), ('ffn_conv_w', [DEPTH, 3, DFF]), ('ffn_conv_b', [DEPTH, DFF]),
    ('ffn_w_down', [DEPTH, DFF, D]), ('norm_final', [1, D]),
]
OUT_SPECS = [
    ('y_p', [T, D]), ('y_s', [NS, D]),
    ('p_s5_re', [DEPTH, 1024]), ('p_s5_im', [DEPTH, 1024]), ('p_ml_C', [DEPTH, 16384]),
    ('p_ml_n', [DEPTH, 256]), ('p_ml_m', [DEPTH, 4]), ('p_gla_S', [DEPTH, 16384]),
    ('p_rw_S', [DEPTH, 16384]), ('p_rw_shift', [DEPTH, 896]), ('p_ffn_conv', [DEPTH, 2, DFF]),
    ('s_s5_re', [DEPTH, NS, 1024]), ('s_s5_im', [DEPTH, NS, 1024]), ('s_ml_C', [DEPTH, NS, 16384]),
    ('s_ml_n', [DEPTH, NS, 256]), ('s_ml_m', [DEPTH, NS, 4]), ('s_gla_S', [DEPTH, NS, 16384]),
    ('s_rw_S', [DEPTH, NS, 16384]), ('s_rw_shift', [DEPTH, NS, 896]), ('s_ffn_conv', [DEPTH, NS, 2, DFF]),
]


class K:
    pass


def pkeys(ti, c0, c1):
    return ['P%d_b%d' % (ti, b) for b in range(c0 // 512, (c1 - 1) // 512 + 1)]


def v3(ap2d, n, j=8):
    return ap2d[:, 0:j * n].rearrange("p (j i) -> p j i", i=n)


def build(debug=(), ngroups=None, nlayers_setup=DEPTH):
    nc = bass.Bass("TRN2", target_bir_lowering=False)
    I = {}
    for name, shp in IN_SPECS:
        I[name] = nc.dram_tensor(name, shp, F32, kind="ExternalInput").ap()
    O = {}
    for name, shp in OUT_SPECS:
        O[name] = nc.dram_tensor(name, shp, F32, kind="ExternalOutput").ap()
    DBG = {}
    for name, shp in debug:
        DBG[name] = nc.dram_tensor(name, shp, F32, kind="ExternalOutput").ap()
    p = Prog(nc)
    k = K()
    k.p, k.I, k.O, k.DBG = p, I, O, DBG

    def mk_mask(name, pattern, op, fill, base, cm, init=1.0):
        t = p.sb([128, 128], name=name)
        p.memset(t[:], init, writes=[name], eng='gpsimd')
        p.add('gpsimd', lambda e: e.affine_select(t[:], t[:], pattern, op, fill, base=base,
                                                  channel_multiplier=cm), reads=[name], writes=[name])
        return t

    k.ident = mk_mask('ident', [[-1, 128]], ALU.is_equal, 0.0, 0, 1)
    k.triu = mk_mask('triu', [[1, 128]], ALU.is_ge, 0.0, 0, -1)
    k.maskadd = mk_mask('maskadd', [[-1, 128]], ALU.is_ge, -1e30, 0, 1, init=0.0)
    k.sellast = mk_mask('sellast', [[0, 128]], ALU.is_equal, 0.0, -127, 1)
    k.masksl = mk_mask('masksl', [[-1, 128]], ALU.is_ge, 0.0, -1, 1)
    k.masksu = mk_mask('masksu', [[1, 128]], ALU.is_ge, 0.0, -1, -1)
    k.shm = mk_mask('shm', [[1, 128]], ALU.is_equal, 0.0, -1, -1)
    k.shprev = mk_mask('shprev', [[128, 128]], ALU.is_equal, 0.0, -127, 1)
    k.ones = p.sb([128, 128], name='ones')
    p.memset(k.ones[:], 1.0, writes=['ones'])
    k.mhalf = p.sb([128, 4], name='mhalf')
    p.memset(k.mhalf[:], -0.5, writes=['mhalf'])
    gq = p.sb([128, 1], I32, name='gq')
    p.add('gpsimd', lambda e: e.iota(gq[:], [[0, 1]], base=0, channel_multiplier=1), (), ['gq'])
    gf = p.sb([128, 8], name='gf')
    gi2 = p.sb([128, 2], I32, name='gi2')
    p.cp(gf[:, 0:1], gq[:], reads=['gq'], writes=['gf'])
    p.ts(gf[:, 1:2], gf[:, 0:1], -7.5, 1.0 / 16, ALU.add, ALU.mult, reads=['gf'], writes=['gf'])
    p.cp(gi2[:, 0:1], gf[:, 1:2], reads=['gf'], writes=['gi2'])
    p.cp(gf[:, 2:3], gi2[:, 0:1], reads=['gi2'], writes=['gf'])
    p.ts(gf[:, 3:4], gf[:, 2:3], -0.5, 0.5, ALU.add, ALU.mult, reads=['gf'], writes=['gf'])
    p.cp(gi2[:, 1:2], gf[:, 3:4], reads=['gf'], writes=['gi2'])
    p.cp(gf[:, 4:5], gi2[:, 1:2], reads=['gi2'], writes=['gf'])
    k.gpar = p.sb([128, 2], name='gpar')
    p.stt(k.gpar[:, 1:2], gf[:, 4:5], -2.0, gf[:, 2:3], ALU.mult, ALU.add, reads=['gf'], writes=['gpar'])
    p.ts(k.gpar[:, 0:1], k.gpar[:, 1:2], -1.0, 1.0, ALU.mult, ALU.add, reads=['gpar'], writes=['gpar'])
    k.halfpi = p.sb([128, 1], name='halfpi')
    p.memset(k.halfpi[:], float(np.pi / 2), writes=['halfpi'])
    k.onec = p.sb([128, 1], name='onec')
    p.memset(k.onec[:], 1.0, writes=['onec'])
    kvi = p.sb([128, 128], I32, name='kvi')
    p.add('gpsimd', lambda e: e.iota(kvi[:], [[1, 128]], base=1, channel_multiplier=0), (), ['kvi'])
    k.kvec = p.sb([128, 128], name='kvec')
    p.cp(k.kvec[:], kvi[:], reads=['kvi'], writes=['kvec'])

    k.pp = [p.ps([128, 1024], name='pp%d' % i) for i in range(4)]

    def bank(i):
        return k.pp[i // 2][:, (i % 2) * 512:(i % 2) * 512 + 512]
    k.bank = bank
    k.scd = [p.sb([128, 2048], name='scd%d' % i) for i in range(4)]
    k.sc = [k.scd[i // 2][:, (i % 2) * 1024:(i % 2) * 1024 + 1024] for i in range(8)]
    k.qi = k.sc[7].bitcast(I32)
    k.uTb = p.sb([128, 2, 128], BF16, name='uTb')
    k.hmask = p.sb([128, 2], name='hmask')
    p.memset(k.hmask[:], 0.0, writes=['hmask'])
    p.memset(k.hmask[0:64, 0:1], 1.0, writes=['hmask'])
    p.memset(k.hmask[64:128, 1:2], 1.0, writes=['hmask'])
    k.hmask8 = p.sb([128, 2], name='hmask8')
    p.ts(k.hmask8[:], k.hmask[:], 0.125, None, ALU.mult, reads=['hmask'], writes=['hmask8'])
    k.scr = {}
    for nme, shp in (('q', [NS, 256]), ('k', [NS, 256]), ('v', [NS, 256]), ('a', [NS, 256]), ('b', [NS, 256]),
                     ('c', [NS, 256]), ('o', [NS, 256]), ('s', [NS, 64])):
        k.scr[nme] = nc.dram_tensor('scr_' + nme, shp, F32, kind="Internal").ap()
    k.sm = p.sb([128, 64], name='smallsc')

    k.NB = 3
    k.wb = [p.sb([128, 8, 512], BF16, name='wb%d' % i) for i in range(k.NB)]
    k.wcnt = 0

    k.bcs = p.sb([128, BCW], name='bcs')
    k.bcscr = nc.dram_tensor('bcscr', [DEPTH, BCW], F32, kind="Internal").ap()
    k.tab = p.sb([128, 4, 8, 128], name='tab')
    k.tabscr = nc.dram_tensor('tabscr', [DEPTH, 128, 4096], F32, kind="Internal").ap()
    k.wblk = {}
    k.nc = nc
    convert_weights(k)
    k.W = [setup_layer(k, l) for l in range(nlayers_setup)]

    k.X = [p.sb([128, D], name='X%d' % i) for i in range(NT)]
    k.P = [p.sb([128, DIN], name='P%d' % i) for i in range(NT)]
    k.Y = [p.sb([128, D], name='Y%d' % i) for i in range(NT)]
    if NT >= 2:
        k.repS = k.P[1][:, 0:2048]
        k.rep = k.P[1][:, 2048:2560].rearrange("p (a b) -> p a b", b=64)
    else:
        k.repS = p.sb([128, 2048], name='repS')[:, :]
        k.rep = p.sb([128, 8, 64], name='rep')[:, :, :]
    NTOK = NT * 128
    k.hT = p.sb([128, 8, NTOK], BF16, name='hT')
    k.actT = p.sb([128, NFF, NTOK], BF16, name='actT')
    k.ugb = p.sb([128, 4, NTOK + 2], name='ugb')
    k.ugb2 = k.ugb
    k.mqT = p.sb([128, 4, 128], BF16, name='mqT')
    k.mkT = p.sb([128, 4, 128], BF16, name='mkT')
    k.mAT = p.sb([128, 4, 128], BF16, name='mAT')
    k.mV = p.sb([128, 264], BF16, name='mV')
    k.mKh = p.sb([128, 256], BF16, name='mKh')
    k.gsm = p.sb([128, 16], name='gsm')
    k.rstdm = p.sb([128, max(NT, 1)], name='rstdm')
    k.rwb = p.sb([128, 8, 512], BF16, name='rwb')
    k.rwb2 = p.sb([128, 5, 256], BF16, name='rwb2')
    k.rwT = p.sb([128, 12, 128], BF16, name='rwT')
    k.mVa = p.sb([128, 4, 65], BF16, name='mVa')
    p.memset(k.mVa[:], 1.0, writes=['mVa'])
    k.msm = p.sb([128, 32], name='msm')
    k.msm2 = p.sb([128, 32], name='msm2')
    k.msm3 = p.sb([128, 16], name='msm3')
    k.scr['s16'] = nc.dram_tensor('scr_s16', [NS, 16], F32, kind="Internal").ap()

    groups = [dict(kind='s', tiles=[(0, NS)])]
    for g0 in range(0, T // 128, NT):
        groups.append(dict(kind='p', tiles=[(g0 + i, 128) for i in range(NT)], first=(g0 == 0),
                           last=(g0 + NT >= T // 128)))
    for g in (groups if ngroups is None else groups[:ngroups]):
        run_group(k, g)

    p.emit()
    p.close()
    return nc


def setup_layer(k, l):
    p, I = k.p, k.I
    W = {}
    nm = lambda s: '%s_l%d' % (s, l)

    def colvec(name, src_ap, ncols):
        t = p.sb([128, ncols], name=nm(name))
        p.dma(t[:], src_ap, writes=[nm(name)], allow_slow_non_contiguous=True)
        return t

    W['gmix'] = colvec('gmix', I['norm_mix'][l].rearrange("(k q) -> q k", q=128), 8)
    W['gffn'] = colvec('gffn', I['norm_ffn'][l].rearrange("(k q) -> q k", q=128), 8)
    W['convb'] = colvec('convb', I['ffn_conv_b'][l].rearrange("(c q) -> q c", q=128), NFF)
    cw = p.sb([128, 3, NFF], name=nm('convw'))
    for j in range(3):
        p.dma(cw[:, j, :], I['ffn_conv_w'][l, j].rearrange("(c q) -> q c", q=128), writes=[nm('convw')],
              allow_slow_non_contiguous=True)
    W['convw'] = cw
    for name, off in BO.items():
        src = I[name]
        w = src.shape[-1]
        p.dma(k.bcscr[l:l + 1, off:off + w], src[l:l + 1, :], writes=['bcscr'])
    W['bc'] = k.bcs
    wg = p.sb([128, 2, 256], name=nm('wglu'))
    p.dma(wg[:], I['s5_w_glu'][l].rearrange("(c q) n -> q c n", q=128), writes=[nm('wglu')])
    W['wglu'] = wg

    bre = p.sb([128, 8, 128], BF16, name=nm('bre'))
    bim = p.sb([128, 8, 128], BF16, name=nm('bim'))
    cre = p.sb([128, 2, 128], name=nm('cre'))
    cim = p.sb([128, 2, 128], name=nm('cim'))
    scs = k.sc
    for dst, key, src in ((bre, 'bre', 's5_b_re'), (bim, 'bim', 's5_b_im')):
        Bn = scs[1][:, 0:128].rearrange("p (j h) -> p j h", h=16)
        p.dma(Bn, I[src][l].rearrange("(j gl) p h -> (gl p) j h", gl=2), writes=['sc1'])
        for j in range(8):
            c0 = ((2 * j) % 8) * 16
            zi = 2 + (j % 2)
            Z, zk, bk = scs[zi][:, 0:128], 'sc%d' % zi, 'b%d' % zi
            p.memset(Z, 0.0, writes=[zk])
            p.cp(Z[0:64, c0:c0 + 16], Bn[0:64, j, :], reads=['sc1'], writes=[zk])
            p.cp(Z[64:128, c0 + 16:c0 + 32], Bn[64:128, j, :], reads=['sc1'], writes=[zk])
            p.tr(k.bank(zi)[:, 0:128], Z, k.ident[:, :], reads=[zk, 'ident'], writes=[bk])
            p.cp(dst[:, j, :], k.bank(zi)[:, 0:128], reads=[bk], writes=[nm(key)], eng='scalar')
    for dst, key, src in ((cre, 'cre', 's5_c_re'), (cim, 'cim', 's5_c_im')):
        Cn = scs[4][:, 0:128].rearrange("p (c q) -> p c q", q=64)
        p.dma(Cn, I[src][l].rearrange("(c g) h p -> (g h) c p", g=8), writes=['sc4'])
        for kc in range(2):
            Zc = scs[5][:, kc * 128:(kc + 1) * 128]
            p.ts(Zc[:, 0:64], Cn[:, kc, :], k.gpar[:, 0:1], None, ALU.mult, reads=['sc4', 'gpar'], writes=['sc5'])
            p.ts(Zc[:, 64:128], Cn[:, kc, :], k.gpar[:, 1:2], None, ALU.mult, reads=['sc4', 'gpar'], writes=['sc5'])
            p.tr(k.bank(4 + kc)[:, 0:128], Zc, k.ident[:, :], reads=['sc5', 'ident'], writes=['b%d' % (4 + kc)])
            p.cp(dst[:, kc, :], k.bank(4 + kc)[:, 0:128], reads=['b%d' % (4 + kc)], writes=[nm(key)], eng='scalar')
    p.ts(cim[:], cim[:], -1.0, None, ALU.mult, reads=[nm('cim')], writes=[nm('cim')])
    W['bre'], W['bim'], W['cre'], W['ncim'] = bre, bim, cre, cim

    lre = colvec('lamre', I['s5_lam_re'][l].rearrange("(j gl) p -> (gl p) j", gl=2), 8)
    lim = colvec('lamim', I['s5_lam_im'][l].rearrange("(j gl) p -> (gl p) j", gl=2), 8)
    ldt = p.sb([128, 8], name=nm('ldt'))
    for gl in range(2):
        src = I['s5_log_dt'][l:l + 1, :].rearrange("o (j gl) -> o gl j", gl=2)[:, gl, :]
        p.dma(ldt[gl * 64:gl * 64 + 64, :], src.partition_broadcast(64), writes=[nm('ldt')],
              allow_slow_non_contiguous=True)
    sv = p.sb([128, 12, 8], name=nm('s5sv'))
    svk = nm('s5sv')
    rw_ = dict(reads=[svk, nm('lamre'), nm('lamim'), nm('ldt')], writes=[svk])
    dt, a_, th, na = sv[:, 0, :], sv[:, 1, :], sv[:, 2, :], sv[:, 3, :]
    p.act(dt, ldt[:], AF.Exp, **rw_)
    p.tt(a_, lre[:], dt, ALU.mult, **rw_)
    p.tt(th, lim[:], dt, ALU.mult, **rw_)
    p.ts(na, a_, -1.0, None, ALU.mult, **rw_)
    sc = k.sc
    S = lambda i: sc[i][:].rearrange("p (j i) -> p j i", i=128)
    kb = k.kvec[:].unsqueeze(1).to_broadcast([128, 8, 128])
    bj = lambda v: v.unsqueeze(2).to_broadcast([128, 8, 128])
    allk = ['sc%d' % i for i in range(8)]
    rws = dict(reads=allk + [svk, 'kvec', 'halfpi'], writes=allk)
    p.tt(S(0), kb, bj(a_), ALU.mult, **rws)
    p.act(sc[0][:], sc[0][:], AF.Exp, **rws)
    p.tt(S(1), kb, bj(na), ALU.mult, **rws)
    p.act(sc[1][:], sc[1][:], AF.Exp, **rws)
    p.tt(S(2), kb, bj(th), ALU.mult, **rws)
    qi = k.qi
    p.ts(sc[3][:], sc[2][:], float(1.0 / (2 * np.pi)), None, ALU.mult, **rws)
    p.cp(qi[:], sc[3][:], reads=allk, writes=['sc7'])
    p.cp(sc[3][:], qi[:], reads=['sc7'], writes=allk)
    p.stt(sc[2][:], sc[3][:], float(-2 * np.pi), sc[2][:], ALU.mult, ALU.add, **rws)
    p.act(sc[3][:], sc[2][:], AF.Sin, scale=0.5, **rws)
    p.act(sc[4][:], sc[2][:], AF.Abs, **rws)
    p.act(sc[4][:], sc[4][:], AF.Sin, scale=-0.5, bias=k.halfpi[:], **rws)
    p.stt(sc[5][:], sc[3][:], 2.0, sc[4][:], ALU.mult, ALU.mult, **rws)
    p.tt(sc[6][:], sc[3][:], sc[3][:], ALU.mult, **rws)
    p.ts(sc[6][:], sc[6][:], -2.0, 1.0, ALU.mult, ALU.add, **rws)
    ep_re, ep_im, ei_re, ei_im = (k.tab[:, i, :, :] for i in range(4))
    tk = ['tab']
    rwt = dict(reads=allk + tk + [svk], writes=tk + allk + [svk])
    F = lambda t: t.rearrange("p j i -> p (j i)")
    p.tt(F(ep_re), sc[0][:], sc[6][:], ALU.mult, **rwt)
    p.tt(F(ep_im), sc[0][:], sc[5][:], ALU.mult, **rwt)
    p.tt(sc[2][:], sc[1][:], sc[6][:], ALU.mult, **rwt)
    p.stt(sc[3][:], sc[1][:], -1.0, sc[5][:], ALU.mult, ALU.mult, **rwt)
    nre, nim, den, cre_, cim_, t1, t2 = (sv[:, i, :] for i in range(4, 11))
    rwv = dict(reads=[svk, nm('lamre'), nm('lamim')] + tk, writes=[svk])
    p.ts(nre, ep_re[:, :, 0], -1.0, None, ALU.add, **rwv)
    p.cp(nim, ep_im[:, :, 0], **rwv)
    p.tt(den, lre[:], lre[:], ALU.mult, **rwv)
    p.tt(t1, lim[:], lim[:], ALU.mult, **rwv)
    p.tt(den, den, t1, ALU.add, **rwv)
    p.add('vector', lambda e: e.reciprocal(den, den), **rwv)
    p.tt(t1, nre, lre[:], ALU.mult, **rwv)
    p.tt(t2, nim, lim[:], ALU.mult, **rwv)
    p.tt(t1, t1, t2, ALU.add, **rwv)
    p.tt(cre_, t1, den, ALU.mult, **rwv)
    p.tt(t1, nim, lre[:], ALU.mult, **rwv)
    p.tt(t2, nre, lim[:], ALU.mult, **rwv)
    p.tt(t1, t1, t2, ALU.subtract, **rwv)
    p.tt(cim_, t1, den, ALU.mult, **rwv)
    p.tt(S(4), S(2), bj(cre_), ALU.mult, **rwt)
    p.tt(S(5), S(3), bj(cim_), ALU.mult, **rwt)
    p.tt(F(ei_re), sc[4][:], sc[5][:], ALU.subtract, **rwt)
    p.tt(S(4), S(3), bj(cre_), ALU.mult, **rwt)
    p.tt(S(5), S(2), bj(cim_), ALU.mult, **rwt)
    p.tt(F(ei_im), sc[4][:], sc[5][:], ALU.add, **rwt)
    W['ep_re'], W['ep_im'], W['ei_re'], W['ei_im'] = ep_re, ep_im, ei_re, ei_im
    W['tk'] = tk
    p.dma(k.tabscr[l], k.tab[:].rearrange("p a j i -> p (a j i)"), reads=['tab'], writes=['tabscr'])
    W['hp_re'] = p.sb([128, 8], name=nm('hp_re'))
    W['hp_im'] = p.sb([128, 8], name=nm('hp_im'))
    p.memset(W['hp_re'][:], 0.0, writes=[nm('hp')])
    p.memset(W['hp_im'][:], 0.0, writes=[nm('hp')])
    W['carry'] = p.sb([128, NFF, 2], name=nm('carry'))
    p.memset(W['carry'][:], 0.0, writes=[nm('carry')])

    wa = p.sb([16, 256], name=nm('walpha'))
    p.dma(wa[:], I['gla_w_alpha'][l], writes=[nm('walpha')])
    W['walpha'] = wa
    nb = colvec('nbalpha', I['gla_b_alpha'][l].rearrange("(c q) -> q c", q=128), 2)
    p.ts(nb[:], nb[:], -1.0, None, ALU.mult, reads=[nm('nbalpha')], writes=[nm('nbalpha')])
    W['nbalpha'] = nb
    W['glaS'] = p.sb([128, 2, 64], name=nm('glaS'))
    W['glaSb'] = p.sb([128, 2, 64], BF16, name=nm('glaSb'))
    p.memset(W['glaS'][:], 0.0, writes=[nm('glaS')])
    p.memset(W['glaSb'][:], 0.0, writes=[nm('glaS')])

    W['mlCT'] = p.sb([128, 2, 65], name=nm('mlCT'))
    W['mlCTb'] = p.sb([128, 2, 65], BF16, name=nm('mlCTb'))
    W['mprev'] = p.sb([128, 4], name=nm('mprev'))
    p.memset(W['mlCT'][:], 0.0, writes=[nm('mlCT')])
    p.memset(W['mlCTb'][:], 0.0, writes=[nm('mlCT')])
    p.memset(W['mprev'][:], 0.0, writes=[nm('mprev')])

    for key, src, r0, r1 in (('rwWw', 'rw_w2', 0, 32), ('rwWa', 'rw_a2', 32, 64), ('rwWg', 'rw_g2', 64, 128)):
        t = p.sb([128, 256], name=nm(key))
        p.memset(t[:], 0.0, writes=[nm(key)])
        p.dma(t[r0:r1, :], I[src][l], writes=[nm(key)])
        W[key] = t
    W['rwA'] = p.sb([128, 2, 64], name=nm('rwA'))
    p.memset(W['rwA'][:], 0.0, writes=[nm('rwA')])
    W['rwAb'] = p.sb([128, 2, 64], BF16, name=nm('rwAb'))
    p.memset(W['rwAb'][:], 0.0, writes=[nm('rwA')])
    W['rcprev'] = p.sb([128, 896], name=nm('rcprev'))
    p.memset(W['rcprev'][:], 0.0, writes=[nm('rcprev')])
    return W


def wslice(k, name, l, r0, r1, c0, c1):
    key = 'ws_%s_%d_%d_%d' % (name, l, r0, c0)
    if key not in k.wblk:
        k.wblk[key] = k.nc.dram_tensor(key, [128, (r1 - r0) // 128, c1 - c0], BF16, kind="Internal").ap()
    return k.wblk[key], key


def convert_weights(k):
    p, I = k.p, k.I
    for l in range(DEPTH):
        blocks = []
        for c0 in range(0, DIN, 512):
            blocks.append(('w_in', 0, D, c0, min(c0 + 512, DIN)))
        for half in range(2):
            blocks.append(('w_out', 0, D, half * 512, half * 512 + 512))
        for jb in range(0, DFF, 512):
            w = min(512, DFF - jb)
            blocks.append(('ffn_w_up', 0, D, jb, jb + w))
            blocks.append(('ffn_w_up', 0, D, DFF + jb, DFF + jb + w))
        for half in range(2):
            for (k0, kc_n) in ((0, 8), (8, 8), (16, 6)):
                blocks.append(('ffn_w_down', k0 * 128, (k0 + kc_n) * 128, half * 512, half * 512 + 512))
        for name, r0, r1, c0, c1 in blocks:
            dst, key = wslice(k, name, l, r0, r1, c0, c1)
            p.dma(dst, I[name][l, r0:r1, c0:c1].rearrange("(k q) c -> q k c", q=128), writes=[key], eng='gpsimd')


def wblock(k, name, l, r0, r1, c0, c1):
    p = k.p
    i = k.wcnt % k.NB
    k.wcnt += 1
    key = 'wb%d' % i
    src, ckey = wslice(k, name, l, r0, r1, c0, c1)
    kc = (r1 - r0) // 128
    p.dma(k.wb[i][:, 0:kc, 0:c1 - c0], src, reads=[ckey], writes=[key])
    return k.wb[i], key


def rmsnorm_T(k, xt, xkey, n, gcol, gkey, tok0, outT, outkey, rstd_out=None, rstd_key=None):
    p = k.p
    sm = k.sm
    p.act(k.sc[7][:n, :], xt[:n, :], AF.Square, reads=[xkey], writes=['sc7', 'sm'], accum_out=sm[:n, 0:1])
    p.ts(sm[:n, 1:2], sm[:n, 0:1], 1.0 / D, EPS, ALU.mult, ALU.add, reads=['sm'], writes=['sm'])
    if rstd_out is not None:
        p.tt(rstd_out[:n, 0:1], sm[:n, 1:2], k.mhalf[:n, 0:1], ALU.pow, reads=['sm', 'mhalf'], writes=[rstd_key], eng='gpsimd')
        src, skey = xt, xkey
    else:
        p.tt(sm[:n, 2:3], sm[:n, 1:2], k.mhalf[:n, 0:1], ALU.pow, reads=['sm', 'mhalf'], writes=['sm'], eng='gpsimd')
        p.ts(k.sc[7][:n, :], xt[:n, :], sm[:n, 2:3], None, ALU.mult, reads=[xkey, 'sm'], writes=['sc7'])
        src, skey = k.sc[7], 'sc7'
    for kc in range(8):
        p.tr(k.pp[0][:, kc * n:(kc + 1) * n], src[:n, kc * 128:(kc + 1) * 128], k.ident[:n, :n],
             reads=[skey, 'ident'], writes=['b0', 'b1'])
    p.tt(outT[:, :, tok0:tok0 + n], v3(k.pp[0], n), gcol[:, :].unsqueeze(2).to_broadcast([128, 8, n]),
         ALU.mult, reads=['b0', 'b1', gkey], writes=[outkey])


def gelu(k, out, x, t1, t2, rw):
    p = k.p
    p.tt(t1, x, x, ALU.mult, **rw)
    p.ts(t1, t1, 0.044715, 1.0, ALU.mult, ALU.add, **rw)
    p.tt(t1, t1, x, ALU.mult, **rw)
    p.act(t2, t1, AF.Sigmoid, scale=GELU_C, **rw)
    p.tt(out, x, t2, ALU.mult, **rw)


def s5_tile(k, l, g, ti, n, tidx):
    p, W = k.p, k.W[l]
    nm = lambda s: '%s_l%d' % (s, l)
    Pt, Yt = k.P[ti], k.Y[ti]
    pk, yk = pkeys(ti, 0, 256), 'Y%d' % ti
    sc = k.sc
    prompt = g['kind'] == 'p'
    for c in range(2):
        p.tr(k.bank(7)[:, c * n:(c + 1) * n], Pt[:n, c * 128:(c + 1) * 128], k.ident[:n, :n],
             reads=[pk, 'ident'], writes=['b7'])
    uT = k.uTb[:, :, 0:n]
    p.cp(uT, k.bank(7)[:, 0:2 * n].rearrange("p (c i) -> p c i", i=n), reads=['b7'], writes=['sc0'], eng='scalar')
    for j in range(8):
        p.mm(k.pp[1][:, j * n:(j + 1) * n], W['bre'][:, j, :], uT[:, j // 4, :],
             reads=['sc0', nm('bre')], writes=['b2', 'b3'])
        p.mm(k.pp[2][:, j * n:(j + 1) * n], W['bim'][:, j, :], uT[:, j // 4, :],
             reads=['sc0', nm('bim')], writes=['b4', 'b5'])
    if prompt:
        tab = lambda t: t[:, :, :]
    else:
        tab = lambda t: t[:, :, 0:1].to_broadcast([128, 8, n])
    bre3, bim3 = v3(k.pp[1], n), v3(k.pp[2], n)
    S = lambda i: v3(sc[i], n)
    tk = W['tk']
    rw = dict(reads=['b2', 'b3', 'b4', 'b5', 'sc1', 'sc2', 'sc3', 'sc4', 'sc5', 'sc6'] + tk,
              writes=['sc1', 'sc2', 'sc3', 'sc4', 'sc5', 'sc6'])
    p.tt(S(1), bre3, tab(W['ei_re']), ALU.mult, **rw)
    p.tt(S(2), bim3, tab(W['ei_im']), ALU.mult, **rw)
    p.tt(S(3), S(1), S(2), ALU.subtract, **rw)
    p.tt(S(1), bim3, tab(W['ei_re']), ALU.mult, **rw)
    p.tt(S(2), bre3, tab(W['ei_im']), ALU.mult, **rw)
    p.tt(S(4), S(1), S(2), ALU.add, **rw)
    if prompt and STOP[0] == 11:
        return
    if prompt:
        for j in range(8):
            for zi, ci, hp in ((3, 5, 'hp_re'), (4, 6, 'hp_im')):
                p.add('vector', lambda e, zi=zi, ci=ci, hp=hp, j=j: e.tensor_tensor_scan(
                    sc[ci][:, j * n:(j + 1) * n], k.ones[:, 0:n], sc[zi][:, j * n:(j + 1) * n],
                    W[hp][:, j:j + 1], ALU.mult, ALU.add),
                    reads=['sc3', 'sc4', 'ones', nm('hp')], writes=['sc%d' % ci])
    else:
        for si, ci, src in ((3, 5, 'st_s5_re'), (4, 6, 'st_s5_im')):
            p.dma(sc[7][:n, :], k.I[src][l], writes=['sc7'])
            for j in range(8):
                p.tr(k.bank(0)[:, j * n:(j + 1) * n], sc[7][:n, j * 128:(j + 1) * 128], k.ident[:n, :n],
                     reads=['sc7', 'ident'], writes=['b0'])
            p.tt(S(ci), S(si), v3(k.bank(0), n), ALU.add, reads=['b0', 'sc%d' % si], writes=['sc%d' % ci])
    if prompt and STOP[0] == 12:
        return
    rw = dict(reads=['sc1', 'sc2', 'sc3', 'sc4', 'sc5', 'sc6'] + tk, writes=['sc1', 'sc2', 'sc3', 'sc4'])
    p.tt(S(3), S(5), tab(W['ep_re']), ALU.mult, **rw)
    p.tt(S(4), S(6), tab(W['ep_im']), ALU.mult, **rw)
    p.tt(S(1), S(3), S(4), ALU.subtract, **rw)
    p.tt(S(3), S(6), tab(W['ep_re']), ALU.mult, **rw)
    p.tt(S(4), S(5), tab(W['ep_im']), ALU.mult, **rw)
    p.tt(S(2), S(3), S(4), ALU.add, **rw)
    if prompt:
        p.cp(W['hp_re'][:, :], S(1)[:, :, n - 1], reads=['sc1'], writes=[nm('hp')])
        p.cp(W['hp_im'][:, :], S(2)[:, :, n - 1], reads=['sc2'], writes=[nm('hp')])
        if g['last'] and tidx == T // 128 - 1:
            for hp, dst in (('hp_re', 'p_s5_re'), ('hp_im', 'p_s5_im')):
                p.dma(k.O[dst][l].rearrange("(j q) -> q j", q=128), W[hp][:, :], reads=[nm('hp')],
                      is_out=True, allow_slow_non_contiguous=True)
    else:
        for hi, dst in ((1, 's_s5_re'), (2, 's_s5_im')):
            for j in range(8):
                p.tr(k.pp[0][:n, j * 128:(j + 1) * 128], sc[hi][:, j * n:(j + 1) * n], k.ident[:, :],
                     reads=['sc%d' % hi, 'ident'], writes=['b0', 'b1'] if j >= 4 else ['b0'])
            p.cp(sc[7][:n, :], k.pp[0][:n, :], reads=['b0', 'b1'], writes=['sc7'], eng='scalar')
            p.dma(k.O[dst][l], sc[7][:n, :], reads=['sc7'], is_out=True)
    yield
    if prompt and STOP[0] == 13:
        return
    for j in range(8):
        cc0 = ((2 * j) % 8) * 16
        p.mm(k.bank(6)[:n, 32 * j:32 * j + 32], S(1)[:, j, :], W['cre'][:, j // 4, cc0:cc0 + 32], start=True, stop=False,
             reads=['sc1', nm('cre')], writes=['b6'])
        p.mm(k.bank(6)[:n, 32 * j:32 * j + 32], S(2)[:, j, :], W['ncim'][:, j // 4, cc0:cc0 + 32], start=False, stop=True,
             reads=['sc2', nm('cim')], writes=['b6'])
    if prompt and STOP[0] == 14:
        return
    bcs = W['bc']
    rw = dict(reads=['b6', pk, 'bc', 'sc3', 'sc4', 'sc5'], writes=['sc3', 'sc4', 'sc5'])
    ys, t1, t2 = sc[3][:n, 0:256], sc[4][:n, 0:256], sc[5][:n, 0:256]
    p.tt(ys, Pt[:n, 0:256], bcs[:n, BO['s5_d']:BO['s5_d'] + 256], ALU.mult, **rw)
    p.tt(ys, ys, k.bank(6)[:n, 0:256], ALU.add, **rw)
    z = sc[3][:n, 256:512]
    gelu(k, z, ys, t1, t2, rw)
    yield
    for c in range(2):
        p.tr(k.bank(7)[:, c * n:(c + 1) * n], sc[3][:n, 256 + c * 128:256 + (c + 1) * 128], k.ident[:n, :n],
             reads=['sc3', 'ident'], writes=['b7'])
    p.cp(sc[4][:, 0:2 * n], k.bank(7)[:, 0:2 * n], reads=['b7'], writes=['sc4'], eng='scalar')
    zT = sc[4][:, 0:2 * n].rearrange("p (c i) -> p c i", i=n)
    for c in range(2):
        p.mm(k.bank(6)[:n, 0:256], zT[:, c, :], W['wglu'][:, c, :], start=(c == 0), stop=(c == 1),
             reads=['sc4', nm('wglu')], writes=['b6'])
    p.act(t2, k.bank(6)[:n, 0:256], AF.Sigmoid, reads=['b6'], writes=['sc5'])
    p.tt(Yt[:n, 0:256], z, t2, ALU.mult, reads=['sc3', 'sc5'], writes=[yk])


def head_post(k, o_ap, o_reads, n, gvec, gvkey, gate, gate_reads, out_ap, out_key, layernorm=False,
              extra=None, extra_reads=()):
    p = k.p
    sm = k.sm
    s7 = k.sc[7]
    h3 = lambda ap: ap.rearrange("p (h d) -> p h d", d=64)
    yv = s7[:n, 256:512]
    rw = dict(reads=list(o_reads) + ['sc7', 'sm'], writes=['sc7', 'sm'])
    if layernorm:
        p.red(sm[:n, 4:8], h3(o_ap), ALU.add, **rw)
        p.ts(sm[:n, 4:8], sm[:n, 4:8], -1.0 / 64, None, ALU.mult, **rw)
        p.tt(h3(yv), h3(o_ap), sm[:n, 4:8].unsqueeze(2).to_broadcast([n, 4, 64]), ALU.add, **rw)
        src = yv
        eps = GN_EPS
    else:
        src = o_ap
        eps = EPS
    p.act(s7[:n, 0:256], src, AF.Square, **rw)
    p.red(sm[:n, 8:12], h3(s7[:n, 0:256]), ALU.add, **rw)
    p.ts(sm[:n, 8:12], sm[:n, 8:12], 1.0 / 64, eps, ALU.mult, ALU.add, **rw)
    p.tt(sm[:n, 12:16], sm[:n, 8:12], k.mhalf[:n, 0:4], ALU.pow, reads=rw['reads'] + ['mhalf'], writes=rw['writes'], eng='gpsimd')
    p.tt(h3(yv), h3(src), sm[:n, 12:16].unsqueeze(2).to_broadcast([n, 4, 64]), ALU.mult, **rw)
    p.tt(yv, yv, gvec, ALU.mult, reads=['sc7', gvkey], writes=['sc7'])
    if extra is not None:
        p.tt(yv, yv, extra, ALU.add, reads=['sc7'] + list(extra_reads), writes=['sc7'])
    p.tt(out_ap, yv, gate, ALU.mult, reads=['sc7'] + list(gate_reads), writes=[out_key])


def gla_pre(k, l, ti, n, prompt):
    p, W = k.p, k.W[l]
    nm = lambda s: '%s_l%d' % (s, l)
    Pt, pk, sc = k.P[ti], pkeys(ti, 1288, 2328), k.sc
    p.tr(k.bank(2)[0:16, 0:n], Pt[:n, OFF['ga']:OFF['ga'] + 16], k.ident[:n, :n], reads=[pk, 'ident'], writes=['b2'])
    p.cp(sc[0][0:16, 0:n], k.bank(2)[0:16, 0:n], reads=['b2'], writes=['sc0'], eng='scalar')
    for c in range(2):
        p.mm(k.bank(3)[:, c * n:(c + 1) * n], W['walpha'][0:16, c * 128:(c + 1) * 128], sc[0][0:16, 0:n],
             reads=['sc0', nm('walpha')], writes=['b3'])
    for c in range(2):
        p.act(sc[1][:, c * n:(c + 1) * n], k.bank(3)[:, c * n:(c + 1) * n], AF.Exp, scale=-1.0,
              bias=W['nbalpha'][:, c:c + 1], reads=['b3', nm('nbalpha')], writes=['sc1'])
    p.act(sc[1][:, 0:2 * n], sc[1][:, 0:2 * n], AF.Ln, bias=k.onec[:], reads=['sc1', 'onec'], writes=['sc1'])
    if prompt:
        for c in range(2):
            p.add('vector', lambda e, c=c: e.tensor_tensor_scan(
                sc[2][:, c * n:(c + 1) * n], k.ones[:, 0:n], sc[1][:, c * n:(c + 1) * n], 0.0, ALU.mult, ALU.add),
                reads=['sc1', 'ones'], writes=['sc2'])
    else:
        p.cp(sc[2][:, 0:2 * n], sc[1][:, 0:2 * n], reads=['sc1'], writes=['sc2'])


def gla_prompt(k, l, g, ti, n, tidx):
    p, W = k.p, k.W[l]
    nm = lambda s: '%s_l%d' % (s, l)
    Pt, pk, sc = k.P[ti], pkeys(ti, 1288, 2328), k.sc
    gla_pre(k, l, ti, n, True)
    cum = sc[2][:, 0:2 * n]
    cl = cum.rearrange("p (c i) -> p c i", i=n)[:, :, n - 1]
    gs = k.gsm
    p.ts(gs[:, 0:2], cl, -1.0 / 16, None, ALU.mult, reads=['sc2'], writes=['gsm'])
    p.act(gs[:, 2:4], cl, AF.Exp, scale=-1.0 / 16, reads=['sc2'], writes=['gsm'])
    p.act(sc[3][:, 0:2 * n], cum, AF.Exp, scale=-1.0 / 16, reads=['sc2'], writes=['sc3'])
    p.act(sc[4][:, 0:2 * n], cum, AF.Exp, scale=1.0 / 16, reads=['sc2'], writes=['sc4'])
    for c in range(2):
        p.act(sc[5][:, c * n:(c + 1) * n], sc[2][:, c * n:(c + 1) * n], AF.Exp, scale=1.0 / 16,
              bias=gs[:, c:c + 1], reads=['sc2', 'gsm'], writes=['sc5'])
    for c in range(2):
        p.tr(k.bank(2)[:, c * n:(c + 1) * n], Pt[:n, OFF['gq'] + c * 128:OFF['gq'] + (c + 1) * 128],
             k.ident[:n, :n], reads=[pk, 'ident'], writes=['b2'])
        p.tr(k.bank(2)[:, (2 + c) * n:(3 + c) * n], Pt[:n, OFF['gk'] + c * 128:OFF['gk'] + (c + 1) * 128],
             k.ident[:n, :n], reads=[pk, 'ident'], writes=['b2'])
    for h in range(4):
        pr, hl = h // 2, h % 2
        p.stt(k.mqT[:, h, :n], k.bank(2)[:, pr * n:(pr + 1) * n], k.hmask8[:, hl:hl + 1],
              sc[3][:, pr * n:(pr + 1) * n], ALU.mult, ALU.mult, reads=['b2', 'hmask8', 'sc3'], writes=['mqT'])
        p.stt(k.mkT[:, h, :n], k.bank(2)[:, (2 + pr) * n:(3 + pr) * n], k.hmask[:, hl:hl + 1],
              sc[4][:, pr * n:(pr + 1) * n], ALU.mult, ALU.mult, reads=['b2', 'hmask', 'sc4'], writes=['mkT'])
    p.tt(sc[6][:, 0:2 * n], k.bank(2)[:, 2 * n:4 * n], sc[5][:, 0:2 * n], ALU.mult, reads=['b2', 'sc5'], writes=['sc6'])
    for h in range(4):
        p.mm(k.bank(3)[:n, h * n:(h + 1) * n], k.mkT[:, h, :n], k.mqT[:, h, :n], reads=['mkT', 'mqT'], writes=['b3'])
    p.tt(k.mAT[:n, :, :n], v3(k.bank(3), n, 4)[:n], k.triu[:n, :n].unsqueeze(1).to_broadcast([n, 4, n]), ALU.mult,
         reads=['b3', 'triu'], writes=['mAT'])
    p.cp(k.mV[:n, 0:256], Pt[:n, OFF['gv']:OFF['gv'] + 256], reads=[pk], writes=['mV'], eng='scalar')
    for h in range(4):
        p.mm(k.bank(4)[:n, h * 64:(h + 1) * 64], k.mAT[:n, h, :n], k.mV[:n, h * 64:(h + 1) * 64], start=True, stop=False,
             reads=['mAT', 'mV'], writes=['b4'])
        p.mm(k.bank(4)[:n, h * 64:(h + 1) * 64], k.mqT[:, h, :n], W['glaSb'][:, h // 2, :], start=False, stop=True,
             reads=['mqT', nm('glaS')], writes=['b4'])
    p.cp(sc[0][:n, 256:512], k.bank(4)[:n, 0:256], reads=['b4'], writes=['sc0'], eng='scalar')
    for c in range(2):
        p.tr(k.bank(5)[:n, c * 128:(c + 1) * 128], sc[6][:, c * n:(c + 1) * n], k.ident[:, :],
             reads=['sc6', 'ident'], writes=['b5'])
    p.cp(k.mKh[:n, 0:256], k.bank(5)[:n, 0:256], reads=['b5'], writes=['mKh'])
    for pr in range(2):
        p.mm(k.bank(5)[:, pr * 256:(pr + 1) * 256], k.mKh[:n, pr * 128:(pr + 1) * 128], k.mV[:n, 0:256],
             reads=['mKh', 'mV'], writes=['b5'])
    S = W['glaS']
    for pr in range(2):
        for hl in range(2):
            r0 = hl * 64
            c0 = pr * 256 + (2 * pr + hl) * 64
            p.stt(S[r0:r0 + 64, pr, :], S[r0:r0 + 64, pr, :], gs[r0:r0 + 64, 2 + pr:3 + pr], k.bank(5)[r0:r0 + 64, c0:c0 + 64],
                  ALU.mult, ALU.add, reads=['b5', 'gsm', nm('glaS')], writes=[nm('glaS')])
    p.cp(W['glaSb'][:], S[:], reads=[nm('glaS')], writes=[nm('glaS')])
    if g['last'] and tidx == T // 128 - 1:
        for pr in range(2):
            p.dma(k.O['p_gla_S'][l].rearrange("(pr q v) -> pr q v", pr=2, v=64)[pr], S[:, pr, :],
                  reads=[nm('glaS')], is_out=True)
    head_post(k, sc[0][:n, 256:512], ['sc0'], n, W['bc'][:n, BO['gla_norm']:BO['gla_norm'] + 256], 'bc',
              k.Y[ti][:n, 512:768], ['Y%d' % ti], k.Y[ti][:n, 512:768], 'Y%d' % ti)


def rep_gather(k, name, src_ap, src_reads, dst, width=64):
    p = k.p
    scr = k.scr[name]
    p.dma(scr, src_ap, reads=src_reads, writes=['scr_' + name])
    if width == 64:
        for vh in range(2):
            p.dma(dst[vh:128:2, :], scr.rearrange("i (h d) -> (i h) d", d=64), reads=['scr_' + name], writes=['rep'])
    else:
        p.dma(dst, scr.rearrange("i (q w) -> (i q) w", w=32), reads=['scr_' + name], writes=['rep'])


def rep_scatter(k, src, dst_sb, dst_key):
    p = k.p
    scr = k.scr['o']
    p.dma(scr.rearrange("i (q w) -> (i q) w", w=32), src, reads=['rep', 'repS'], writes=['scr_o'])
    p.dma(dst_sb, scr, reads=['scr_o'], writes=[dst_key])


def gla_sample(k, l, ti, n):
    p, W = k.p, k.W[l]
    nm = lambda s: '%s_l%d' % (s, l)
    Pt, pk, sc = k.P[ti], pkeys(ti, 1288, 2328), k.sc
    gla_pre(k, l, ti, n, False)
    p.act(sc[3][:, 0:2 * n], sc[2][:, 0:2 * n], AF.Exp, scale=-1.0 / 16, reads=['sc2'], writes=['sc3'])
    for c in range(2):
        p.tr(k.bank(2)[:n, c * 128:(c + 1) * 128], sc[3][:, c * n:(c + 1) * n], k.ident[:, :], reads=['sc3', 'ident'],
             writes=['b2'])
    p.cp(sc[4][:n, 0:256], k.bank(2)[:n, 0:256], reads=['b2'], writes=['sc4'], eng='scalar')
    rep = k.rep
    rep_gather(k, 'q', Pt[:n, OFF['gq']:OFF['gq'] + 256], [pk], rep[:, 0, :])
    rep_gather(k, 'k', Pt[:n, OFF['gk']:OFF['gk'] + 256], [pk], rep[:, 1, :])
    rep_gather(k, 'a', sc[4][:n, 0:256], ['sc4'], rep[:, 2, :])
    rep_gather(k, 'v', Pt[:n, OFF['gv']:OFF['gv'] + 256], [pk], rep[:, 3, 0:32], width=32)
    S = k.repS
    S3 = S[:].rearrange("p (d v) -> p d v", v=32)
    src = k.I['st_gla_S'][l].rearrange("i (h d vh v) -> i h vh d v", h=4, d=64, vh=2, v=32)
    for h in range(4):
        for vh in range(2):
            q = h * 2 + vh
            p.dma(S[q:128:8, :].rearrange("p (d v) -> p d v", v=32), src[:, h, vh, :, :], writes=['repS'])
    A = k.scd[0][:].rearrange("p (d v) -> p d v", v=32)
    B = k.scd[1][:].rearrange("p (d v) -> p d v", v=32)
    bd = lambda ap: ap.unsqueeze(2).to_broadcast([128, 64, 32])
    bv = lambda ap: ap.unsqueeze(1).to_broadcast([128, 64, 32])
    rw = dict(reads=['rep', 'repS', 'sc0', 'sc1', 'sc2', 'sc3'], writes=['sc0', 'sc1', 'sc2', 'sc3'])
    p.tt(A, S3, bd(rep[:, 2, :]), ALU.mult, **rw)
    p.tt(B, bd(rep[:, 1, :]), bv(rep[:, 3, 0:32]), ALU.mult, **rw)
    p.tt(S3, A, B, ALU.add, reads=['sc0', 'sc1', 'sc2', 'sc3'], writes=['repS'])
    dst = k.O['s_gla_S'][l].rearrange("i (h d vh v) -> i h vh d v", h=4, d=64, vh=2, v=32)
    for h in range(4):
        for vh in range(2):
            q = h * 2 + vh
            p.dma(dst[:, h, vh, :, :], S[q:128:8, :].rearrange("p (d v) -> p d v", v=32), reads=['repS'], is_out=True)
    p.tt(A, S3, bd(rep[:, 0, :]), ALU.mult, reads=['rep', 'repS'], writes=['sc0', 'sc1'])
    p.red(rep[:, 4, 0:32], k.scd[0][:].rearrange("p (d v) -> p v d", v=32), ALU.add, reads=['sc0', 'sc1'], writes=['rep'])
    p.ts(rep[:, 4, 0:32], rep[:, 4, 0:32], 0.125, None, ALU.mult, reads=['rep'], writes=['rep'])
    rep_scatter(k, rep[:, 4, 0:32], sc[0][:n, 256:512], 'sc0')
    head_post(k, sc[0][:n, 256:512], ['sc0'], n, W['bc'][:n, BO['gla_norm']:BO['gla_norm'] + 256], 'bc',
              k.Y[ti][:n, 512:768], ['Y%d' % ti], k.Y[ti][:n, 512:768], 'Y%d' % ti)


def ml_gates(k, l, ti, n):
    p, W = k.p, k.W[l]
    nm = lambda s: '%s_l%d' % (s, l)
    Pt, pk = k.P[ti], pkeys(ti, 256, 1288)
    m = k.msm
    bcs = W['bc']
    rw = dict(reads=[pk, 'bc', 'msm', 'onec'], writes=['msm'])
    p.tt(m[:n, 0:4], Pt[:n, OFF['mi']:OFF['mi'] + 4], bcs[:n, 3200:3204], ALU.add, **rw)
    p.tt(m[:n, 4:8], Pt[:n, OFF['mf']:OFF['mf'] + 4], bcs[:n, 3204:3208], ALU.add, **rw)
    p.act(m[:n, 4:8], m[:n, 4:8], AF.Exp, scale=-1.0, **rw)
    p.act(m[:n, 4:8], m[:n, 4:8], AF.Ln, bias=k.onec[:n, :], **rw)


def ml_prompt(k, l, g, ti, n, tidx):
    p, W = k.p, k.W[l]
    nm = lambda s: '%s_l%d' % (s, l)
    Pt, pk, sc = k.P[ti], pkeys(ti, 256, 1288), k.sc
    m, m2, m3 = k.msm, k.msm2, k.msm3
    ml_gates(k, l, ti, n)
    rw = dict(reads=['msm', 'msm2', 'msm3', 'b2', nm('mprev')], writes=['msm', 'msm2', 'msm3'])
    p.mm(k.bank(2)[:n, 0:4], k.triu[:n, :n], m[:n, 4:8], reads=['triu', 'msm'], writes=['b2'])
    p.ts(m[:n, 8:12], k.bank(2)[:n, 0:4], -1.0, None, ALU.mult, **rw)
    p.cp(m[:n, 28:32], m[:n, 8:12], **rw)
    p.tt(m[:n, 12:16], m[:n, 0:4], m[:n, 8:12], ALU.subtract, **rw)
    for h in range(4):
        p.ts(sc[1][:n, h * 128:(h + 1) * 128], k.ident[:n, :n], m[:n, 12 + h:13 + h], None, ALU.mult,
             reads=['ident', 'msm'], writes=['sc1'])
    for h in range(4):
        p.mm(k.bank(2)[:n, h * 128:(h + 1) * 128], k.ones[:n, :n], sc[1][:n, h * 128:(h + 1) * 128],
             reads=['ones', 'sc1'], writes=['b2'])
    for h in range(4):
        p.stt(sc[2][:n, h * 128:(h + 1) * 128], k.bank(2)[:n, h * 128:(h + 1) * 128], m[:n, 8 + h:9 + h],
              k.maskadd[:n, :n], ALU.add, ALU.add, reads=['b2', 'msm', 'maskadd'], writes=['sc2'])
    p.red(m[:n, 16:20], sc[2][:n, 0:512].rearrange("p (h s) -> p h s", s=128), ALU.max, reads=['sc2', 'msm'], writes=['msm'])
    p.tt(m[:n, 20:24], m[:n, 8:12], W['mprev'][:n, :], ALU.add, **rw)
    p.tt(m[:n, 24:28], m[:n, 20:24], m[:n, 16:20], ALU.max, **rw)
    p.ts(m2[:n, 20:24], m[:n, 24:28], -1.0, None, ALU.mult, **rw)
    for h in range(4):
        p.act(sc[3][:n, h * 128:(h + 1) * 128], sc[2][:n, h * 128:(h + 1) * 128], AF.Exp, bias=m2[:n, 20 + h:21 + h],
              reads=['sc2', 'msm2'], writes=['sc3'])
    p.tt(m2[:n, 0:4], m[:n, 20:24], m2[:n, 20:24], ALU.add, **rw)
    p.act(m2[:n, 0:4], m2[:n, 0:4], AF.Exp, **rw)
    p.act(m2[:n, 12:16], m2[:n, 20:24], AF.Exp, **rw)
    for c in range(2):
        p.tr(k.bank(3)[:, c * n:(c + 1) * n], Pt[:n, OFF['mq'] + c * 128:OFF['mq'] + (c + 1) * 128],
             k.ident[:n, :n], reads=[pk, 'ident'], writes=['b3'])
        p.tr(k.bank(3)[:, (2 + c) * n:(3 + c) * n], Pt[:n, OFF['mk'] + c * 128:OFF['mk'] + (c + 1) * 128],
             k.ident[:n, :n], reads=[pk, 'ident'], writes=['b3'])
    for h in range(4):
        pr, hl = h // 2, h % 2
        p.ts(k.mqT[:, h, :n], k.bank(3)[:, pr * n:(pr + 1) * n], k.hmask[:, hl:hl + 1], None, ALU.mult,
             reads=['b3', 'hmask'], writes=['mqT'])
        p.ts(k.mkT[:, h, :n], k.bank(3)[:, (2 + pr) * n:(3 + pr) * n], k.hmask8[:, hl:hl + 1], None, ALU.mult,
             reads=['b3', 'hmask8'], writes=['mkT'])
    for h in range(4):
        p.mm(k.bank(4)[:n, h * n:(h + 1) * n], k.mqT[:, h, :n], k.mkT[:, h, :n], reads=['mqT', 'mkT'], writes=['b4'])
    p.tt(sc[4][:n, 0:512], sc[3][:n, 0:512], k.bank(4)[:n, 0:512], ALU.mult, reads=['sc3', 'b4'], writes=['sc4'])
    for h in range(4):
        p.tr(k.bank(5)[:n, h * n:(h + 1) * n], sc[4][:n, h * 128:(h + 1) * 128], k.ident[:n, :n],
             reads=['sc4', 'ident'], writes=['b5'])
    p.cp(k.mAT[:n, :, :n], v3(k.bank(5), n, 4)[:n], reads=['b5'], writes=['mAT'], eng='scalar')
    p.cp(k.mVa[:n, :, 0:64], Pt[:n, OFF['mv']:OFF['mv'] + 256].rearrange("p (h d) -> p h d", d=64), reads=[pk],
         writes=['mVa'], eng='scalar')
    for h in range(4):
        p.mm(k.bank(6)[:n, h * 65:(h + 1) * 65], k.mAT[:n, h, :n], k.mVa[:n, h, :], reads=['mAT', 'mVa'], writes=['b6'])
        p.mm(k.bank(7)[:n, h * 65:(h + 1) * 65], k.mqT[:, h, :n], W['mlCTb'][:, h // 2, :], reads=['mqT', nm('mlCT')],
             writes=['b7'])
    nd = sc[5][:n, 0:260].rearrange("p (h e) -> p h e", e=65)
    p.cp(sc[5][:n, 0:260], k.bank(6)[:n, 0:260], reads=['b6'], writes=['sc5'], eng='scalar')
    p.tt(sc[6][:n, 0:260].rearrange("p (h e) -> p h e", e=65), k.bank(7)[:n, 0:260].rearrange("p (h e) -> p h e", e=65),
         m2[:n, 0:4].unsqueeze(2).to_broadcast([n, 4, 65]), ALU.mult, reads=['b7', 'msm2'], writes=['sc6'])
    p.tt(sc[5][:n, 0:260], sc[5][:n, 0:260], sc[6][:n, 0:260], ALU.add, reads=['sc5', 'sc6'], writes=['sc5'])
    p.ts(m2[:n, 4:8], nd[:, :, 64], -1.0, None, ALU.mult, reads=['sc5', 'msm2'], writes=['msm2'])
    p.tt(m2[:n, 8:12], nd[:, :, 64], m2[:n, 4:8], ALU.max, reads=['sc5', 'msm2'], writes=['msm2'])
    p.tt(m2[:n, 8:12], m2[:n, 8:12], m2[:n, 12:16], ALU.max, **rw)
    p.add('vector', lambda e: e.reciprocal(m2[:n, 16:20], m2[:n, 8:12]), **rw)
    p.tt(sc[0][:n, 256:512].rearrange("p (h d) -> p h d", d=64), nd[:, :, 0:64],
         m2[:n, 16:20].unsqueeze(2).to_broadcast([n, 4, 64]), ALU.mult, reads=['sc5', 'msm2'], writes=['sc0'])
    p.mm(k.bank(2)[:, 0:8], k.sellast[:n, :], m[:n, 24:32], reads=['sellast', 'msm'], writes=['b2'])
    p.cp(m3[:, 0:8], k.bank(2)[:, 0:8], **rw)
    p.tt(m3[:, 12:16], m3[:, 4:8], m3[:, 0:4], ALU.subtract, **rw)
    p.tt(m3[:, 8:12], m3[:, 12:16], W['mprev'][:, :], ALU.add, **rw)
    p.act(m3[:, 8:12], m3[:, 8:12], AF.Exp, **rw)
    p.tt(m2[:n, 24:28], m[:n, 12:16], m3[:n, 12:16], ALU.add, **rw)
    p.act(m2[:n, 24:28], m2[:n, 24:28], AF.Exp, **rw)
    p.cp(W['mprev'][:, :], m3[:, 0:4], reads=['msm3'], writes=[nm('mprev')])
    p.stt(k.mKh[:n, 0:256].rearrange("p (h d) -> p h d", d=64),
          Pt[:n, OFF['mk']:OFF['mk'] + 256].rearrange("p (h d) -> p h d", d=64), 0.125,
          m2[:n, 24:28].unsqueeze(2).to_broadcast([n, 4, 64]), ALU.mult, ALU.mult, reads=[pk, 'msm2'], writes=['mKh'])
    CT = W['mlCT']
    for pr in range(2):
        p.mm(k.bank(2 + pr)[:, 0:260], k.mKh[:n, pr * 128:(pr + 1) * 128], k.mVa[:n].rearrange("p h e -> p (h e)"),
             reads=['mKh', 'mVa'], writes=['b%d' % (2 + pr)])
        for hl in range(2):
            r0 = hl * 64
            h = 2 * pr + hl
            p.stt(CT[r0:r0 + 64, pr, :], CT[r0:r0 + 64, pr, :], m3[r0:r0 + 64, 8 + h:9 + h],
                  k.bank(2 + pr)[r0:r0 + 64, h * 65:(h + 1) * 65], ALU.mult, ALU.add,
                  reads=['b%d' % (2 + pr), 'msm3', nm('mlCT')], writes=[nm('mlCT')])
    p.cp(W['mlCTb'][:], CT[:], reads=[nm('mlCT')], writes=[nm('mlCT')])
    if g['last'] and tidx == T // 128 - 1:
        for pr in range(2):
            p.tr(k.bank(4)[0:64, pr * 128:(pr + 1) * 128], CT[:, pr, 0:64], k.ident[:, :], reads=[nm('mlCT'), 'ident'],
                 writes=['b4'])
        p.cp(sc[1][0:64, 0:256], k.bank(4)[0:64, 0:256], reads=['b4'], writes=['sc1'])
        p.dma(k.O['p_ml_C'][l].rearrange("(h v d) -> v h d", h=4, v=64), sc[1][0:64, 0:256].rearrange("p (h d) -> p h d", d=64),
              reads=['sc1'], is_out=True)
        p.dma(k.O['p_ml_n'][l].rearrange("(pr q) -> q pr", q=128), CT[:, :, 64], reads=[nm('mlCT')], is_out=True,
              allow_slow_non_contiguous=True)
        p.dma(k.O['p_ml_m'][l:l + 1, :], W['mprev'][0:1, :], reads=[nm('mprev')], is_out=True)
    head_post(k, sc[0][:n, 256:512], ['sc0'], n, W['bc'][:n, BO['ml_norm']:BO['ml_norm'] + 256], 'bc',
              k.Y[ti][:n, 256:512], ['Y%d' % ti], k.Y[ti][:n, 256:512], 'Y%d' % ti)


def ml_sample(k, l, ti, n):
    p, W = k.p, k.W[l]
    nm = lambda s: '%s_l%d' % (s, l)
    Pt, pk, sc = k.P[ti], pkeys(ti, 256, 1288), k.sc
    m, m2, m3 = k.msm, k.msm2, k.msm3
    ml_gates(k, l, ti, n)
    rw = dict(reads=['msm', 'msm2', 'msm3'], writes=['msm', 'msm2', 'msm3'])
    p.dma(m3[:n, 0:4], k.I['st_ml_m'][l], writes=['msm3'])
    p.tt(m[:n, 8:12], m3[:n, 0:4], m[:n, 4:8], ALU.subtract, **rw)
    p.tt(m[:n, 12:16], m[:n, 8:12], m[:n, 0:4], ALU.max, **rw)
    p.dma(k.O['s_ml_m'][l], m[:n, 12:16], reads=['msm'], is_out=True)
    pk4 = m2[:n, 0:16].rearrange("p (h w) -> p h w", w=4)
    p.tt(pk4[:, :, 0], m[:n, 8:12], m[:n, 12:16], ALU.subtract, **rw)
    p.tt(pk4[:, :, 1], m[:n, 0:4], m[:n, 12:16], ALU.subtract, **rw)
    p.ts(pk4[:, :, 2], m[:n, 12:16], -1.0, None, ALU.mult, **rw)
    p.act(m2[:n, 0:16], m2[:n, 0:16], AF.Exp, **rw)
    rep = k.rep
    p.dma(k.scr['s16'], m2[:n, 0:16], reads=['msm2'], writes=['scr_s16'])
    for vh in range(2):
        p.dma(rep[vh:128:2, 5, 0:4], k.scr['s16'].rearrange("i (h w) -> (i h) w", w=4), reads=['scr_s16'], writes=['rep'])
        p.dma(rep[vh:128:2, 2, :], k.I['st_ml_n'][l].rearrange("i (h d) -> (i h) d", d=64), writes=['rep'])
    rep_gather(k, 'q', Pt[:n, OFF['mq']:OFF['mq'] + 256], [pk], rep[:, 0, :])
    rep_gather(k, 'k', Pt[:n, OFF['mk']:OFF['mk'] + 256], [pk], rep[:, 1, :])
    rep_gather(k, 'v', Pt[:n, OFF['mv']:OFF['mv'] + 256], [pk], rep[:, 3, 0:32], width=32)
    C = k.repS
    C3 = C[:].rearrange("p (v d) -> p v d", d=64)
    p.dma(C[:], k.I['st_ml_C'][l].rearrange("i (q f) -> (i q) f", f=2048), writes=['repS'])
    A = k.scd[0][:].rearrange("p (v d) -> p v d", d=64)
    bq = lambda ap: ap.unsqueeze(1).to_broadcast([128, 32, 64])
    bvv = lambda ap: ap.unsqueeze(2).to_broadcast([128, 32, 64])
    w0c, wc, enc = rep[:, 5, 0:1], rep[:, 5, 1:2], rep[:, 5, 2:3]
    rr = dict(reads=['rep', 'repS', 'sc0', 'sc1', 'sc2', 'sc3'], writes=['rep', 'sc0', 'sc1', 'sc2', 'sc3'])
    t6 = rep[:, 6, :]
    sA = rep[:, 7, :]
    p.tt(t6, rep[:, 0, :], rep[:, 1, :], ALU.mult, **rr)
    p.red(sA[:, 0:1], t6, ALU.add, **rr)
    p.tt(t6, rep[:, 0, :], rep[:, 2, :], ALU.mult, **rr)
    p.red(sA[:, 1:2], t6, ALU.add, **rr)
    p.stt(sA[:, 2:3], sA[:, 0:1], 0.125, wc, ALU.mult, ALU.mult, **rr)
    p.stt(sA[:, 3:4], sA[:, 1:2], w0c, sA[:, 2:3], ALU.mult, ALU.add, **rr)
    p.ts(sA[:, 4:5], sA[:, 3:4], -1.0, None, ALU.mult, **rr)
    p.tt(sA[:, 5:6], sA[:, 3:4], sA[:, 4:5], ALU.max, **rr)
    p.tt(sA[:, 5:6], sA[:, 5:6], enc, ALU.max, **rr)
    p.add('vector', lambda e: e.reciprocal(sA[:, 5:6], sA[:, 5:6]), **rr)
    p.tt(A, C3, bq(rep[:, 0, :]), ALU.mult, **rr)
    p.red(rep[:, 4, 0:32], A, ALU.add, **rr)
    p.ts(rep[:, 4, 0:32], rep[:, 4, 0:32], w0c, None, ALU.mult, **rr)
    p.stt(rep[:, 4, 0:32], rep[:, 3, 0:32], sA[:, 2:3], rep[:, 4, 0:32], ALU.mult, ALU.add, **rr)
    p.ts(rep[:, 4, 0:32], rep[:, 4, 0:32], sA[:, 5:6], None, ALU.mult, **rr)
    rep_scatter(k, rep[:, 4, 0:32], sc[6][:n, 256:512], 'sc6')
    p.ts(t6, rep[:, 1, :], wc, 0.125, ALU.mult, ALU.mult, **rr)
    p.tt(A, bvv(rep[:, 3, 0:32]), bq(t6), ALU.mult, **rr)
    p.stt(C3, C3, w0c, A, ALU.mult, ALU.add, reads=['rep', 'repS', 'sc0', 'sc1'], writes=['repS'])
    p.dma(k.O['s_ml_C'][l].rearrange("i (q f) -> (i q) f", f=2048), C[:], reads=['repS'], is_out=True)
    p.stt(rep[:, 2, :], rep[:, 2, :], w0c, t6, ALU.mult, ALU.add, **rr)
    p.dma(k.O['s_ml_n'][l].rearrange("i (h d) -> (i h) d", d=64), rep[0:128:2, 2, :], reads=['rep'], is_out=True)
    head_post(k, sc[6][:n, 256:512], ['sc6'], n, W['bc'][:n, BO['ml_norm']:BO['ml_norm'] + 256], 'bc',
              k.Y[ti][:n, 256:512], ['Y%d' % ti], k.Y[ti][:n, 256:512], 'Y%d' % ti)


RWS = 0.6065306597126334


def rw_pre(k, l, g, ti, n, prompt):
    p, W = k.p, k.W[l]
    nm = lambda s: '%s_l%d' % (s, l)
    Pt, pk, sc = k.P[ti], pkeys(ti, 2328, 3224), k.sc
    bcs = W['bc']
    rc0 = OFF['rc']
    B = lambda name, w=256: bcs[:n, BO[name]:BO[name] + w]
    if prompt:
        for half in range(2):
            c0 = half * 448
            p.mm(k.bank(2 + half)[:n, 0:448], k.shm[:n, :n], Pt[:n, rc0 + c0:rc0 + c0 + 448], start=True, stop=False,
                 reads=[pk, 'shm'], writes=['b%d' % (2 + half)])
            p.mm(k.bank(2 + half)[:n, 0:448], k.shprev[:, :n], W['rcprev'][:, c0:c0 + 448], start=False, stop=True,
                 reads=[nm('rcprev'), 'shprev'], writes=['b%d' % (2 + half)])
            p.tt(sc[0][:n, c0:c0 + 448], k.bank(2 + half)[:n, 0:448], Pt[:n, rc0 + c0:rc0 + c0 + 448], ALU.subtract,
                 reads=['b%d' % (2 + half), pk], writes=['sc0'])
        p.cp(W['rcprev'][:, :], Pt[:, rc0:rc0 + 896], reads=[pk], writes=[nm('rcprev')], eng='gpsimd')
    else:
        p.dma(sc[0][:n, 0:896], k.I['st_rw_shift'][l], writes=['sc0'])
        p.tt(sc[0][:n, 0:896], sc[0][:n, 0:896], Pt[:n, rc0:rc0 + 896], ALU.subtract, reads=['sc0', pk], writes=['sc0'])
    p.tt(sc[0][:n, 0:896], sc[0][:n, 0:896], B('rw_mu', 896), ALU.mult, reads=['sc0', 'bc'], writes=['sc0'])
    p.tt(sc[0][:n, 0:896], sc[0][:n, 0:896], Pt[:n, rc0:rc0 + 896], ALU.add, reads=['sc0', pk], writes=['sc0'])
    rr, rk, rv = sc[0][:n, 0:256], sc[0][:n, 256:512], sc[0][:n, 512:768]
    p.act(sc[1][:n, 0:32], sc[0][:n, 768:800], AF.Tanh, reads=['sc0'], writes=['sc1'])
    p.cp(sc[1][:n, 32:64], sc[0][:n, 800:832], reads=['sc0'], writes=['sc1'])
    p.act(sc[1][:n, 64:128], sc[0][:n, 832:896], AF.Sigmoid, reads=['sc0'], writes=['sc1'])
    p.tr(k.bank(4)[:, 0:n], sc[1][:n, 0:128], k.ident[:n, :n], reads=['sc1', 'ident'], writes=['b4'])
    p.cp(sc[1][:, 128:128 + n], k.bank(4)[:, 0:n], reads=['b4'], writes=['sc1'], eng='scalar')
    T3T = sc[1][:, 128:128 + n]
    p.mm(k.bank(2)[:n, 0:256], T3T, W['rwWw'][:, :], reads=['sc1', nm('rwWw')], writes=['b2'])
    p.mm(k.bank(2)[:n, 256:512], T3T, W['rwWa'][:, :], reads=['sc1', nm('rwWa')], writes=['b2'])
    p.mm(k.bank(3)[:n, 0:256], T3T, W['rwWg'][:, :], reads=['sc1', nm('rwWg')], writes=['b3'])
    r2 = dict(reads=['b2', 'b3', 'sc0', 'sc2', 'sc3', 'sc4', 'sm', 'bc'], writes=['sc2', 'sc3', 'sc4', 'sm'])
    sgw, a_, g_ = sc[2][:n, 0:256], sc[2][:n, 256:512], sc[2][:n, 512:768]
    p.tt(sgw, k.bank(2)[:n, 0:256], B('rw_w0'), ALU.add, **r2)
    p.act(sgw, sgw, AF.Sigmoid, **r2)
    p.tt(a_, k.bank(2)[:n, 256:512], B('rw_a0'), ALU.add, **r2)
    p.act(a_, a_, AF.Sigmoid, **r2)
    p.cp(g_, k.bank(3)[:n, 0:256], **r2)
    kk, kt, be, tmp = sc[3][:n, 0:256], sc[3][:n, 256:512], sc[3][:n, 512:768], sc[3][:n, 768:1024]
    sm = k.sm
    h3 = lambda ap: ap.rearrange("p (h d) -> p h d", d=64)
    p.tt(kk, rk, B('rw_k_k'), ALU.mult, **r2)
    p.tt(tmp, kk, kk, ALU.mult, **r2)
    p.red(sm[:n, 16:20], h3(tmp), ALU.add, **r2)
    p.ts(sm[:n, 16:20], sm[:n, 16:20], 1e-24, None, ALU.max, **r2)
    p.tt(sm[:n, 20:24], sm[:n, 16:20], k.mhalf[:n, 0:4], ALU.pow, reads=r2['reads'] + ['mhalf'], writes=r2['writes'], eng='gpsimd')
    p.tt(h3(kk), h3(kk), sm[:n, 20:24].unsqueeze(2).to_broadcast([n, 4, 64]), ALU.mult, **r2)
    p.stt(tmp, a_, -1.0, B('rw_k_a'), ALU.add, ALU.mult, **r2)
    p.stt(kt, tmp, 1.0, rk, ALU.add, ALU.mult, **r2)
    p.tt(be, a_, kk, ALU.mult, **r2)
    p.tt(tmp, rr, kt, ALU.mult, **r2)
    p.tt(tmp, tmp, B('rw_r_k'), ALU.mult, **r2)
    p.red(sm[:n, 24:28], h3(tmp), ALU.add, **r2)
    p.tt(h3(sc[4][:n, 0:256]), h3(rv), sm[:n, 24:28].unsqueeze(2).to_broadcast([n, 4, 64]), ALU.mult, **r2)
    p.cp(sc[4][:n, 256:512], rv, **r2)


def rw_post(k, l, ti, n, o_ap, o_reads, g_ap, g_reads, bv_ap, bv_reads):
    W = k.W[l]
    nm = lambda s: '%s_l%d' % (s, l)
    head_post(k, o_ap, o_reads, n, W['bc'][:n, BO['rw_norm']:BO['rw_norm'] + 256], 'bc', g_ap, g_reads,
              k.Y[ti][:n, 768:1024], 'Y%d' % ti, layernorm=True, extra=bv_ap, extra_reads=bv_reads)


def rw_prompt(k, l, g, ti, n, tidx):
    p, W = k.p, k.W[l]
    nm = lambda s: '%s_l%d' % (s, l)
    Pt, pk, sc = k.P[ti], pkeys(ti, 2328, 3224), k.sc
    rw_pre(k, l, g, ti, n, True)
    if STOP[0] == 21:
        return
    rr = sc[0][:n, 0:256]
    sgw = sc[2][:n, 0:256]
    kk, kt, be = sc[3][:n, 0:256], sc[3][:n, 256:512], sc[3][:n, 512:768]
    V = sc[4][:n, 256:512]
    gs = k.gsm
    p.mm(k.bank(3)[:n, 256:512], k.triu[:n, :n], sgw, reads=['triu', 'sc2'], writes=['b3'])
    cws = k.bank(3)[:n, 256:512]
    eP, eN, ePm1 = sc[5][:n, 0:256], sc[5][:n, 256:512], sc[5][:n, 512:768]
    p.act(eP, cws, AF.Exp, scale=-RWS, reads=['b3'], writes=['sc5'])
    p.act(eN, cws, AF.Exp, scale=RWS, reads=['b3'], writes=['sc5'])
    p.tt(ePm1, cws, sgw, ALU.subtract, reads=['b3', 'sc2'], writes=['sc5'])
    p.act(ePm1, ePm1, AF.Exp, scale=-RWS, reads=['sc5'], writes=['sc5'])
    p.cp(sc[5][:n, 768:1024], cws, reads=['b3'], writes=['sc5'])
    for pr in range(2):
        p.mm(k.bank(4)[:, 256 + 8 * pr:264 + 8 * pr], sc[5][:n, 768 + pr * 128:768 + (pr + 1) * 128], k.sellast[:n, 0:8],
             reads=['sc5', 'sellast'], writes=['b4'])
    p.act(gs[:, 4:6], k.bank(4)[:, 256:272:8], AF.Exp, scale=-RWS, reads=['b4'], writes=['gsm'])
    r6 = dict(reads=['sc0', 'sc3', 'sc5', 'sc6'], writes=['sc6'])
    p.tt(sc[6][:n, 0:256], kk, ePm1, ALU.mult, **r6)
    p.tt(sc[6][:n, 256:512], be, eN, ALU.mult, **r6)
    p.tt(sc[6][:n, 512:768], kt, eN, ALU.mult, **r6)
    p.tt(sc[6][:n, 768:1024], rr, eP, ALU.mult, **r6)
    if STOP[0] == 22:
        return
    for X in range(4):
        for c in range(2):
            p.tr(k.pp[3][:, (X * 2 + c) * n:(X * 2 + c + 1) * n], sc[6][:n, X * 256 + c * 128:X * 256 + (c + 1) * 128],
                 k.ident[:n, :n], reads=['sc6', 'ident'], writes=['b6', 'b7'])
    if STOP[0] == 27:
        return
    kaTm = k.rwT[:, 0:4, :]
    rTm = k.rwT[:, 4:8, :]
    for h in range(4):
        pr, hl = h // 2, h % 2
        p.ts(kaTm[:, h, :n], k.pp[3][:, pr * n:(pr + 1) * n], k.hmask[:, hl:hl + 1], None, ALU.mult,
             reads=['b6', 'b7', 'hmask'], writes=['rwT'])
        p.ts(rTm[:, h, :n], k.pp[3][:, (6 + pr) * n:(7 + pr) * n], k.hmask[:, hl:hl + 1], None, ALU.mult,
             reads=['b6', 'b7', 'hmask'], writes=['rwT'])
    if STOP[0] == 28:
        return
    beT = k.rwT[:, 8:10, :]
    ktT = k.rwT[:, 10:12, :]
    p.cp(beT, k.pp[3][:, 2 * n:4 * n].rearrange("p (c i) -> p c i", i=128), reads=['b6', 'b7'], writes=['rwT'], eng='scalar')
    p.cp(ktT, k.pp[3][:, 4 * n:6 * n].rearrange("p (c i) -> p c i", i=128), reads=['b6', 'b7'], writes=['rwT'], eng='scalar')
    p.ts(k.rwb2[:n, 3, :], sc[6][:n, 256:512], -1.0, None, ALU.mult, reads=['sc6'], writes=['rwTok'])
    p.cp(k.rwb2[:n, 4, :], sc[6][:n, 512:768], reads=['sc6'], writes=['rwTok'], eng='gpsimd')
    if STOP[0] == 23:
        return
    H4 = lambda ap: ap.rearrange("p (h i) -> p h i", i=128)
    mb = lambda m_: m_[:n, :n].unsqueeze(1).to_broadcast([n, 4, n])
    rb = k.rwb
    LA, LAT, MT, QKT, nQBT, TT, LB, LBT = (rb[:n, i, :] for i in range(8))
    Vb, RHSb, Ub = k.rwb2[:n, 0, :], k.rwb2[:n, 1, :], k.rwb2[:n, 2, :]
    for h in range(4):
        pr = h // 2
        p.mm(k.bank(2)[:n, h * n:(h + 1) * n], kaTm[:, h, :n], beT[:, pr, :n], reads=['rwT'], writes=['b2'])
        p.mm(k.bank(3)[:n, h * n:(h + 1) * n], beT[:, pr, :n], kaTm[:, h, :n], reads=['rwT'], writes=['b3'])
        p.mm(k.bank(4)[:n, h * n:(h + 1) * n], ktT[:, pr, :n], kaTm[:, h, :n], reads=['rwT'], writes=['b4'])
        p.mm(k.bank(5)[:n, h * n:(h + 1) * n], ktT[:, pr, :n], rTm[:, h, :n], reads=['rwT'], writes=['b5'])
    p.stt(H4(LA), H4(k.bank(2)[:n, 0:512]), -1.0, mb(k.masksl), ALU.mult, ALU.mult, reads=['b2', 'masksl'], writes=['rwLA'])
    p.stt(H4(LAT), H4(k.bank(3)[:n, 0:512]), -1.0, mb(k.masksu), ALU.mult, ALU.mult, reads=['b3', 'masksu'], writes=['rwLAT'])
    p.tt(H4(MT), H4(k.bank(4)[:n, 0:512]), mb(k.masksu), ALU.mult, reads=['b4', 'masksu'], writes=['rwMT'])
    p.tt(H4(QKT), H4(k.bank(5)[:n, 0:512]), mb(k.triu), ALU.mult, reads=['b5', 'triu'], writes=['rwQKT'])
    for h in range(4):
        pr = h // 2
        p.mm(k.bank(2)[:n, h * n:(h + 1) * n], beT[:, pr, :n], rTm[:, h, :n], reads=['rwT'], writes=['b2'])
    p.stt(H4(nQBT), H4(k.bank(2)[:n, 0:512]), -1.0, mb(k.triu), ALU.mult, ALU.mult, reads=['b2', 'triu'], writes=['rwQBT'])
    if STOP[0] == 24:
        return
    p.tt(H4(TT), H4(LAT), mb(k.ident), ALU.add, reads=['rwLAT', 'ident'], writes=['rwTT'])
    p.cp(Vb, V, reads=['sc4'], writes=['rwVb'], eng='gpsimd')
    cur = (LA, LAT, 'rwLA', 'rwLAT')
    nxt = (LB, LBT, 'rwLB', 'rwLBT')
    for lvl in range(1, 7):
        cL, cLT, ckL, ckLT = cur
        nL, nLT, nkL, nkLT = nxt
        last = lvl == 6
        for h in range(4):
            sl = slice(h * n, (h + 1) * n)
            p.mm(k.bank(2)[:n, sl], cLT[:, sl], cL[:, sl], reads=[ckL, ckLT], writes=['b2'])
            if not last:
                p.mm(k.bank(3)[:n, sl], cL[:, sl], cLT[:, sl], reads=[ckL, ckLT], writes=['b3'])
        p.cp(nL, k.bank(2)[:n, 0:512], reads=['b2', nkL], writes=[nkL])
        if not last:
            p.cp(nLT, k.bank(3)[:n, 0:512], reads=['b3', nkLT], writes=[nkLT], eng='scalar')
        for h in range(4):
            sl = slice(h * n, (h + 1) * n)
            p.mm(k.bank(4)[:n, sl], nL[:, sl], TT[:, sl], reads=[nkL, 'rwTT'], writes=['b4'])
        p.tt(TT, TT, k.bank(4)[:n, 0:512], ALU.add, reads=['b4', 'rwTT'], writes=['rwTT'])
        cur, nxt = nxt, cur
    if STOP[0] == 25:
        return
    A = W['rwA']
    RHS, U, Osb = sc[1][:n, 768:1024], sc[6][:n, 0:256], sc[6][:n, 768:1024]
    for h in range(4):
        pr = h // 2
        hs = slice(h * 64, (h + 1) * 64)
        p.mm(k.bank(5)[:n, hs], kaTm[:, h, :n], W['rwAb'][:, pr, :], start=True, stop=False, reads=['rwT', nm('rwA')], writes=['b5'])
        p.mm(k.bank(5)[:n, hs], H4(MT)[:, h, :], Vb[:, hs], start=False, stop=True, reads=['rwMT', 'rwVb'], writes=['b5'])
    p.cp(RHSb, k.bank(5)[:n, 0:256], reads=['b5'], writes=['rwRHS'])
    for h in range(4):
        hs = slice(h * 64, (h + 1) * 64)
        p.mm(k.bank(6)[:n, hs], H4(TT)[:, h, :], RHSb[:, hs], reads=['rwTT', 'rwRHS'], writes=['b6'])
    p.cp(U, k.bank(6)[:n, 0:256], reads=['b6'], writes=['sc6'])
    p.cp(Ub, k.bank(6)[:n, 0:256], reads=['b6'], writes=['rwUb'], eng='scalar')
    for h in range(4):
        pr = h // 2
        hs = slice(h * 64, (h + 1) * 64)
        p.mm(k.bank(7)[:n, hs], rTm[:, h, :n], W['rwAb'][:, pr, :], start=True, stop=False, reads=['rwT', nm('rwA')], writes=['b7'])
        p.mm(k.bank(7)[:n, hs], H4(QKT)[:, h, :], Vb[:, hs], start=False, stop=False, reads=['rwQKT', 'rwVb'], writes=['b7'])
        p.mm(k.bank(7)[:n, hs], H4(nQBT)[:, h, :], Ub[:, hs], start=False, stop=True, reads=['rwQBT', 'rwUb'], writes=['b7'])
    p.cp(Osb, k.bank(7)[:n, 0:256], reads=['b7'], writes=['sc6'], eng='scalar')
    if STOP[0] == 26:
        return
    for pr in range(2):
        ps = slice(pr * 128, (pr + 1) * 128)
        p.mm(k.bank(5)[:, 256 + pr * 128:256 + (pr + 1) * 128], k.rwb2[:n, 4, ps], Vb[:, ps],
             start=True, stop=False, reads=['rwTok', 'rwVb'], writes=['b5'])
        p.mm(k.bank(5)[:, 256 + pr * 128:256 + (pr + 1) * 128], k.rwb2[:n, 3, ps], Ub[:, ps],
             start=False, stop=True, reads=['rwTok', 'rwUb'], writes=['b5'])
        for hl in range(2):
            r0 = hl * 64
            c0 = 256 + pr * 128 + hl * 64
            p.tt(A[r0:r0 + 64, pr, :], A[r0:r0 + 64, pr, :], k.bank(5)[r0:r0 + 64, c0:c0 + 64], ALU.add,
                 reads=['b5', nm('rwA')], writes=[nm('rwA')])
            p.ts(A[r0:r0 + 64, pr, :], A[r0:r0 + 64, pr, :], gs[r0:r0 + 64, 4 + pr:5 + pr], None, ALU.mult,
                 reads=['gsm', nm('rwA')], writes=[nm('rwA')])
    p.cp(W['rwAb'][:], A[:], reads=[nm('rwA')], writes=[nm('rwA')])
    if g['last'] and tidx == T // 128 - 1:
        for pr in range(2):
            p.tr(k.bank(4)[0:64, pr * 128:(pr + 1) * 128], A[:, pr, :], k.ident[:, :], reads=[nm('rwA'), 'ident'], writes=['b4'])
        p.cp(sc[1][0:64, 0:256], k.bank(4)[0:64, 0:256], reads=['b4'], writes=['sc1'])
        p.dma(k.O['p_rw_S'][l].rearrange("(h v d) -> v h d", h=4, v=64), sc[1][0:64, 0:256].rearrange("p (h d) -> p h d", d=64),
              reads=['sc1'], is_out=True)
    rw_post(k, l, ti, n, Osb, ['sc6'], sc[2][:n, 512:768], ['sc2'], sc[4][:n, 0:256], ['sc4'])


def rw_sample(k, l, g, ti, n):
    p, W = k.p, k.W[l]
    nm = lambda s: '%s_l%d' % (s, l)
    Pt, pk, sc = k.P[ti], pkeys(ti, 2328, 3224), k.sc
    rw_pre(k, l, g, ti, n, False)
    p.act(sc[2][:n, 0:256], sc[2][:n, 0:256], AF.Exp, scale=-RWS, reads=['sc2'], writes=['sc2'])
    p.cp(sc[6][:n, 0:256], sc[2][:n, 512:768], reads=['sc2'], writes=['sc6'])
    p.cp(sc[6][:n, 256:512], sc[4][:n, 0:256], reads=['sc4'], writes=['sc6'])
    rep = k.rep
    rep_gather(k, 'q', sc[0][:n, 0:256], ['sc0'], rep[:, 0, :])
    rep_gather(k, 'k', sc[3][:n, 256:512], ['sc3'], rep[:, 1, :])
    rep_gather(k, 'a', sc[3][:n, 0:256], ['sc3'], rep[:, 2, :])
    rep_gather(k, 'b', sc[3][:n, 512:768], ['sc3'], rep[:, 5, :])
    rep_gather(k, 'c', sc[2][:n, 0:256], ['sc2'], rep[:, 6, :])
    rep_gather(k, 'v', sc[0][:n, 512:768], ['sc0'], rep[:, 3, 0:32], width=32)
    S = k.repS
    S3 = S[:].rearrange("p (v d) -> p v d", d=64)
    p.dma(S[:], k.I['st_rw_S'][l].rearrange("i (q f) -> (i q) f", f=2048), writes=['repS'])
    A = k.scd[0][:].rearrange("p (v d) -> p v d", d=64)
    Bt = k.scd[1][:].rearrange("p (v d) -> p v d", d=64)
    bq = lambda ap: ap.unsqueeze(1).to_broadcast([128, 32, 64])
    bvv = lambda ap: ap.unsqueeze(2).to_broadcast([128, 32, 64])
    rr_ = dict(reads=['rep', 'repS', 'sc0', 'sc1', 'sc2', 'sc3'], writes=['rep', 'sc0', 'sc1', 'sc2', 'sc3'])
    p.tt(A, S3, bq(rep[:, 2, :]), ALU.mult, **rr_)
    p.red(rep[:, 4, 0:32], A, ALU.add, **rr_)
    p.tt(A, S3, bq(rep[:, 6, :]), ALU.mult, **rr_)
    p.tt(Bt, bvv(rep[:, 4, 0:32]), bq(rep[:, 5, :]), ALU.mult, **rr_)
    p.tt(A, A, Bt, ALU.subtract, **rr_)
    p.tt(Bt, bvv(rep[:, 3, 0:32]), bq(rep[:, 1, :]), ALU.mult, **rr_)
    p.tt(S3, A, Bt, ALU.add, reads=['sc0', 'sc1', 'sc2', 'sc3', 'repS'], writes=['repS'])
    p.dma(k.O['s_rw_S'][l].rearrange("i (q f) -> (i q) f", f=2048), S[:], reads=['repS'], is_out=True)
    p.tt(A, S3, bq(rep[:, 0, :]), ALU.mult, **rr_)
    p.red(rep[:, 4, 32:64], A, ALU.add, **rr_)
    rep_scatter(k, rep[:, 4, 32:64], sc[6][:n, 768:1024], 'sc6')
    rw_post(k, l, ti, n, sc[6][:n, 768:1024], ['sc6'], sc[6][:n, 0:256], ['sc6'], sc[6][:n, 256:512], ['sc6'])


def ptsel(k, l, src_dbg, name, ap, reads):
    if name in k.DBG:
        k.p.dma(k.DBG[name], ap, reads=reads, is_out=True)


def run_group(k, g):
    p, I, O = k.p, k.I, k.O
    prompt = g['kind'] == 'p'
    tiles = g['tiles']
    ntok = sum(n for _, n in tiles)
    for ti, (tidx, n) in enumerate(tiles):
        src = I['xp'][tidx * 128:tidx * 128 + n, :] if prompt else I['xs']
        p.dma(k.X[ti][:n, :], src, writes=['X%d' % ti])
    for l in range(DEPTH):
        W = k.W[l]
        nm = lambda s: '%s_l%d' % (s, l)
        p.dma(k.bcs[:], k.bcscr[l:l + 1, :].partition_broadcast(128), reads=['bcscr'], writes=['bc'])
        p.dma(k.tab[:].rearrange("p a j i -> p (a j i)"), k.tabscr[l], reads=['tabscr'], writes=['tab'])
        tok0 = 0
        for ti, (tidx, n) in enumerate(tiles):
            rmsnorm_T(k, k.X[ti], 'X%d' % ti, n, W['gmix'], nm('gmix'), tok0, k.hT, 'hT',
                      rstd_out=k.rstdm[:, ti:ti + 1], rstd_key='rstd%d' % ti)
            tok0 += n
        if len(MIXERS) < 3:
            for ti, (tidx, n) in enumerate(tiles):
                p.memset(k.Y[ti][:, 256:1024], 0.0, writes=['Y%d' % ti])

        s5g = [s5_tile(k, l, g, ti, n, tidx) for ti, (tidx, n) in enumerate(tiles)]
        if STOP[0] <= 1 and prompt:
            s5g = []
        sched = {0: [0], 3: [0], 4: [0, 1], 6: [1]} if len(s5g) == 2 else {0: [0], 3: [0], 4: [0]}
        for c0 in range(0, DIN, 512):
            w = min(512, DIN - c0)
            wt, wk = wblock(k, 'w_in', l, 0, D, c0, c0 + w)
            tok0 = 0
            for ti, (tidx, n) in enumerate(tiles):
                bi = ((c0 // 512) * len(tiles) + ti) % 2
                for kc in range(8):
                    p.mm(k.bank(bi)[:n, 0:w], k.hT[:, kc, tok0:tok0 + n], wt[:, kc, 0:w],
                         start=(kc == 0), stop=(kc == 7), reads=['hT', wk], writes=['b%d' % bi])
                p.act(k.P[ti][:n, c0:c0 + w], k.bank(bi)[:n, 0:w], AF.Identity, scale=k.rstdm[:n, ti:ti + 1],
                      reads=['b%d' % bi, 'rstd%d' % ti],
                      writes=['P%d_b%d' % (ti, c0 // 512)] + (['rep', 'repS'] if ti == 1 else []))
                tok0 += n
            for gi in sched.get(c0 // 512, []):
                if gi < len(s5g):
                    next(s5g[gi], None)
        for gen in s5g:
            for _ in gen:
                pass
        if STOP[0] <= 1 and prompt:
            return
        for ti, (tidx, n) in enumerate(tiles):
            yk = 'Y%d' % ti
            if 'ml' in MIXERS:
                p.act(k.Y[ti][:n, 256:512], k.P[ti][:n, OFF['mo']:OFF['mo'] + 256], AF.Sigmoid,
                      reads=pkeys(ti, OFF['mo'], OFF['mo'] + 256), writes=[yk])
            if 'gla' in MIXERS:
                p.act(k.Y[ti][:n, 512:768], k.P[ti][:n, OFF['gg']:OFF['gg'] + 256], AF.Sigmoid,
                      reads=pkeys(ti, OFF['gg'], OFF['gg'] + 256), writes=[yk])
                p.tt(k.Y[ti][:n, 512:768], k.Y[ti][:n, 512:768], k.P[ti][:n, OFF['gg']:OFF['gg'] + 256], ALU.mult,
                     reads=pkeys(ti, OFF['gg'], OFF['gg'] + 256) + [yk], writes=[yk])
        for ti, (tidx, n) in enumerate(tiles):
            yk = 'Y%d' % ti
            if 'rw' in MIXERS:
                if prompt:
                    rw_prompt(k, l, g, ti, n, tidx)
                else:
                    rw_sample(k, l, g, ti, n)
            if 'ml' in MIXERS:
                if prompt:
                    ml_prompt(k, l, g, ti, n, tidx)
                else:
                    ml_sample(k, l, ti, n)
            if 'gla' in MIXERS:
                if prompt:
                    gla_prompt(k, l, g, ti, n, tidx)
                else:
                    gla_sample(k, l, ti, n)
            if prompt and g['last'] and tidx == T // 128 - 1:
                p.dma(O['p_rw_shift'][l:l + 1, :], k.P[ti][n - 1:n, OFF['rc']:OFF['rc'] + 896], reads=pkeys(ti, 2328, 3224),
                      is_out=True)
            if not prompt:
                p.dma(O['s_rw_shift'][l], k.P[ti][:n, OFF['rc']:OFF['rc'] + 896], reads=pkeys(ti, 2328, 3224), is_out=True)
            if l == 0 and prompt and tidx == 0:
                ptsel(k, l, None, 'dbg_P', k.P[ti][:, :], pkeys(ti, 0, DIN))
                ptsel(k, l, None, 'dbg_Y', k.Y[ti][:, :], [yk])
        if (STOP[0] <= 2 or 10 < STOP[0] < 20) and prompt:
            return
        tok0 = 0
        for ti, (tidx, n) in enumerate(tiles):
            yt = k.Y[ti]
            for kc in range(8):
                p.tr(k.pp[0][:, kc * n:(kc + 1) * n], yt[:n, kc * 128:(kc + 1) * 128], k.ident[:n, :n],
                     reads=['Y%d' % ti, 'ident'], writes=['b0', 'b1'])
            p.cp(k.hT[:, :, tok0:tok0 + n], v3(k.pp[0], n), reads=['b0', 'b1'], writes=['hT'])
            tok0 += n
        for half in range(2):
            wt, wk = wblock(k, 'w_out', l, 0, D, half * 512, half * 512 + 512)
            tok0 = 0
            for ti, (tidx, n) in enumerate(tiles):
                bi = 2 * half + ti % 2
                for kc in range(8):
                    p.mm(k.bank(bi)[:n, :], k.hT[:, kc, tok0:tok0 + n], wt[:, kc, :],
                         start=(kc == 0), stop=(kc == 7), reads=['hT', wk], writes=['b%d' % bi])
                xs_ = k.X[ti][:n, half * 512:half * 512 + 512]
                p.tt(xs_, xs_, k.bank(bi)[:n, :], ALU.add, reads=['b%d' % bi, 'X%d' % ti], writes=['X%d' % ti])
                tok0 += n
        if STOP[0] <= 3 and prompt:
            return
        tok0 = 0
        for ti, (tidx, n) in enumerate(tiles):
            rmsnorm_T(k, k.X[ti], 'X%d' % ti, n, W['gffn'], nm('gffn'), tok0, k.hT, 'hT')
            tok0 += n
        N = ntok
        if not prompt:
            taps = []
            for j in range(2):
                tp = k.sc[1 + j]
                p.dma(k.sc[0][:N, 0:1024], I['st_ffn_conv'][l, :, j, 0:1024], writes=['sc0'])
                p.dma(k.sc[7][:N, 0:1024], I['st_ffn_conv'][l, :, j, 1024:2048], writes=['sc7'])
                p.dma(k.sc[3][:N, 0:768], I['st_ffn_conv'][l, :, j, 2048:2816], writes=['sc3'])
                for c in range(NFF):
                    srct, sk_ = ((k.sc[0], 'sc0'), (k.sc[7], 'sc7'), (k.sc[3], 'sc3'))[c // 8]
                    cc = c % 8
                    p.tr(k.bank(2 + j)[:, c * N:(c + 1) * N], srct[:N, cc * 128:(cc + 1) * 128],
                         k.ident[:N, :N], reads=[sk_, 'ident'], writes=['b%d' % (2 + j)])
                p.cp(tp[:, 0:NFF * N], k.bank(2 + j)[:, 0:NFF * N], reads=['b%d' % (2 + j)],
                     writes=['sc%d' % (1 + j)], eng='scalar')
                taps.append(tp)
            p.dma(O['s_ffn_conv'][l, :, 0, :], I['st_ffn_conv'][l, :, 1, :], is_out=True)
        ugb = k.ugb
        cw, cb = W['convw'], W['convb']
        for jb in range(0, DFF, 512):
            w = min(512, DFF - jb)
            nch = w // 128
            c0 = jb // 128
            wtg, wkg = wblock(k, 'ffn_w_up', l, 0, D, jb, jb + w)
            wtv, wkv = wblock(k, 'ffn_w_up', l, 0, D, DFF + jb, DFF + jb + w)
            par = (jb // 512) % 2
            ugP, uvP = k.pp[2 * par], k.pp[2 * par + 1]
            ugk = ['b%d' % (4 * par), 'b%d' % (4 * par + 1)]
            uvk = ['b%d' % (4 * par + 2), 'b%d' % (4 * par + 3)]
            for cl in range(nch):
                for kc in range(8):
                    p.mm(ugP[:, cl * N:(cl + 1) * N], wtg[:, kc, cl * 128:(cl + 1) * 128], k.hT[:, kc, 0:N],
                         start=(kc == 0), stop=(kc == 7), reads=['hT', wkg], writes=ugk)
            for cl in range(nch):
                for kc in range(8):
                    p.mm(uvP[:, cl * N:(cl + 1) * N], wtv[:, kc, cl * 128:(cl + 1) * 128], k.hT[:, kc, 0:N],
                         start=(kc == 0), stop=(kc == 7), reads=['hT', wkv], writes=uvk)
            V3 = lambda ap: ap[:, 0:nch * N].rearrange("p (c i) -> p c i", i=N)
            ug3, uv3 = V3(ugP), V3(uvP)
            if prompt and par == 1:
                fsi = (0, 1, 2, 5)
            else:
                fsi = (3, 4, 6, 7)
            c1, t1, t2, t3 = (V3(k.sc[i]) for i in fsi)
            kc1, kt1, kt2, kt3 = ('sc%d' % i for i in fsi)
            ugb = k.ugb2 if (prompt and par == 1) else k.ugb
            ugbk = 'ugb'
            wb_ = lambda j: cw[:, j, c0:c0 + nch].unsqueeze(2).to_broadcast([128, nch, N])
            bb_ = cb[:, c0:c0 + nch].unsqueeze(2).to_broadcast([128, nch, N])
            wk_ = [nm('convw'), nm('convb')]
            if prompt:
                p.cp(ugb[:, 0:nch, 0:2], W['carry'][:, c0:c0 + nch, :], reads=[nm('carry')], writes=[ugbk])
                p.cp(ugb[:, 0:nch, 2:2 + N], ug3, reads=ugk, writes=[ugbk], eng='scalar')
                p.cp(W['carry'][:, c0:c0 + nch, :], ugb[:, 0:nch, N:N + 2], reads=[ugbk], writes=[nm('carry')])
                u0, u1, u2 = ugb[:, 0:nch, 0:N], ugb[:, 0:nch, 1:1 + N], ugb[:, 0:nch, 2:2 + N]
                ukeys = [ugbk]
            else:
                p.cp(ugb[:, 0:nch, 2:2 + N], ug3, reads=ugk, writes=[ugbk], eng='scalar')
                u0 = taps[0][:, c0 * N:(c0 + nch) * N].rearrange("p (c i) -> p c i", i=N)
                u1 = taps[1][:, c0 * N:(c0 + nch) * N].rearrange("p (c i) -> p c i", i=N)
                u2 = ugb[:, 0:nch, 2:2 + N]
                ukeys = [ugbk, 'sc1', 'sc2']
                p.cp(k.sc[5][:, c0 * N:(c0 + nch) * N].rearrange("p (c i) -> p c i", i=N), u2, reads=[ugbk], writes=['sc5'])
            for cl in range(nch):
                c = c0 + cl
                a_ = c1[:, cl, :]
                p.ts(a_, ug3[:, cl, :], cw[:, 2, c:c + 1], cb[:, c:c + 1], ALU.mult, ALU.add,
                     reads=ugk + wk_, writes=[kc1])
                p.stt(a_, u1[:, cl, :], cw[:, 1, c:c + 1], a_, ALU.mult, ALU.add, reads=ukeys + wk_ + [kc1], writes=[kc1])
                p.stt(a_, u0[:, cl, :], cw[:, 0, c:c + 1], a_, ALU.mult, ALU.add, reads=ukeys + wk_ + [kc1], writes=[kc1])
            p.act(t3, c1, AF.Gelu_apprx_tanh, reads=[kc1], writes=[kt3])
            p.tt(k.actT[:, c0:c0 + nch, 0:N], t3, uv3, ALU.mult, reads=[kt3] + uvk, writes=['actT'])
        if prompt and g['last']:
            for j in range(2):
                p.dma(O['p_ffn_conv'][l, j].rearrange("(c q) -> q c", q=128), W['carry'][:, :, j],
                      reads=[nm('carry')], is_out=True, allow_slow_non_contiguous=True)
        if not prompt:
            for c in range(NFF):
                p.tr(k.pp[1 + c // 8][:N, (c % 8) * 128:(c % 8 + 1) * 128], k.sc[5][:, c * N:(c + 1) * N],
                     k.ident[:, :], reads=['sc5', 'ident'], writes=['b%d' % (2 + 2 * (c // 8)), 'b%d' % (3 + 2 * (c // 8))])
            for q in range(3):
                wq = 1024 if q < 2 else 768
                p.cp(k.sc[0][:N, 0:wq], k.pp[1 + q][:N, 0:wq], reads=['b%d' % (2 + 2 * q), 'b%d' % (3 + 2 * q)],
                     writes=['sc0'], eng='scalar')
                p.dma(O['s_ffn_conv'][l, :, 1, q * 1024:q * 1024 + wq], k.sc[0][:N, 0:wq], reads=['sc0'], is_out=True)
        if STOP[0] <= 4 and prompt:
            return
        kgs = [(0, 8), (8, 8), (16, 6)]
        for half in range(2):
            for gi, (k0, kc_n) in enumerate(kgs):
                wt, wk = wblock(k, 'ffn_w_down', l, k0 * 128, (k0 + kc_n) * 128, half * 512, half * 512 + 512)
                tok0 = 0
                for ti, (tidx, n) in enumerate(tiles):
                    bi = 2 * half + ti % 2
                    for kc in range(kc_n):
                        p.mm(k.bank(bi)[:n, :], k.actT[:, k0 + kc, tok0:tok0 + n], wt[:, kc, :],
                             start=(gi == 0 and kc == 0), stop=(gi == 2 and kc == kc_n - 1),
                             reads=['actT', wk], writes=['b%d' % bi])
                    if gi == 2:
                        xs_ = k.X[ti][:n, half * 512:half * 512 + 512]
                        p.tt(xs_, xs_, k.bank(bi)[:n, :], ALU.add, reads=['b%d' % bi, 'X%d' % ti],
                             writes=['X%d' % ti])
                    tok0 += n
    p.dma(k.sc[6][:, :], I['norm_final'].partition_broadcast(128), writes=['sc6'])
    for ti, (tidx, n) in enumerate(tiles):
        sm = k.sm
        xt, xkey = k.X[ti], 'X%d' % ti
        p.act(k.sc[7][:n, :], xt[:n, :], AF.Square, reads=[xkey], writes=['sc7', 'sm'], accum_out=sm[:n, 0:1])
        p.ts(sm[:n, 1:2], sm[:n, 0:1], 1.0 / D, EPS, ALU.mult, ALU.add, reads=['sm'], writes=['sm'])
        p.tt(sm[:n, 2:3], sm[:n, 1:2], k.mhalf[:n, 0:1], ALU.pow, reads=['sm', 'mhalf'], writes=['sm'], eng='gpsimd')
        p.stt(k.sc[7][:n, :], xt[:n, :], sm[:n, 2:3], k.sc[6][:n, :], ALU.mult, ALU.mult,
              reads=[xkey, 'sm', 'sc6'], writes=['sc7'])
        dst = O['y_p'][tidx * 128:tidx * 128 + n, :] if prompt else O['y_s']
        p.dma(dst, k.sc[7][:n, :], reads=['sc7'], is_out=True)


_CACHE = {}


def make_in_maps(inputs):
    f = lambda a: np.ascontiguousarray(np.asarray(a, dtype=np.float32))
    maps = []
    for c in range(NCORES):
        r = slice(NS * c, NS * (c + 1))
        m = {}
        m['xp'] = f(inputs['x_prompt'][c])
        m['xs'] = f(inputs['x_sample'][r, 0, :])
        m['st_s5_re'] = f(inputs['state_s5_re'][:, r].reshape(DEPTH, NS, 1024))
        m['st_s5_im'] = f(inputs['state_s5_im'][:, r].reshape(DEPTH, NS, 1024))
        m['st_ml_C'] = f(inputs['state_mlstm_C'][:, r].reshape(DEPTH, NS, 16384))
        m['st_ml_n'] = f(inputs['state_mlstm_n'][:, r].reshape(DEPTH, NS, 256))
        m['st_ml_m'] = f(inputs['state_mlstm_m'][:, r])
        m['st_gla_S'] = f(inputs['state_gla_S'][:, r].reshape(DEPTH, NS, 16384))
        m['st_rw_S'] = f(inputs['state_rwkv_S'][:, r].reshape(DEPTH, NS, 16384))
        m['st_rw_shift'] = f(inputs['state_rwkv_shift'][:, r])
        m['st_ffn_conv'] = f(inputs['state_ffn_conv'][:, r])
        for name, shp in IN_SPECS[11:]:
            m[name] = f(np.asarray(inputs[name]).reshape(shp))
        maps.append(m)
    return maps


def assemble(results):
    g = lambda name: [np.asarray(r[name]) for r in results]
    B = NCORES
    out = []
    out.append(np.stack(g('y_p'), 0))
    out.append(np.concatenate(g('y_s'), 0).reshape(B * NS, 1, D))
    pst = lambda name, shp: np.stack(g(name), 1).reshape((DEPTH, B) + shp)
    out.append(pst('p_s5_re', (16, 64)))
    out.append(pst('p_s5_im', (16, 64)))
    out.append(pst('p_ml_C', (4, 64, 64)))
    out.append(pst('p_ml_n', (4, 64)))
    out.append(pst('p_ml_m', (4,)))
    out.append(pst('p_gla_S', (4, 64, 64)))
    out.append(pst('p_rw_S', (4, 64, 64)))
    out.append(pst('p_rw_shift', (896,)))
    out.append(pst('p_ffn_conv', (2, DFF)))
    sst = lambda name, shp: np.concatenate(g(name), 1).reshape((DEPTH, B * NS) + shp)
    out.append(sst('s_s5_re', (16, 64)))
    out.append(sst('s_s5_im', (16, 64)))
    out.append(sst('s_ml_C', (4, 64, 64)))
    out.append(sst('s_ml_n', (4, 64)))
    out.append(sst('s_ml_m', (4,)))
    out.append(sst('s_gla_S', (4, 64, 64)))
    out.append(sst('s_rw_S', (4, 64, 64)))
    out.append(sst('s_rw_shift', (896,)))
    out.append(sst('s_ffn_conv', (2, DFF)))
    return tuple(np.ascontiguousarray(o, dtype=np.float32) for o in out)


def kernel(**inputs):
    if 'nc' not in _CACHE:
        _CACHE['nc'] = build()
    nc = _CACHE['nc']
    maps = make_in_maps(inputs)
    res = run_bass_kernel_spmd(nc, maps, core_ids=list(range(NCORES)))
    return assemble(res.results)
```
